# Optimizing a Trainium2 kernel written in Bass

```python
import jax, jax.numpy as jnp
from jax import lax
import numpy as np

D_MODEL = 2048
BATCH = 2
SEQ = 8192
DEPTH = 4

GRID_W = 64
EPS = 1e-6
D_SSM = D_MODEL
SSM_HEAD_DIM = 64
SSM_HEADS = D_SSM // SSM_HEAD_DIM
SSM_GROUPS = 8
SSM_STATE = 128
SSM_CONV = 5
SSM_CHUNK = 128
SSM_CONV_CH = D_SSM + 2 * SSM_GROUPS * SSM_STATE
D_NA = D_MODEL
NA_HEAD_DIM = 128
NA_HEADS = D_NA // NA_HEAD_DIM
NA_WIN_ROWS = 8
NA_WIN_COLS = 16
D_CONV = 2 * D_MODEL
CONV_WIDTH = 31
PLE_DIM = 256

N_EVEN = (DEPTH + 1) // 2
N_ODD = DEPTH // 2
EVEN_SPLITS = [D_SSM, SSM_CONV_CH, 2 * SSM_HEADS, D_NA, D_NA, D_NA, D_NA]
EVEN_IN = sum(EVEN_SPLITS)
ODD_IN = 3 * D_CONV

kernel_name = "hybrid_ssd_natten_conformer_encoder"


def rmsnorm(x, w, eps=EPS):
    xf = x.astype(jnp.float32)
    y = xf * lax.rsqrt(jnp.mean(xf * xf, axis=-1, keepdims=True) + eps)
    return (y * w.astype(jnp.float32)).astype(x.dtype)


def layernorm(x, w, b, eps=EPS):
    xf = x.astype(jnp.float32)
    mu = jnp.mean(xf, axis=-1, keepdims=True)
    xc = xf - mu
    y = xc * lax.rsqrt(jnp.mean(xc * xc, axis=-1, keepdims=True) + eps)
    return (y * w.astype(jnp.float32) + b.astype(jnp.float32)).astype(x.dtype)


def depthwise_conv(x, w, b):
    k = w.shape[0]
    y = lax.conv_general_dilated(
        x, w[:, None, :].astype(x.dtype), window_strides=(1,),
        padding=[(k // 2, k - 1 - k // 2)],
        dimension_numbers=("NWC", "WIO", "NWC"),
        feature_group_count=x.shape[-1])
    return y + b.astype(x.dtype)


def segsum(a):
    t = a.shape[-1]
    a_rep = jnp.broadcast_to(a[..., :, None], a.shape + (t,))
    strict = jnp.tril(jnp.ones((t, t), bool), -1)
    cs = jnp.cumsum(jnp.where(strict, a_rep, 0.0), axis=-2)
    return jnp.where(jnp.tril(jnp.ones((t, t), bool)), cs, -jnp.inf)


def ssd_chunked(x, dt, a, bm, cm):
    b, l, h, p = x.shape
    g, n = bm.shape[2], bm.shape[3]
    j = h // g
    q = SSM_CHUNK
    c = l // q
    xdt = (x * dt[..., None]).reshape(b, c, q, g, j, p)
    da = (dt * a).reshape(b, c, q, g, j)
    a_cs = jnp.cumsum(da, axis=2)
    bc = bm.reshape(b, c, q, g, n)
    cc = cm.reshape(b, c, q, g, n)
    decay_in = jnp.exp(segsum(da.transpose(0, 3, 4, 1, 2)))
    scores = jnp.einsum("bclgn,bcsgn->bgcls", cc, bc)[:, :, None] * decay_in
    y_diag = jnp.einsum("bgjcls,bcsgjp->bclgjp", scores, xdt)
    decay_states = jnp.exp(a_cs[:, :, -1:] - a_cs)
    states = jnp.einsum("bcsgn,bcsgjp->cbgjpn", bc, xdt * decay_states[..., None])
    chunk_decay = jnp.exp(jnp.moveaxis(a_cs[:, :, -1], 1, 0))

    def step(carry, inp):
        st, dec = inp
        return carry * dec[..., None, None] + st, carry

    _, prev = lax.scan(step, jnp.zeros(states.shape[1:], states.dtype), (states, chunk_decay))
    y_off = jnp.einsum("bclgn,cbgjpn->bclgjp", cc, prev) * jnp.exp(a_cs)[..., None]
    return (y_diag + y_off).reshape(b, l, h, p)


def ssd_branch(z, xbc, dt_raw, conv_w, conv_b, dt_bias_f, dt_bias_b, a_log_f, a_log_b, d_skip, gnorm_w):
    b, l, _ = z.shape
    xbc = jax.nn.silu(depthwise_conv(xbc, conv_w, conv_b)).astype(jnp.float32)
    xs, bm, cm = jnp.split(xbc, [D_SSM, D_SSM + SSM_GROUPS * SSM_STATE], axis=-1)
    xs = xs.reshape(b, l, SSM_HEADS, SSM_HEAD_DIM)
    bm = bm.reshape(b, l, SSM_GROUPS, SSM_STATE)
    cm = cm.reshape(b, l, SSM_GROUPS, SSM_STATE)
    dt_f, dt_b = jnp.split(dt_raw.astype(jnp.float32), 2, axis=-1)
    dt_f = jax.nn.softplus(dt_f + dt_bias_f.astype(jnp.float32))
    dt_b = jax.nn.softplus(dt_b + dt_bias_b.astype(jnp.float32))
    a_f = -jnp.exp(a_log_f.astype(jnp.float32))
    a_b = -jnp.exp(a_log_b.astype(jnp.float32))
    flip = lambda t: jnp.flip(t, axis=1)
    y_f = ssd_chunked(xs, dt_f, a_f, bm, cm)
    y_b = flip(ssd_chunked(flip(xs), flip(dt_b), a_b, flip(bm), flip(cm)))
    y = y_f + y_b + d_skip.astype(jnp.float32)[:, None] * xs
    y = y.reshape(b, l, D_SSM) * jax.nn.silu(z.astype(jnp.float32))
    yg = y.reshape(b, l, SSM_GROUPS, D_SSM // SSM_GROUPS)
    yg = yg * lax.rsqrt(jnp.mean(yg * yg, axis=-1, keepdims=True) + EPS)
    return (yg.reshape(b, l, D_SSM) * gnorm_w.astype(jnp.float32)).astype(z.dtype)


def neighbourhood_attention(q, k, v, rpb):
    b, s_len, _ = q.shape
    rows = s_len // GRID_W
    kr = min(NA_WIN_ROWS, rows)
    grid = (b, rows, GRID_W, NA_HEADS, NA_HEAD_DIM)
    qg, kg, vg = q.reshape(grid), k.reshape(grid), v.reshape(grid)
    cols = jnp.arange(GRID_W)
    col_start = jnp.clip(cols - NA_WIN_COLS // 2, 0, GRID_W - NA_WIN_COLS)
    col_valid = (cols[None, :] >= col_start[:, None]) & (cols[None, :] < col_start[:, None] + NA_WIN_COLS)
    col_off = jnp.clip(cols[None, :] - cols[:, None], -(NA_WIN_COLS - 1), NA_WIN_COLS - 1) + NA_WIN_COLS - 1
    rpb_cols = rpb.astype(jnp.float32)[:, :, col_off]
    scale = NA_HEAD_DIM ** -0.5

    def row_attend(args):
        q_row, r = args
        rs = jnp.clip(r - kr // 2, 0, rows - kr)
        k_rows = lax.dynamic_slice_in_dim(kg, rs, kr, axis=1).astype(jnp.float32)
        v_rows = lax.dynamic_slice_in_dim(vg, rs, kr, axis=1).astype(jnp.float32)
        s = jnp.einsum("bqhd,bikhd->bhqik", q_row.astype(jnp.float32), k_rows) * scale
        row_off = rs + jnp.arange(kr) - r + NA_WIN_ROWS - 1
        bias = jnp.take(rpb_cols, row_off, axis=1).transpose(0, 2, 1, 3)
        s = jnp.where(col_valid[None, None, :, None, :], s + bias[None], -jnp.inf)
        pr = jax.nn.softmax(s.reshape(b, NA_HEADS, GRID_W, kr * GRID_W), axis=-1).reshape(s.shape)
        return jnp.einsum("bhqik,bikhd->bqhd", pr, v_rows).astype(q_row.dtype)

    out = lax.map(row_attend, (jnp.moveaxis(qg, 1, 0), jnp.arange(rows)))
    return jnp.moveaxis(out, 0, 1).reshape(b, s_len, D_NA)


def even_mixer(hn, w_in, conv_w, conv_b, dt_bias_f, dt_bias_b, a_log_f, a_log_b, d_skip, gnorm_w, rpb, w_out):
    u = hn @ w_in
    z, xbc, dt_raw, q, k, v, g = jnp.split(u, np.cumsum(EVEN_SPLITS)[:-1].tolist(), axis=-1)
    y_ssd = ssd_branch(z, xbc, dt_raw, conv_w, conv_b, dt_bias_f, dt_bias_b, a_log_f, a_log_b, d_skip, gnorm_w)
    y_na = neighbourhood_attention(q, k, v, rpb) * jax.nn.silu(g)
    return jnp.concatenate([y_ssd, y_na], axis=-1) @ w_out


def conv_module(hn, w_in, dw_w, dw_b, ln_w, ln_b, w_out):
    a, a_gate, g = jnp.split(hn @ w_in, 3, axis=-1)
    v = a * jax.nn.sigmoid(a_gate)
    v = depthwise_conv(v, dw_w, dw_b)
    v = layernorm(v, ln_w, ln_b)
    v = jax.nn.silu(v) * jax.nn.silu(g)
    return v @ w_out


def per_layer_embedding(h, p_i, norm_w, w_gate, w_proj):
    gate = jax.nn.sigmoid(rmsnorm(h, norm_w) @ w_gate)
    return gate * (p_i @ w_proj)


def setup_inputs(seed: int = 0) -> dict:
    key = jax.random.key(seed)
    ks = iter(jax.random.split(key, 32))
    nrm = lambda shape, scale: jax.random.normal(next(ks), shape, jnp.float32) * scale
    gain = lambda shape: 1.0 + nrm(shape, 0.02)
    dt = jnp.exp(jax.random.uniform(next(ks), (2, N_EVEN, SSM_HEADS), jnp.float32,
                                    jnp.log(0.001), jnp.log(0.1)))
    dt_bias = dt + jnp.log(-jnp.expm1(-dt))
    a_log = jnp.log(jax.random.uniform(next(ks), (2, N_EVEN, SSM_HEADS), jnp.float32, 1.0, 16.0))
    return {
        "x": nrm((BATCH, SEQ, D_MODEL), 1.0),
        "p": nrm((DEPTH, BATCH, SEQ, PLE_DIM), 1.0),
        "ev_norm_w": gain((N_EVEN, D_MODEL)),
        "ev_w_in": nrm((N_EVEN, D_MODEL, EVEN_IN), D_MODEL ** -0.5),
        "ev_conv_w": nrm((N_EVEN, SSM_CONV, SSM_CONV_CH), SSM_CONV ** -0.5),
        "ev_conv_b": nrm((N_EVEN, SSM_CONV_CH), 0.02),
        "ev_dt_bias_f": dt_bias[0],
        "ev_dt_bias_b": dt_bias[1],
        "ev_a_log_f": a_log[0],
        "ev_a_log_b": a_log[1],
        "ev_d_skip": gain((N_EVEN, SSM_HEADS)),
        "ev_gnorm_w": gain((N_EVEN, D_SSM)),
        "ev_rpb": nrm((N_EVEN, NA_HEADS, 2 * NA_WIN_ROWS - 1, 2 * NA_WIN_COLS - 1), 0.1),
        "ev_w_out": nrm((N_EVEN, D_SSM + D_NA, D_MODEL), (D_SSM + D_NA) ** -0.5),
        "od_norm_w": gain((N_ODD, D_MODEL)),
        "od_w_in": nrm((N_ODD, D_MODEL, ODD_IN), D_MODEL ** -0.5),
        "od_dw_w": nrm((N_ODD, CONV_WIDTH, D_CONV), CONV_WIDTH ** -0.5),
        "od_dw_b": nrm((N_ODD, D_CONV), 0.02),
        "od_ln_w": gain((N_ODD, D_CONV)),
        "od_ln_b": nrm((N_ODD, D_CONV), 0.02),
        "od_w_out": nrm((N_ODD, D_CONV, D_MODEL), D_CONV ** -0.5),
        "ple_norm_w": gain((DEPTH, D_MODEL)),
        "ple_w_gate": nrm((DEPTH, D_MODEL, D_MODEL), D_MODEL ** -0.5),
        "ple_w_proj": nrm((DEPTH, PLE_DIM, D_MODEL), PLE_DIM ** -0.5),
        "final_norm_w": gain((D_MODEL,)),
    }


def reference(x, p, ev_norm_w, ev_w_in, ev_conv_w, ev_conv_b, ev_dt_bias_f, ev_dt_bias_b,
              ev_a_log_f, ev_a_log_b, ev_d_skip, ev_gnorm_w, ev_rpb, ev_w_out,
              od_norm_w, od_w_in, od_dw_w, od_dw_b, od_ln_w, od_ln_b, od_w_out,
              ple_norm_w, ple_w_gate, ple_w_proj, final_norm_w):
    h = x
    for i in range(DEPTH):
        e = i // 2
        if i % 2 == 0:
            h = h + even_mixer(rmsnorm(h, ev_norm_w[e]), ev_w_in[e], ev_conv_w[e], ev_conv_b[e],
                               ev_dt_bias_f[e], ev_dt_bias_b[e], ev_a_log_f[e], ev_a_log_b[e],
                               ev_d_skip[e], ev_gnorm_w[e], ev_rpb[e], ev_w_out[e])
        else:
            h = h + conv_module(rmsnorm(h, od_norm_w[e]), od_w_in[e], od_dw_w[e], od_dw_b[e],
                                od_ln_w[e], od_ln_b[e], od_w_out[e])
        h = h + per_layer_embedding(h, p[i], ple_norm_w[i], ple_w_gate[i], ple_w_proj[i])
    return rmsnorm(h, final_norm_w)
```

```python
import contextlib
import numpy as np
import ml_dtypes
import concourse.bass as bass
import concourse.mybir as mybir
from concourse.bass_utils import run_bass_kernel_spmd

F32 = mybir.dt.float32
BF16 = mybir.dt.bfloat16
AF = mybir.ActivationFunctionType
ALU = mybir.AluOpType
AX = mybir.AxisListType

L = 8192
D = 2048
TS = 512
TPS = TS // 128
NST = L // TS
NT = L // 128
EVEN_IN = 14400
ODD_IN = 12288
NEG = -30000.0
EPS = 1e-6
DEPTH = 4
EV_COLS = ([c for c in range(0, 2048, 512)] + [2048 + c for c in range(0, 4096, 512)] +
           [6208 + c for c in range(0, 8192, 512)])
NCORES = 2

DEBUG = {}
STOP_AFTER = None
NST_RUN = NST
PHASES = None
E_PARTS = "ACN"


def _set_L(n):
    global L, NST, NT, NST_RUN
    L = n
    NST = L // TS
    NT = L // 128
    NST_RUN = NST


class _Eng:
    def __init__(self, kb, name, h):
        self.kb = kb
        self.name = name
        self.h = h
        self.cnt = 0
        self.sem_ids = []
        self.waited = {}
        self.dsl = []
        self.drr = 0

    def event_for(self, cnt):
        cap = 30000
        idx = (cnt - 1) // cap
        while len(self.sem_ids) <= idx:
            self.sem_ids.append(self.kb.new_sem(f"{self.name}_c{len(self.sem_ids)}"))
        return (self.sem_ids[idx], (cnt - 1) % cap + 1)


class KB:
    NDSEM = 10

    def __init__(self, nc, es):
        self.nc = nc
        self.es = es
        self.sems = []
        self.state = {}
        self.eng = {}
        for name, h in (("pe", nc.tensor), ("act", nc.scalar), ("dve", nc.vector),
                        ("pool", nc.gpsimd), ("sp", nc.sync)):
            self.eng[name] = _Eng(self, name, h)
        self.ninst = 0

    def new_sem(self, name):
        s = self.es.enter_context(self.nc.semaphore(name))
        self.sems.append(s)
        return len(self.sems) - 1

    def _wait(self, e, ev):
        sid, val = ev
        if e.name == "pe" and sid in e.sem_ids:
            return
        if e.waited.get(sid, 0) < val:
            e.h.wait_ge(self.sems[sid], val)
            e.waited[sid] = val

    def _gather(self, reads, writes):
        evs = []
        for k in reads:
            st = self.state.get(k)
            if st is not None and st[0] is not None:
                evs.append(st[0])
        for k in writes:
            st = self.state.get(k)
            if st is not None:
                if st[0] is not None:
                    evs.append(st[0])
                evs.extend(st[1].items())
        return evs

    def _record(self, ev, reads, writes):
        for k in reads:
            st = self.state.setdefault(k, [None, {}])
            if st[1].get(ev[0], 0) < ev[1]:
                st[1][ev[0]] = ev[1]
        for k in writes:
            self.state[k] = [ev, {}]

    def op(self, engname, fn, reads=(), writes=(), signal=True):
        e = self.eng[engname]
        for ev in self._gather(reads, writes):
            self._wait(e, ev)
        inst = fn(e.h)
        self.ninst += 1
        if signal:
            e.cnt += 1
            ev = e.event_for(e.cnt)
            inst.then_inc(self.sems[ev[0]], 1)
        else:
            ev = e.event_for(e.cnt + 1)
        self._record(ev, reads, writes)
        return inst

    def dma(self, q, out, in_, reads=(), writes=()):
        e = self.eng[q]
        for ev in self._gather(reads, writes):
            self._wait(e, ev)
        if not e.dsl:
            e.dsl = [[self.new_sem(f"{q}_d{i}"), 0] for i in range(self.NDSEM)]
        slot = e.dsl[e.drr]
        e.drr = (e.drr + 1) % len(e.dsl)
        if slot[1] > 0:
            self._wait(e, (slot[0], slot[1] * 16))
        inst = e.h.dma_start(out=out, in_=in_)
        inst.then_inc(self.sems[slot[0]], 16)
        slot[1] += 1
        self.ninst += 1
        ev = (slot[0], slot[1] * 16)
        self._record(ev, reads, writes)
        return ev

    def barrier(self):
        evs = []
        for e in self.eng.values():
            if e.cnt > 0:
                evs.append(e.event_for(e.cnt))
            for slot in e.dsl:
                if slot[1] > 0:
                    evs.append((slot[0], slot[1] * 16))
        for e in self.eng.values():
            for ev in evs:
                sid, val = ev
                if e.waited.get(sid, 0) < val:
                    e.h.wait_ge(self.sems[sid], val)
                    e.waited[sid] = val
        self.state = {}


def _bc(ap2d, n):
    return ap2d.unsqueeze(2).broadcast_to([ap2d.shape[0], ap2d.shape[1], n])


class Prog:
    def __init__(self):
        self.nc = bass.Bass("TRN2", target_bir_lowering=False)
        self.nsb = 0

    def sb(self, es, shape, dt, name=None):
        self.nsb += 1
        return es.enter_context(self.nc.sbuf_tensor(f"{name or 'sb'}_{self.nsb}", list(shape), dt))

    def din(self, name, shape, dt=F32):
        return self.nc.dram_tensor(name, list(shape), dt, kind="ExternalInput").ap()

    def dscr(self, name, shape, dt):
        return self.nc.dram_tensor(name, list(shape), dt).ap()

    def build(self):
        nc = self.nc
        I = {}
        I["x"] = self.din("x", [L, D])
        I["p"] = self.din("p", [DEPTH, L, 256])
        I["ev_norm_w"] = self.din("ev_norm_w", [2, D])
        I["ev_w_in"] = self.din("ev_w_in", [2, D, EVEN_IN])
        I["ev_conv_wT"] = self.din("ev_conv_wT", [2, 4096, 5])
        I["ev_conv_b"] = self.din("ev_conv_b", [2, 4096])
        I["ev_conv_bpm"] = self.din("ev_conv_bpm", [2, 128, 32])
        I["ev_dt_bias"] = self.din("ev_dt_bias", [2, 64])
        I["ev_a_log"] = self.din("ev_a_log", [2, 64])
        I["ev_d_skip"] = self.din("ev_d_skip", [2, 32])
        I["ev_gnorm_w"] = self.din("ev_gnorm_w", [2, D])
        I["ev_rpbt"] = self.din("ev_rpbt", [2, 16, 8, 128, 256])
        I["ev_w_out"] = self.din("ev_w_out", [2, 4096, D])
        I["od_norm_w"] = self.din("od_norm_w", [2, D])
        I["od_w_in"] = self.din("od_w_in", [2, D, ODD_IN])
        I["od_dw_wT"] = self.din("od_dw_wT", [2, 4096, 31])
        I["od_pm"] = self.din("od_pm", [2, 3, 128, 32])
        I["od_w_out"] = self.din("od_w_out", [2, 4096, D])
        I["ple_norm_w"] = self.din("ple_norm_w", [DEPTH, D])
        I["ple_w_gate"] = self.din("ple_w_gate", [DEPTH, D, D])
        I["ple_w_proj"] = self.din("ple_w_proj", [DEPTH, 256, D])
        I["final_norm_w"] = self.din("final_norm_w", [1, D])
        I["c_ident"] = self.din("c_ident", [128, 128])
        I["c_tri"] = self.din("c_tri", [3, 128, 128])
        I["c_mask"] = self.din("c_mask", [2, 128, 512])
        I["c_namask"] = self.din("c_namask", [128, 256])
        self.I = I
        self.out = nc.dram_tensor("out", [L, D], F32, kind="ExternalOutput").ap()

        S = {}
        S["H"] = self.dscr("H", [L, D], F32)
        S["wk_ev_in"] = self.dscr("wk_ev_in", [2, 29, 128, 16, 512], BF16)
        S["wk_od_in"] = self.dscr("wk_od_in", [2, 24, 128, 16, 512], BF16)
        S["wk_ev_out"] = self.dscr("wk_ev_out", [2, 8, 128, 16, 512], BF16)
        S["wk_od_out"] = self.dscr("wk_od_out", [2, 8, 128, 16, 512], BF16)
        S["wk_gate"] = self.dscr("wk_gate", [DEPTH, 4, 128, 16, 512], BF16)
        S["wk_proj"] = self.dscr("wk_proj", [DEPTH, 4, 128, 2, 512], BF16)
        S["SZ"] = self.dscr("SZ", [L, D], F32)
        S["XBCT"] = self.dscr("XBCT", [4096, L + 4], BF16)
        S["DT"] = self.dscr("DT", [L, 64], F32)
        S["QT"] = self.dscr("QT", [D, L], BF16)
        S["KT"] = self.dscr("KT", [D, L], BF16)
        S["V"] = self.dscr("V", [L, D], BF16)
        S["SG"] = self.dscr("SG", [L, D], F32)
        S["YS"] = self.dscr("YS", [L, D], F32)
        S["YN"] = self.dscr("YN", [L, D], F32)
        S["XC"] = self.dscr("XC", [L, D], BF16)
        S["BTOK"] = self.dscr("BTOK", [L, 1024], BF16)
        S["BT"] = self.dscr("BT", [1024, L], BF16)
        S["CT"] = self.dscr("CT", [1024, L], BF16)
        S["SPF"] = self.dscr("SPF", [NT, 128, D], BF16)
        S["LSB"] = self.dscr("LSB", [NT, 128, D], F32)
        S["VT"] = self.dscr("VT", [4096, L + 30], BF16)
        S["VCT"] = self.dscr("VCT", [4096, L], F32)
        S["SG2T"] = self.dscr("SG2T", [4096, L], F32)
        S["YOT"] = self.dscr("YOT", [4096, L], BF16)
        self.S = S

        self.dbg = {}
        for name, (shape, _fn) in DEBUG.items():
            self.dbg[name] = nc.dram_tensor("dbg_" + name, list(shape), F32, kind="ExternalOutput").ap()

        with contextlib.ExitStack() as es:
            self.kb = KB(nc, es)
            kb = self.kb
            self.psum = es.enter_context(nc.psum_tensor("psum", [128, 4096], F32))
            self.psum_bf = self.psum[:].bitcast(BF16)
            self.ident_f = self.sb(es, [128, 128], F32, "ident_f")
            self.ident_b = self.sb(es, [128, 128], BF16, "ident_b")
            self.ones_b = self.sb(es, [128, 128], BF16, "ones_b")
            self.zero_b = self.sb(es, [128, 32], BF16, "zero_b")
            kb.dma("sp", self.ident_f[:], I["c_ident"][:, :], writes=["ident_f"])
            kb.op("dve", lambda e: e.tensor_copy(out=self.ident_b[:], in_=self.ident_f[:]),
                  reads=["ident_f"], writes=["ident_b"])
            kb.op("dve", lambda e: e.memset(self.ones_b[:], 1.0), writes=["ones_b"])
            kb.op("dve", lambda e: e.memset(self.zero_b[:], 0.0), writes=["zero_b"])
            self.psrr = 0

            self.phase_cast()
            kb.barrier()
            phases = [("P", None, 0), ("E", 0), ("P", 0, 1), ("O", 1), ("P", 1, 2), ("E", 2),
                      ("P", 2, 3), ("O", 3), ("P", 3, None)]
            if PHASES is not None:
                phases = PHASES
            for ph in phases:
                tag = "_".join(str(a) for a in ph)
                if ph[0] == "P":
                    self.phase_P(ph[1], ph[2])
                elif ph[0] == "E":
                    self.phase_E(ph[1])
                else:
                    self.phase_O(ph[1])
                kb.barrier()
                if STOP_AFTER == tag:
                    break
            self.dump_debug()
            kb.barrier()
        return nc

    def bank(self, b, n=512, off=0):
        return self.psum[:, b * 512 + off: b * 512 + off + n]

    def bank_bf(self, b, n=1024, off=0):
        return self.psum_bf[:, b * 1024 + off: b * 1024 + off + n]

    def next_bank(self, lo=0, hi=8):
        b = lo + self.psrr % (hi - lo)
        self.psrr += 1
        return b

    def mm(self, out_ap, pairs, reads, wkey, transpose=False):
        kb = self.kb
        n = len(pairs)
        for i, (l, r) in enumerate(pairs):
            kb.op("pe", lambda e, l=l, r=r, i=i: e.matmul(out_ap, lhsT=l, rhs=r, start=(i == 0), stop=(i == n - 1)),
                  reads=reads, writes=[wkey], signal=(i == n - 1))

    def dump_debug(self):
        kb = self.kb
        for name, ap in self.dbg.items():
            fn = DEBUG[name][1]
            if fn is None:
                continue
            kb.dma("pool", ap, fn(self), writes=["dbg_" + name])

    def phase_cast(self):
        kb, I, S = self.kb, self.I, self.S
        def blk(dst4, src2d, k0, c0, nkc, ncols):
            kb.dma("pool", dst4[:, 0:nkc, 0:ncols],
                   src2d[k0:k0 + nkc * 128, c0:c0 + ncols].rearrange("(kc p) c -> p kc c", p=128))
        for e in range(2):
            for i, c0 in enumerate(EV_COLS):
                blk(S["wk_ev_in"][e, i], I["ev_w_in"][e], 0, c0, 16, 512)
            blk(S["wk_ev_in"][e, 28], I["ev_w_in"][e], 0, 6144, 16, 64)
            for i in range(24):
                blk(S["wk_od_in"][e, i], I["od_w_in"][e], 0, i * 512, 16, 512)
            for nb in range(4):
                for hf in range(2):
                    blk(S["wk_ev_out"][e, nb * 2 + hf], I["ev_w_out"][e], hf * 2048, nb * 512, 16, 512)
                    blk(S["wk_od_out"][e, nb * 2 + hf], I["od_w_out"][e], hf * 2048, nb * 512, 16, 512)
        for l in range(DEPTH):
            for nb in range(4):
                blk(S["wk_gate"][l, nb], I["ple_w_gate"][l], 0, nb * 512, 16, 512)
                blk(S["wk_proj"][l, nb], I["ple_w_proj"][l], 0, nb * 512, 2, 512)
        zt = self.zero_b
        for cb in range(32):
            kb.dma("sp", S["XBCT"][cb * 128:(cb + 1) * 128, 0:2], zt[:, 0:2], reads=["zero_b"])
            kb.dma("sp", S["XBCT"][cb * 128:(cb + 1) * 128, L + 2:L + 4], zt[:, 0:2], reads=["zero_b"])
            kb.dma("sp", S["VT"][cb * 128:(cb + 1) * 128, 0:15], zt[:, 0:15], reads=["zero_b"])
            kb.dma("sp", S["VT"][cb * 128:(cb + 1) * 128, L + 15:L + 30], zt[:, 0:15], reads=["zero_b"])

    def norm_T(self, es_tiles, Hs, wrep, AT, sq_junk, hn_tiles, stat, hkey):
        kb = self.kb
        ss = stat
        for tt in range(TPS):
            kb.op("act", lambda e, tt=tt: e.activation(out=sq_junk[:], in_=Hs[:, tt, :], func=AF.Square,
                                                        accum_out=ss[:, tt:tt + 1]),
                  reads=[hkey], writes=["sq_junk", "stat"])
        kb.op("act", lambda e: e.activation(out=ss[:, TPS:2 * TPS], in_=ss[:, 0:TPS], func=AF.Sqrt,
                                            bias=EPS, scale=1.0 / D), reads=["stat"], writes=["stat"])
        kb.op("dve", lambda e: e.reciprocal(out=ss[:, 2 * TPS:3 * TPS], in_=ss[:, TPS:2 * TPS]),
              reads=["stat"], writes=["stat"])
        for tt in range(TPS):
            hn = hn_tiles[tt % len(hn_tiles)]
            hk = f"hn{tt % len(hn_tiles)}"
            kb.op("dve", lambda e, tt=tt, hn=hn: e.scalar_tensor_tensor(
                out=hn[:], in0=Hs[:, tt, :], scalar=ss[:, 2 * TPS + tt:2 * TPS + tt + 1], in1=wrep[:],
                op0=ALU.mult, op1=ALU.mult), reads=[hkey, "stat", "wrep"], writes=[hk])
            self.transpose_into(hn, 16, AT, tt, hk, "AT")

    def transpose_into(self, src, nkc, dstT, tt, skey, dkey):
        kb = self.kb
        for g0 in range(0, nkc, 8):
            b = self.next_bank()
            n = min(8, nkc - g0)
            for j in range(n):
                kc = g0 + j
                kb.op("pe", lambda e, kc=kc, j=j, b=b: e.transpose(self.bank_bf(b, 128, j * 128),
                                                                src[:, kc * 128:(kc + 1) * 128], self.ident_b[:]),
                      reads=[skey, "ident_b"], writes=[f"ps{b}"], signal=(j == n - 1))
            eng = "act" if (self.psrr % 2 == 0) else "dve"
            dst = dstT[:, g0:g0 + n, tt * 128:(tt + 1) * 128]
            srcp = self.bank_bf(b, n * 128).rearrange("p (a c) -> p a c", a=n)
            if eng == "act":
                kb.op("act", lambda e, dst=dst, srcp=srcp: e.activation(out=dst, in_=srcp, func=AF.Copy),
                      reads=[f"ps{b}"], writes=[dkey])
            else:
                kb.op("dve", lambda e, dst=dst, srcp=srcp: e.tensor_copy(out=dst, in_=srcp),
                      reads=[f"ps{b}"], writes=[dkey])

    def phase_P(self, lp, ln):
        kb, I, S, nc = self.kb, self.I, self.S, self.nc
        with contextlib.ExitStack() as es:
            Hs = self.sb(es, [128, TPS, D], F32, "Hs")
            AT = self.sb(es, [128, 16, TS], BF16, "AT")
            hn_tiles = [self.sb(es, [128, D], BF16, f"hn{i}") for i in range(2)]
            sq_junk = self.sb(es, [128, D], BF16, "sq_junk")
            stat = self.sb(es, [128, 3 * TPS], F32, "stat")
            wrep = self.sb(es, [128, D], F32, "wrep")
            wbuf = [self.sb(es, [128, 16, 512], BF16, f"wbuf{i}") for i in range(3)]
            stg = [self.sb(es, [128, 512], F32, f"stg{i}") for i in range(3)]
            stgb = [self.sb(es, [128, 512], BF16, f"stgb{i}") for i in range(3)]
            self.wrr = 0
            self.srr = 0
            if lp is not None:
                yT = self.sb(es, [128, 32, TS], BF16, "yT")
                ybf = self.sb(es, [128, 4096], BF16, "ybf")
                ld = [self.sb(es, [128, 1024], F32, f"ld{i}") for i in range(4)]
                gst = self.sb(es, [128, 16], F32, "gst")
                grep = self.sb(es, [128, D], F32, "grep")
                pf = self.sb(es, [128, TPS, 256], F32, "pf")
                pb = self.sb(es, [128, TPS, 256], BF16, "pb")
                pT = self.sb(es, [128, 2, TS], BF16, "pT")
                wpj = [self.sb(es, [128, 2, 512], BF16, f"wpj{i}") for i in range(2)]
                sig = [self.sb(es, [128, 512], F32, f"sig{i}") for i in range(2)]
                if lp % 2 == 0:
                    kb.dma("sp", grep[:], I["ev_gnorm_w"][lp // 2:lp // 2 + 1, :].partition_broadcast(128),
                           writes=["grep"])

            def load_w(blk4, ncols=512, nkc=16):
                i = self.wrr % len(wbuf)
                self.wrr += 1
                t = wbuf[i]
                kb.dma("sp", t[:, 0:nkc, 0:ncols], blk4[:, 0:nkc, 0:ncols], writes=[f"wbuf{i}"])
                return t, f"wbuf{i}"

            def next_stg(bf=False):
                i = self.srr % 3
                self.srr += 1
                return (stgb[i], f"stgb{i}") if bf else (stg[i], f"stg{i}")

            for st in range(NST_RUN):
                t0 = st * TS
                srcH = I["x"] if lp is None else S["H"]
                kb.dma("sp", Hs[:], srcH[t0:t0 + TS, :].rearrange("(a p) d -> p a d", p=128), writes=["Hs"])
                if lp is not None:
                    e_idx = lp // 2
                    if lp % 2 == 1:
                        kb.dma("sp", yT[:], S["YOT"][:, t0:t0 + TS].rearrange("(kc p) t -> p kc t", p=128), writes=["yT"])
                    for tt in range(TPS if lp % 2 == 0 else 0):
                        r0 = t0 + tt * 128
                        if lp % 2 == 0:
                            for hf in range(2):
                                c0 = hf * 1024
                                a, b_, c_, d_ = ld[0:4]
                                ka, kb_, kc_, kd_ = [f"ld{j}" for j in range(4)]
                                kb.dma("sp", a[:], S["YS"][r0:r0 + 128, c0:c0 + 1024], writes=[ka])
                                kb.dma("sp", b_[:], S["SZ"][r0:r0 + 128, c0:c0 + 1024], writes=[kb_])
                                kb.dma("sp", c_[:], S["YN"][r0:r0 + 128, c0:c0 + 1024], writes=[kc_])
                                kb.dma("sp", d_[:], S["SG"][r0:r0 + 128, c0:c0 + 1024], writes=[kd_])
                                kb.op("dve", lambda e, a=a, b_=b_: e.tensor_tensor(out=a[:], in0=a[:], in1=b_[:], op=ALU.mult),
                                      reads=[ka, kb_], writes=[ka])
                                kb.op("pool", lambda e, a=a, b_=b_: e.tensor_tensor(out=b_[:], in0=a[:], in1=a[:], op=ALU.mult),
                                      reads=[ka], writes=[kb_])
                                kb.op("dve", lambda e, b_=b_, hf=hf: e.tensor_reduce(
                                    out=gst[:, hf * 4:hf * 4 + 4], in_=b_[:].rearrange("p (g c) -> p g c", g=4),
                                    axis=AX.X, op=ALU.add), reads=[kb_], writes=["gst"])
                                kb.op("act", lambda e, hf=hf: e.activation(out=gst[:, 8 + hf * 4:8 + hf * 4 + 4],
                                                                          in_=gst[:, hf * 4:hf * 4 + 4], func=AF.Sqrt,
                                                                          bias=EPS, scale=1.0 / 256), reads=["gst"], writes=["gst"])
                                kb.op("dve", lambda e, hf=hf: e.reciprocal(out=gst[:, hf * 4:hf * 4 + 4],
                                                                          in_=gst[:, 8 + hf * 4:8 + hf * 4 + 4]),
                                      reads=["gst"], writes=["gst"])
                                kb.op("dve", lambda e, a=a, hf=hf: e.tensor_tensor(
                                    out=a[:].rearrange("p (g c) -> p g c", g=4), in0=a[:].rearrange("p (g c) -> p g c", g=4),
                                    in1=_bc(gst[:, hf * 4:hf * 4 + 4], 256), op=ALU.mult), reads=[ka, "gst"], writes=[ka])
                                kb.op("dve", lambda e, a=a, c0=c0: e.tensor_tensor(out=ybf[:, c0:c0 + 1024], in0=a[:],
                                                                                 in1=grep[:, c0:c0 + 1024], op=ALU.mult),
                                      reads=[ka, "grep"], writes=["ybf"])
                                kb.op("pool", lambda e, c_=c_, d_=d_, c0=c0: e.tensor_tensor(
                                    out=ybf[:, 2048 + c0:2048 + c0 + 1024], in0=c_[:], in1=d_[:], op=ALU.mult),
                                    reads=[kc_, kd_], writes=["ybf"])
                        self.transpose_into(ybf, 32, yT, tt, "ybf", "yT")
                    wo = S["wk_ev_out"][e_idx] if lp % 2 == 0 else S["wk_od_out"][e_idx]
                    for nb in range(4):
                        wa, kwa = load_w(wo[nb * 2])
                        wb_, kwb = load_w(wo[nb * 2 + 1])
                        for tt in range(TPS):
                            b = self.next_bank()
                            pairs = [(yT[:, kc, tt * 128:(tt + 1) * 128], (wa if kc < 16 else wb_)[:, kc % 16, :])
                                     for kc in range(32)]
                            self.mm(self.bank(b), pairs, ["yT", kwa, kwb], f"ps{b}")
                            kb.op("dve", lambda e, tt=tt, nb=nb, b=b: e.tensor_tensor(
                                out=Hs[:, tt, nb * 512:(nb + 1) * 512], in0=self.bank(b),
                                in1=Hs[:, tt, nb * 512:(nb + 1) * 512], op=ALU.add),
                                reads=[f"ps{b}", "Hs"], writes=["Hs"])
                    if "hmix" in self.dbg and lp == 0:
                        kb.dma("pool", self.dbg["hmix"][t0:t0 + TS, :].rearrange("(a p) d -> p a d", p=128), Hs[:],
                               reads=["Hs"], writes=["dbg_hmix"])
                    kb.dma("sp", wrep[:], I["ple_norm_w"][lp:lp + 1, :].partition_broadcast(128), writes=["wrep"])
                    self.norm_T(es, Hs, wrep, AT, sq_junk, hn_tiles, stat, "Hs")
                    kb.dma("sp", pf[:], I["p"][lp, t0:t0 + TS, :].rearrange("(a p) d -> p a d", p=128), writes=["pf"])
                    kb.op("pool", lambda e: e.tensor_copy(out=pb[:], in_=pf[:]), reads=["pf"], writes=["pb"])
                    for tt in range(TPS):
                        b = self.next_bank()
                        for j in range(2):
                            kb.op("pe", lambda e, tt=tt, j=j, b=b: e.transpose(self.bank_bf(b, 128, j * 128),
                                                                           pb[:, tt, j * 128:(j + 1) * 128], self.ident_b[:]),
                                  reads=["pb", "ident_b"], writes=[f"ps{b}"], signal=(j == 1))
                        kb.op("act", lambda e, tt=tt, b=b: e.activation(
                            out=pT[:, :, tt * 128:(tt + 1) * 128],
                            in_=self.bank_bf(b, 256).rearrange("p (a c) -> p a c", a=2), func=AF.Copy),
                            reads=[f"ps{b}"], writes=["pT"])
                    for nb in range(4):
                        wg, kwg = load_w(S["wk_gate"][lp, nb])
                        wp = wpj[nb % 2]
                        kwp = f"wpj{nb % 2}"
                        kb.dma("sp", wp[:], S["wk_proj"][lp, nb], writes=[kwp])
                        for tt in range(TPS):
                            b = self.next_bank()
                            b2 = self.next_bank()
                            self.mm(self.bank(b), [(AT[:, kc, tt * 128:(tt + 1) * 128], wg[:, kc, :]) for kc in range(16)],
                                    ["AT", kwg], f"ps{b}")
                            self.mm(self.bank(b2), [(pT[:, kc, tt * 128:(tt + 1) * 128], wp[:, kc, :]) for kc in range(2)],
                                    ["pT", kwp], f"ps{b2}")
                            sg_ = sig[tt % 2]
                            ks = f"sig{tt % 2}"
                            kb.op("act", lambda e, sg_=sg_, b=b: e.activation(out=sg_[:], in_=self.bank(b), func=AF.Sigmoid),
                                  reads=[f"ps{b}"], writes=[ks])
                            kb.op("dve", lambda e, sg_=sg_, b2=b2: e.tensor_tensor(out=sg_[:], in0=self.bank(b2), in1=sg_[:], op=ALU.mult),
                                  reads=[f"ps{b2}", ks], writes=[ks])
                            kb.op("pool", lambda e, sg_=sg_, tt=tt, nb=nb: e.tensor_tensor(
                                out=Hs[:, tt, nb * 512:(nb + 1) * 512], in0=Hs[:, tt, nb * 512:(nb + 1) * 512],
                                in1=sg_[:], op=ALU.add), reads=[ks, "Hs"], writes=["Hs"])
                if ln is not None:
                    kb.dma("pool", S["H"][t0:t0 + TS, :].rearrange("(a p) d -> p a d", p=128), Hs[:], reads=["Hs"], writes=["Hd"])
                    if "hple" in self.dbg and lp == 0:
                        kb.dma("pool", self.dbg["hple"][t0:t0 + TS, :].rearrange("(a p) d -> p a d", p=128), Hs[:],
                               reads=["Hs"], writes=["dbg_hple"])
                else:
                    kb.dma("sp", wrep[:], I["final_norm_w"][0:1, :].partition_broadcast(128), writes=["wrep"])
                    ss = stat
                    for tt in range(TPS):
                        kb.op("act", lambda e, tt=tt: e.activation(out=sq_junk[:], in_=Hs[:, tt, :], func=AF.Square,
                                                                    accum_out=ss[:, tt:tt + 1]),
                              reads=["Hs"], writes=["sq_junk", "stat"])
                    kb.op("act", lambda e: e.activation(out=ss[:, TPS:2 * TPS], in_=ss[:, 0:TPS], func=AF.Sqrt,
                                                        bias=EPS, scale=1.0 / D), reads=["stat"], writes=["stat"])
                    kb.op("dve", lambda e: e.reciprocal(out=ss[:, 2 * TPS:3 * TPS], in_=ss[:, TPS:2 * TPS]),
                          reads=["stat"], writes=["stat"])
                    for tt in range(TPS):
                        kb.op("dve", lambda e, tt=tt: e.scalar_tensor_tensor(
                            out=Hs[:, tt, :], in0=Hs[:, tt, :], scalar=ss[:, 2 * TPS + tt:2 * TPS + tt + 1], in1=wrep[:],
                            op0=ALU.mult, op1=ALU.mult), reads=["Hs", "stat", "wrep"], writes=["Hs"])
                    kb.dma("pool", self.out[t0:t0 + TS, :].rearrange("(a p) d -> p a d", p=128), Hs[:], reads=["Hs"], writes=["outd"])
                    continue
                e2 = ln // 2
                nw = I["ev_norm_w"] if ln % 2 == 0 else I["od_norm_w"]
                kb.dma("sp", wrep[:], nw[e2:e2 + 1, :].partition_broadcast(128), writes=["wrep"])
                self.norm_T(es, Hs, wrep, AT, sq_junk, hn_tiles, stat, "Hs")
                if ln % 2 == 0:
                    self.inproj_even(e2, t0, AT, load_w, next_stg)
                else:
                    self.inproj_odd(e2, t0, AT, load_w, next_stg)

    def tok_block(self, AT, w, kw, tt, ncols=512):
        b = self.next_bank()
        self.mm(self.bank(b, ncols), [(AT[:, kc, tt * 128:(tt + 1) * 128], w[:, kc, 0:ncols]) for kc in range(16)],
                ["AT", kw], f"ps{b}")
        return b

    def feat_block(self, AT, w, kw, cl):
        b = self.next_bank()
        self.mm(self.bank(b), [(w[:, kc, cl * 128:(cl + 1) * 128], AT[:, kc, :]) for kc in range(16)],
                ["AT", kw], f"ps{b}")
        return b

    def inproj_even(self, e, t0, AT, load_w, next_stg):
        kb, S = self.kb, self.S
        W = S["wk_ev_in"][e]
        ci = {c: i for i, c in enumerate(EV_COLS)}
        for (c_base, dst, mode) in ((0, "SZ", "silu"), (12352, "SG", "silu"), (10304, "V", "bf")):
            for nb in range(4):
                w, kw = load_w(W[ci[c_base + nb * 512]])
                for tt in range(TPS):
                    b = self.tok_block(AT, w, kw, tt)
                    r0 = t0 + tt * 128
                    if mode == "silu":
                        s, ks = next_stg()
                        kb.op("act", lambda e_, s=s, b=b: e_.activation(out=s[:], in_=self.bank(b), func=AF.Silu),
                              reads=[f"ps{b}"], writes=[ks])
                    else:
                        s, ks = next_stg(bf=True)
                        kb.op("dve", lambda e_, s=s, b=b: e_.tensor_copy(out=s[:], in_=self.bank(b)),
                              reads=[f"ps{b}"], writes=[ks])
                    kb.dma("pool", S[dst][r0:r0 + 128, nb * 512:(nb + 1) * 512], s[:], reads=[ks], writes=[dst])
        for (c_base, nblk, dst, doff) in ((2048, 8, "XBCT", 2), (6208, 4, "QT", 0), (8256, 4, "KT", 0)):
            for nb in range(nblk):
                w, kw = load_w(W[ci[c_base + nb * 512]])
                for cl in range(4):
                    b = self.feat_block(AT, w, kw, cl)
                    s, ks = next_stg(bf=True)
                    eng = "act" if cl % 2 == 0 else "dve"
                    if eng == "act":
                        kb.op("act", lambda e_, s=s, b=b: e_.activation(out=s[:], in_=self.bank(b), func=AF.Copy),
                              reads=[f"ps{b}"], writes=[ks])
                    else:
                        kb.op("dve", lambda e_, s=s, b=b: e_.tensor_copy(out=s[:], in_=self.bank(b)),
                              reads=[f"ps{b}"], writes=[ks])
                    ch0 = (nb * 4 + cl) * 128
                    kb.dma("pool", S[dst][ch0:ch0 + 128, doff + t0:doff + t0 + TS], s[:], reads=[ks], writes=[dst])
        w, kw = load_w(W[28], 64)
        s, ks = next_stg()
        for tt in range(TPS):
            b = self.tok_block(AT, w, kw, tt, 64)
            kb.op("dve", lambda e_, s=s, b=b, tt=tt: e_.tensor_copy(out=s[:, tt * 64:(tt + 1) * 64], in_=self.bank(b, 64)),
                  reads=[f"ps{b}"], writes=[ks])
        kb.dma("pool", S["DT"][t0:t0 + TS, :].rearrange("(a p) c -> p a c", p=128),
               s[:, 0:TPS * 64].rearrange("p (a c) -> p a c", a=TPS), reads=[ks], writes=["DT"])

    def inproj_odd(self, e, t0, AT, load_w, next_stg):
        kb, S = self.kb, self.S
        W = S["wk_od_in"][e]
        for nb in range(8):
            wa, kwa = load_w(W[nb])
            wg, kwg = load_w(W[8 + nb])
            for cl in range(4):
                ba = self.feat_block(AT, wa, kwa, cl)
                bg = self.feat_block(AT, wg, kwg, cl)
                s, ks = next_stg()
                sb_, ksb = next_stg(bf=True)
                kb.op("act", lambda e_, s=s, bg=bg: e_.activation(out=s[:], in_=self.bank(bg), func=AF.Sigmoid),
                      reads=[f"ps{bg}"], writes=[ks])
                kb.op("dve", lambda e_, s=s, sb_=sb_, ba=ba: e_.tensor_tensor(out=sb_[:], in0=self.bank(ba), in1=s[:], op=ALU.mult),
                      reads=[f"ps{ba}", ks], writes=[ksb])
                ch0 = (nb * 4 + cl) * 128
                kb.dma("pool", S["VT"][ch0:ch0 + 128, 15 + t0:15 + t0 + TS], sb_[:], reads=[ksb], writes=["VT"])
        for nb in range(8):
            w, kw = load_w(W[16 + nb])
            for cl in range(4):
                b = self.feat_block(AT, w, kw, cl)
                s, ks = next_stg()
                kb.op("act", lambda e_, s=s, b=b: e_.activation(out=s[:], in_=self.bank(b), func=AF.Silu),
                      reads=[f"ps{b}"], writes=[ks])
                ch0 = (nb * 4 + cl) * 128
                kb.dma("pool", S["SG2T"][ch0:ch0 + 128, t0:t0 + TS], s[:], reads=[ks], writes=["SG2T"])

    def phase_E(self, layer):
        kb, I, S = self.kb, self.I, self.S
        e = layer // 2
        with contextlib.ExitStack() as es:
            C = {}
            C["tri"] = self.sb(es, [128, 3, 128], F32, "tri")
            C["maskb"] = self.sb(es, [128, 2, 512], BF16, "maskb")
            C["dtb"] = self.sb(es, [128, 64], F32, "dtb")
            C["arep"] = self.sb(es, [128, 64], F32, "arep")
            C["dsk"] = self.sb(es, [128, 32], F32, "dsk")
            kb.dma("sp", C["tri"][:], I["c_tri"].rearrange("a p c -> p a c"), writes=["tri"])
            kb.dma("pool", C["maskb"][:], I["c_mask"].rearrange("a p c -> p a c"), writes=["maskb"])
            kb.dma("sp", C["dtb"][:], I["ev_dt_bias"][e:e + 1, :].partition_broadcast(128), writes=["dtb"])
            kb.dma("sp", C["arep"][:], I["ev_a_log"][e:e + 1, :].partition_broadcast(128), writes=["arep"])
            kb.dma("sp", C["dsk"][:], I["ev_d_skip"][e:e + 1, :].partition_broadcast(128), writes=["dsk"])
            kb.op("act", lambda e_: e_.activation(out=C["arep"][:], in_=C["arep"][:], func=AF.Exp), reads=["arep"], writes=["arep"])
            kb.op("dve", lambda e_: e_.tensor_scalar(out=C["arep"][:], in0=C["arep"][:], scalar1=-1.0, scalar2=None, op0=ALU.mult),
                  reads=["arep"], writes=["arep"])
            for nm, shp in (("dtr", [128, 64]), ("x1", [128, 64]), ("dtv", [128, 64]), ("lndt", [128, 64]), ("da", [128, 64]),
                            ("nda", [128, 32]), ("cums", [128, 192]), ("Ein", [128, 6, 32]), ("Eout", [128, 6, 32]), ("biasf", [128, 32])):
                C[nm] = self.sb(es, shp, F32, nm)
            self.C = C
            if "A" in E_PARTS:
                self.ssd_pass_A(e)
                kb.barrier()
            if "C" in E_PARTS:
                self.ssd_pass_C(e)
                kb.barrier()
        if "N" in E_PARTS:
            self.na(e)

    def dtq(self, c):
        kb, S, C = self.kb, self.S, self.C
        K_ = ["dtq"]
        kb.dma("sp", C["dtr"][:], S["DT"][c * 128:(c + 1) * 128, :], reads=["DT"], writes=["dtr"])
        kb.op("dve", lambda e: e.tensor_tensor(out=C["x1"][:], in0=C["dtr"][:], in1=C["dtb"][:], op=ALU.add),
              reads=["dtr", "dtb"], writes=K_)
        kb.op("act", lambda e: e.activation(out=C["x1"][:], in_=C["x1"][:], func=AF.Exp), reads=K_, writes=K_)
        kb.op("act", lambda e: e.activation(out=C["dtv"][:], in_=C["x1"][:], func=AF.Ln, bias=1.0, scale=1.0), reads=K_, writes=K_)
        kb.op("act", lambda e: e.activation(out=C["lndt"][:], in_=C["dtv"][:], func=AF.Ln), reads=K_, writes=K_)
        kb.op("dve", lambda e: e.tensor_tensor(out=C["da"][:], in0=C["dtv"][:], in1=C["arep"][:], op=ALU.mult),
              reads=K_ + ["arep"], writes=K_)
        kb.op("dve", lambda e: e.tensor_scalar(out=C["nda"][:], in0=C["da"][:, 32:64], scalar1=-1.0, scalar2=None, op0=ALU.mult),
              reads=K_, writes=K_)
        b = self.next_bank()
        tri = C["tri"]
        for j in range(3):
            kb.op("pe", lambda e, j=j: e.matmul(self.bank(b, 64, j * 64), lhsT=tri[:, j, :], rhs=C["da"][:], start=True, stop=True),
                  reads=K_ + ["tri"], writes=[f"ps{b}"], signal=(j == 2))
        cums = C["cums"]
        kb.op("act", lambda e: e.activation(out=cums[:], in_=self.bank(b, 192), func=AF.Copy), reads=[f"ps{b}"], writes=K_)
        cI, cE, tot = cums[:, 0:64], cums[:, 64:128], cums[:, 128:192]
        Ein, Eout = C["Ein"], C["Eout"]
        R = K_
        kb.op("dve", lambda e: e.tensor_copy(out=Ein[:, 2, :], in_=cI[:, 0:32]), reads=R, writes=K_)
        kb.op("dve", lambda e: e.tensor_copy(out=Ein[:, 4:6, :], in_=tot.rearrange("p (a c) -> p a c", a=2)), reads=R, writes=K_)
        kb.op("dve", lambda e: e.tensor_tensor(out=Ein[:, 0, :], in0=tot[:, 0:32], in1=Ein[:, 2, :], op=ALU.subtract), reads=R, writes=K_)
        kb.op("dve", lambda e: e.tensor_tensor(out=Ein[:, 0, :], in0=Ein[:, 0, :], in1=C["lndt"][:, 0:32], op=ALU.add), reads=R, writes=K_)
        kb.op("dve", lambda e: e.tensor_tensor(out=Ein[:, 1, :], in0=cE[:, 32:64], in1=C["lndt"][:, 32:64], op=ALU.add), reads=R, writes=K_)
        kb.op("dve", lambda e: e.tensor_tensor(out=Ein[:, 3, :], in0=tot[:, 32:64], in1=cE[:, 32:64], op=ALU.subtract), reads=R, writes=K_)
        kb.op("dve", lambda e: e.tensor_tensor(out=C["biasf"][:], in0=C["lndt"][:, 0:32], in1=Ein[:, 2, :], op=ALU.subtract), reads=R, writes=K_)
        kb.op("act", lambda e: e.activation(out=Eout[:], in_=Ein[:], func=AF.Exp), reads=K_, writes=K_)

    def ssd_pass_A(self, e):
        kb, I, S, C = self.kb, self.I, self.S, self.C
        with contextlib.ExitStack() as es:
            wT = self.sb(es, [128, 32, 5], F32, "wT")
            diag = self.sb(es, [128, 32, 5, 128], BF16, "diag")
            cbias = self.sb(es, [128, 32], F32, "cbias")
            brow = self.sb(es, [1, 3072], BF16, "brow")
            xin = [self.sb(es, [128, 32, 516], BF16, f"xin{i}") for i in range(2)]
            xc = self.sb(es, [128, 4, 2048], BF16, "xc")
            btok = self.sb(es, [128, 4, 1024], BF16, "btok")
            bT = self.sb(es, [128, 8, 512], BF16, "bT")
            cT = self.sb(es, [128, 8, 512], BF16, "cT")
            Sf = self.sb(es, [128, 2048], F32, "Sf")
            xw = [self.sb(es, [128, 2048], BF16, f"xw{i}") for i in range(2)]
            stgb = self.sb(es, [128, 2048], BF16, "stgb")
            stgf = self.sb(es, [128, 2048], F32, "stgf")
            kb.dma("sp", wT[:], I["ev_conv_wT"][e].rearrange("(cb p) k -> p cb k", p=128), writes=["wT"])
            kb.dma("sp", cbias[:], I["ev_conv_bpm"][e], writes=["cbias"])
            kb.dma("pool", brow[:], I["ev_conv_b"][e:e + 1, 0:3072], writes=["brow"])
            for cb in range(32):
                for k in range(5):
                    kb.op("dve", lambda e_, cb=cb, k=k: e_.tensor_scalar(out=diag[:, cb, k, :], in0=self.ident_f[:],
                                                                       scalar1=wT[:, cb, k:k + 1], scalar2=None, op0=ALU.mult),
                          reads=["wT", "ident_f"], writes=["diag"])
            kb.op("dve", lambda e_: e_.memset(Sf[:], 0.0), writes=["Sf"])
            for blk in range(NT // 4):
                t0 = blk * 512
                xi = xin[blk % 2]
                kx = f"xin{blk % 2}"
                kb.dma("sp", xi[:], S["XBCT"][:, t0:t0 + 516].rearrange("(cb p) t -> p cb t", p=128), reads=["XBCT"], writes=[kx])
                for ci in range(4):
                    for cbg in range(6):
                        b = self.next_bank()
                        for j in range(4):
                            cb = cbg * 4 + j
                            pairs = [(xi[:, cb, ci * 128 + k:ci * 128 + k + 128], diag[:, cb, k, :]) for k in range(5)]
                            pairs.append((self.ones_b[0:1, :], brow[0:1, cb * 128:(cb + 1) * 128]))
                            self.mm(self.bank(b, 128, j * 128), pairs, [kx, "diag", "brow", "ones_b"], f"ps{b}")
                        dst = xc[:, ci, cbg * 512:(cbg + 1) * 512] if cbg < 4 else btok[:, ci, (cbg - 4) * 512:(cbg - 3) * 512]
                        kb.op("act", lambda e_, dst=dst, b=b: e_.activation(out=dst, in_=self.bank(b), func=AF.Silu),
                              reads=[f"ps{b}"], writes=["xc" if cbg < 4 else "btok"])
                for cb in range(16, 32):
                    b = self.next_bank()
                    pairs = [(diag[:, cb, k, :], xi[:, cb, k:k + 512]) for k in range(5)]
                    self.mm(self.bank(b), pairs, [kx, "diag"], f"ps{b}")
                    dst = bT[:, cb - 16, :] if cb < 24 else cT[:, cb - 24, :]
                    kb.op("act", lambda e_, dst=dst, b=b, cb=cb: e_.activation(out=dst, in_=self.bank(b), func=AF.Silu,
                                                                           bias=cbias[:, cb:cb + 1]),
                          reads=[f"ps{b}", "cbias"], writes=["bT" if cb < 24 else "cT"])
                kb.dma("pool", S["XC"][t0:t0 + 512, :].rearrange("(a p) d -> p a d", p=128), xc[:], reads=["xc"], writes=["XC"])
                kb.dma("pool", S["BT"][:, t0:t0 + 512].rearrange("(g p) t -> p g t", p=128), bT[:], reads=["bT"], writes=["BT"])
                kb.dma("pool", S["CT"][:, t0:t0 + 512].rearrange("(g p) t -> p g t", p=128), cT[:], reads=["cT"], writes=["CT"])
                for ci in range(4):
                    c = blk * 4 + ci
                    self.dtq(c)
                    Eout = C["Eout"]
                    for d_ in range(2):
                        xw_ = xw[d_]
                        kxw = f"xw{d_}"
                        kb.op("dve", lambda e_, xw_=xw_, ci=ci, d_=d_: e_.tensor_tensor(
                            out=xw_[:].rearrange("p (h c) -> p h c", h=32), in0=xc[:, ci, :].rearrange("p (h c) -> p h c", h=32),
                            in1=_bc(Eout[:, d_, :], 64), op=ALU.mult), reads=["xc", "dtq"], writes=[kxw])
                        banks = [self.next_bank() for _ in range(4)]
                        for g in range(8):
                            bk = banks[g // 2]
                            self.mm(self.bank(bk, 256, (g % 2) * 256), [(btok[:, ci, g * 128:(g + 1) * 128], xw_[:, g * 256:(g + 1) * 256])],
                                    ["btok", kxw], f"ps{bk}")
                        if d_ == 0:
                            kb.op("act", lambda e_: e_.activation(out=stgb[:], in_=Sf[:], func=AF.Copy), reads=["Sf"], writes=["stgb"])
                            kb.dma("pool", S["SPF"][c], stgb[:], reads=["stgb"], writes=["SPF"])
                            kb.op("dve", lambda e_: e_.tensor_tensor(out=Sf[:].rearrange("p (h c) -> p h c", h=32),
                                                                    in0=Sf[:].rearrange("p (h c) -> p h c", h=32),
                                                                    in1=_bc(Eout[:, 4, :], 64), op=ALU.mult),
                                  reads=["Sf", "dtq", "stgb"], writes=["Sf"])
                            for j, bk in enumerate(banks):
                                kb.op("dve", lambda e_, j=j, bk=bk: e_.tensor_tensor(out=Sf[:, j * 512:(j + 1) * 512], in0=self.bank(bk),
                                                                                 in1=Sf[:, j * 512:(j + 1) * 512], op=ALU.add),
                                      reads=[f"ps{bk}", "Sf"], writes=["Sf"])
                        else:
                            for j, bk in enumerate(banks):
                                kb.op("act", lambda e_, j=j, bk=bk: e_.activation(out=stgf[:, j * 512:(j + 1) * 512], in_=self.bank(bk), func=AF.Copy),
                                      reads=[f"ps{bk}"], writes=["stgf"])
                            kb.dma("pool", S["LSB"][c], stgf[:], reads=["stgf"], writes=["LSB"])

    def ssd_pass_C(self, e):
        kb, I, S, C = self.kb, self.I, self.S, self.C
        with contextlib.ExitStack() as es:
            xc = self.sb(es, [128, 2048], BF16, "xcC")
            bT = self.sb(es, [128, 8, 128], BF16, "bTC")
            cT = self.sb(es, [128, 8, 128], BF16, "cTC")
            spf = self.sb(es, [128, 2048], BF16, "spf")
            lsb = self.sb(es, [128, 2048], F32, "lsb")
            Sb = self.sb(es, [128, 2048], F32, "Sb")
            Sbb = self.sb(es, [128, 2048], BF16, "Sbb")
            Xf = self.sb(es, [128, 32, 128], F32, "Xf")
            Xb = self.sb(es, [128, 32, 128], F32, "Xb")
            G = [self.sb(es, [128, 4, 128], F32, f"G{i}") for i in range(4)]
            Wt = [self.sb(es, [128, 4, 128], BF16, f"Wt{i}") for i in range(2)]
            ys = self.sb(es, [128, 2048], F32, "ysC")
            t1 = self.sb(es, [128, 512], F32, "t1")
            t2 = self.sb(es, [128, 512], F32, "t2")
            xd = self.sb(es, [128, 2048], F32, "xd")
            ones_f = C["tri"][:, 2, :]
            kb.op("dve", lambda e_: e_.memset(Sb[:], 0.0), writes=["Sb"])
            kb.op("dve", lambda e_: e_.memset(Sbb[:], 0.0), writes=["Sbb"])
            for c in range(NT - 1, -1, -1):
                r0 = c * 128
                kb.dma("sp", xc[:], S["XC"][r0:r0 + 128, :], reads=["XC"], writes=["xcC"])
                kb.dma("sp", bT[:], S["BT"][:, r0:r0 + 128].rearrange("(g p) t -> p g t", p=128), reads=["BT"], writes=["bTC"])
                kb.dma("sp", cT[:], S["CT"][:, r0:r0 + 128].rearrange("(g p) t -> p g t", p=128), reads=["CT"], writes=["cTC"])
                kb.dma("sp", spf[:], S["SPF"][c], reads=["SPF"], writes=["spf"])
                kb.dma("sp", lsb[:], S["LSB"][c], reads=["LSB"], writes=["lsb"])
                self.dtq(c)
                Ein, Eout = C["Ein"], C["Eout"]
                kb.op("dve", lambda e_: e_.tensor_tensor(out=Xf[:], in0=_bc(C["da"][:, 0:32], 128),
                                                        in1=C["tri"][:, 0, :].unsqueeze(1).broadcast_to([128, 32, 128]), op=ALU.mult),
                      reads=["dtq", "tri"], writes=["Xf"])
                kb.op("dve", lambda e_: e_.tensor_tensor(out=Xb[:], in0=_bc(C["nda"][:, 0:32], 128),
                                                        in1=C["tri"][:, 1, :].unsqueeze(1).broadcast_to([128, 32, 128]), op=ALU.mult),
                      reads=["dtq", "tri"], writes=["Xb"])
                for q in range(4):
                    bY, bF, bB, bS = 0, 1, 2, 3
                    for gi in range(2):
                        g = q * 2 + gi
                        self.mm(self.bank(bS, 128, gi * 128), [(bT[:, g, :], cT[:, g, :])], ["bTC", "cTC"], f"ps{bS}")
                        for d_ in range(2):
                            bR = 4 + (self.psrr % 4)
                            self.psrr += 1
                            X_ = Xf if d_ == 0 else Xb
                            pairs = [(ones_f, X_[:, g * 4:(g + 1) * 4, :].rearrange("p a c -> p (a c)")),
                                     (self.ident_b[:], C["maskb"][:, d_, :])]
                            self.mm(self.bank(bR), pairs, ["Xf" if d_ == 0 else "Xb", "tri", "ident_b", "maskb"], f"ps{bR}")
                            Gd = G[gi * 2 + d_]
                            kG = f"G{gi * 2 + d_}"
                            for h in range(4):
                                bias = C["biasf"][:, g * 4 + h:g * 4 + h + 1] if d_ == 0 else Ein[:, 1, g * 4 + h:g * 4 + h + 1]
                                kb.op("act", lambda e_, Gd=Gd, h=h, bR=bR, bias=bias: e_.activation(
                                    out=Gd[:, h, :], in_=self.bank(bR, 128, h * 128), func=AF.Exp, bias=bias, scale=1.0),
                                    reads=[f"ps{bR}", "dtq"], writes=[kG])
                        Gf, Gb = G[gi * 2], G[gi * 2 + 1]
                        kb.op("pool", lambda e_, Gf=Gf, Gb=Gb: e_.tensor_tensor(out=Gf[:], in0=Gf[:], in1=Gb[:], op=ALU.add),
                              reads=[f"G{gi * 2}", f"G{gi * 2 + 1}"], writes=[f"G{gi * 2}"])
                        W_ = Wt[gi]
                        kb.op("dve", lambda e_, W_=W_, Gf=Gf, gi=gi, bS=bS: e_.tensor_tensor(
                            out=W_[:], in0=Gf[:], in1=self.bank(bS, 128, gi * 128).unsqueeze(1).broadcast_to([128, 4, 128]), op=ALU.mult),
                            reads=[f"G{gi * 2}", f"ps{bS}"], writes=[f"Wt{gi}"])
                        for h in range(4):
                            hh = g * 4 + h
                            self.mm(self.bank(bY, 64, (gi * 4 + h) * 64), [(W_[:, h, :], xc[:, hh * 64:(hh + 1) * 64])],
                                    [f"Wt{gi}", "xcC"], f"ps{bY}")
                        self.mm(self.bank(bF, 256, gi * 256), [(cT[:, g, :], spf[:, g * 256:(g + 1) * 256])], ["cTC", "spf"], f"ps{bF}")
                        self.mm(self.bank(bB, 256, gi * 256), [(cT[:, g, :], Sbb[:, g * 256:(g + 1) * 256])], ["cTC", "Sbb"], f"ps{bB}")
                    hs = slice(q * 8, q * 8 + 8)
                    kb.op("dve", lambda e_, hs=hs: e_.tensor_tensor(out=t1[:].rearrange("p (h c) -> p h c", h=8),
                                                                  in0=self.bank(bF).rearrange("p (h c) -> p h c", h=8),
                                                                  in1=_bc(Eout[:, 2, hs], 64), op=ALU.mult),
                          reads=[f"ps{bF}", "dtq"], writes=["t1"])
                    kb.op("dve", lambda e_, hs=hs: e_.tensor_tensor(out=t2[:].rearrange("p (h c) -> p h c", h=8),
                                                                  in0=self.bank(bB).rearrange("p (h c) -> p h c", h=8),
                                                                  in1=_bc(Eout[:, 3, hs], 64), op=ALU.mult),
                          reads=[f"ps{bB}", "dtq"], writes=["t2"])
                    kb.op("pool", lambda e_: e_.tensor_tensor(out=t1[:], in0=t1[:], in1=t2[:], op=ALU.add), reads=["t1", "t2"], writes=["t1"])
                    kb.op("dve", lambda e_, q=q: e_.tensor_tensor(out=ys[:, q * 512:(q + 1) * 512], in0=self.bank(bY), in1=t1[:], op=ALU.add),
                          reads=[f"ps{bY}", "t1"], writes=["ysC"])
                kb.op("pool", lambda e_: e_.tensor_tensor(out=xd[:].rearrange("p (h c) -> p h c", h=32),
                                                         in0=xc[:].rearrange("p (h c) -> p h c", h=32),
                                                         in1=_bc(C["dsk"][:, :], 64), op=ALU.mult), reads=["xcC", "dsk"], writes=["xd"])
                kb.op("dve", lambda e_: e_.tensor_tensor(out=ys[:], in0=ys[:], in1=xd[:], op=ALU.add), reads=["ysC", "xd"], writes=["ysC"])
                kb.dma("pool", S["YS"][r0:r0 + 128, :], ys[:], reads=["ysC"], writes=["YS"])
                kb.op("dve", lambda e_: e_.tensor_tensor(out=Sb[:].rearrange("p (h c) -> p h c", h=32),
                                                        in0=Sb[:].rearrange("p (h c) -> p h c", h=32),
                                                        in1=_bc(Eout[:, 5, :], 64), op=ALU.mult), reads=["Sb", "dtq"], writes=["Sb"])
                kb.op("dve", lambda e_: e_.tensor_tensor(out=Sb[:], in0=Sb[:], in1=lsb[:], op=ALU.add), reads=["Sb", "lsb"], writes=["Sb"])
                kb.op("act", lambda e_: e_.activation(out=Sbb[:], in_=Sb[:], func=AF.Copy), reads=["Sb"], writes=["Sbb"])

    def na(self, e):
        kb, I, S = self.kb, self.I, self.S
        rows = L // 64
        scale = 128 ** -0.5
        RB = 16
        with contextlib.ExitStack() as es:
            KTh = self.sb(es, [128, L], BF16, "KTh")
            QTh = self.sb(es, [128, L], BF16, "QTh")
            V0 = self.sb(es, [128, NT, 132], BF16, "V0")
            V1 = self.sb(es, [128, NT, 132], BF16, "V1")
            bt = self.sb(es, [128, 8, 256], F32, "bt")
            nam = self.sb(es, [128, 256], F32, "nam")
            s2 = [self.sb(es, [128, 256], F32, f"s2_{i}") for i in range(2)]
            pT = [self.sb(es, [128, 256], BF16, f"pT_{i}") for i in range(2)]
            rinv = [self.sb(es, [64, 1], F32, f"rinv{i}") for i in range(2)]
            obuf = [self.sb(es, [64, RB, 128], F32, f"obuf{i}") for i in range(2)]
            kb.dma("sp", nam[:], I["c_namask"], writes=["nam"])
            kb.op("dve", lambda e_: e_.memset(V0[:, :, 128:129], 1.0), writes=["V0"])
            kb.op("dve", lambda e_: e_.memset(V1[:, :, 128:129], 1.0), writes=["V1"])
            it = 0
            for h in range(16):
                kb.dma("sp", KTh[:], S["KT"][h * 128:(h + 1) * 128, :], reads=["KT"], writes=["KTh"])
                kb.dma("sp", QTh[:], S["QT"][h * 128:(h + 1) * 128, :], reads=["QT"], writes=["QTh"])
                kb.dma("sp", V0[:, :, 0:128], S["V"][:, h * 128:(h + 1) * 128].rearrange("(j p) d -> p j d", p=128), reads=["V"], writes=["V0"])
                kb.dma("sp", V1[:, 0:NT - 1, 0:128], S["V"][64:L - 64, h * 128:(h + 1) * 128].rearrange("(j p) d -> p j d", p=128),
                       reads=["V"], writes=["V1"])
                kb.dma("sp", bt[:], I["ev_rpbt"][e, h].rearrange("a p c -> p a c"), writes=["bt"])
                kb.op("dve", lambda e_: e_.tensor_tensor(out=bt[:], in0=bt[:], in1=nam[:].unsqueeze(1).broadcast_to([128, 8, 256]), op=ALU.add),
                      reads=["bt", "nam"], writes=["bt"])
                for r in range(rows):
                    rs = min(max(r - 4, 0), rows - 8)
                    pat = r - rs
                    ks = rs * 64
                    i2 = it % 2
                    it += 1
                    b = self.next_bank(0, 4)
                    for c_ in range(4):
                        self.mm(self.bank(b, 64, c_ * 64), [(KTh[:, ks + c_ * 128:ks + (c_ + 1) * 128], QTh[:, r * 64:(r + 1) * 64])],
                                ["KTh", "QTh"], f"ps{b}")
                    kb.op("dve", lambda e_, i2=i2, b=b, pat=pat: e_.scalar_tensor_tensor(
                        out=s2[i2][:], in0=self.bank(b, 256), scalar=scale, in1=bt[:, pat, :], op0=ALU.mult, op1=ALU.add),
                        reads=[f"ps{b}", "bt"], writes=[f"s2_{i2}"])
                    kb.op("act", lambda e_, i2=i2: e_.activation(out=pT[i2][:], in_=s2[i2][:], func=AF.Exp),
                          reads=[f"s2_{i2}"], writes=[f"pT_{i2}"])
                    b2 = self.next_bank(4, 8)
                    pairs = []
                    for c_ in range(4):
                        tk = ks + c_ * 128
                        vsel = V0[:, tk // 128, 0:129] if tk % 128 == 0 else V1[:, (tk - 64) // 128, 0:129]
                        pairs.append((pT[i2][:, c_ * 64:(c_ + 1) * 64], vsel))
                    self.mm(self.psum[0:64, b2 * 512:b2 * 512 + 129], pairs, [f"pT_{i2}", "V0", "V1"], f"ps{b2}")
                    kb.op("dve", lambda e_, i2=i2, b2=b2: e_.reciprocal(out=rinv[i2][:], in_=self.psum[0:64, b2 * 512 + 128:b2 * 512 + 129]),
                          reads=[f"ps{b2}"], writes=[f"rinv{i2}"])
                    ob = obuf[(r // RB) % 2]
                    kob = f"obuf{(r // RB) % 2}"
                    kb.op("act", lambda e_, ob=ob, r=r, i2=i2, b2=b2: e_.activation(
                        out=ob[:, r % RB, :], in_=self.psum[0:64, b2 * 512:b2 * 512 + 128], func=AF.Copy, scale=rinv[i2][:]),
                        reads=[f"ps{b2}", f"rinv{i2}"], writes=[kob])
                    if r % RB == RB - 1:
                        rr0 = (r - RB + 1) * 64
                        kb.dma("pool", S["YN"][rr0:rr0 + RB * 64, h * 128:(h + 1) * 128].rearrange("(r q) d -> q r d", q=64), ob[:],
                               reads=[kob], writes=["YN"])

    def phase_O(self, layer):
        kb, I, S = self.kb, self.I, self.S
        e = layer // 2
        NB = L // 512
        with contextlib.ExitStack() as es:
            wT = self.sb(es, [128, 32, 31], F32, "wT31")
            pm = self.sb(es, [128, 3, 32], F32, "pm31")
            dgs = [self.sb(es, [128, 31, 128], BF16, f"dg{i}") for i in range(2)]
            vins = [self.sb(es, [128, L + 30], BF16, f"vin{i}") for i in range(2)]
            stg = [self.sb(es, [128, 512], F32, f"stgO{i}") for i in range(3)]
            kb.dma("sp", wT[:], I["od_dw_wT"][e].rearrange("(cb p) k -> p cb k", p=128), writes=["wT31"])
            kb.dma("sp", pm[:], I["od_pm"][e].rearrange("a p c -> p a c"), writes=["pm31"])
            si = 0
            for cb in range(32):
                dg, kd = dgs[cb % 2], f"dg{cb % 2}"
                vi, kv = vins[cb % 2], f"vin{cb % 2}"
                for k in range(31):
                    kb.op("dve", lambda e_, k=k, dg=dg, cb=cb: e_.tensor_scalar(
                        out=dg[:, k, :], in0=self.ident_f[:], scalar1=wT[:, cb, k:k + 1], scalar2=None, op0=ALU.mult),
                        reads=["wT31", "ident_f"], writes=[kd])
                kb.dma("sp", vi[:], S["VT"][cb * 128:(cb + 1) * 128, :], reads=["VT"], writes=[kv])
                for tb in range(NB):
                    b = self.next_bank()
                    self.mm(self.bank(b), [(dg[:, k, :], vi[:, tb * 512 + k:tb * 512 + k + 512]) for k in range(31)],
                            [kd, kv], f"ps{b}")
                    s_, ks_ = stg[si % 3], f"stgO{si % 3}"
                    si += 1
                    kb.op("act", lambda e_, s_=s_, b=b, cb=cb: e_.activation(out=s_[:], in_=self.bank(b), func=AF.Identity,
                                                                         bias=pm[:, 0, cb:cb + 1], scale=1.0),
                          reads=[f"ps{b}", "pm31"], writes=[ks_])
                    kb.dma("pool", S["VCT"][cb * 128:(cb + 1) * 128, tb * 512:(tb + 1) * 512], s_[:], reads=[ks_], writes=["VCT"])
        kb.barrier()
        with contextlib.ExitStack() as es:
            pm = self.sb(es, [128, 3, 32], F32, "pm31b")
            ones_f = self.sb(es, [128, 128], F32, "ones_f")
            vblk = self.sb(es, [128, 32, 512], F32, "vblk")
            sq = [self.sb(es, [128, 512], F32, f"sq{i}") for i in range(2)]
            sgt = [self.sb(es, [128, 512], F32, f"sgt{i}") for i in range(3)]
            tt_ = [self.sb(es, [128, 512], F32, f"tO{i}") for i in range(2)]
            yb = [self.sb(es, [128, 512], BF16, f"ybO{i}") for i in range(3)]
            mean = self.sb(es, [128, 512], F32, "meanO")
            rstd = self.sb(es, [128, 512], F32, "rstdO")
            tmp = self.sb(es, [128, 512], F32, "tmpO")
            kb.dma("sp", pm[:], I["od_pm"][e].rearrange("a p c -> p a c"), writes=["pm31b"])
            kb.dma("sp", ones_f[:], I["c_tri"][2], writes=["ones_f"])
            for tb in range(NB):
                c0 = tb * 512
                kb.dma("sp", vblk[:], S["VCT"][:, c0:c0 + 512].rearrange("(cb p) t -> p cb t", p=128), reads=["VCT"], writes=["vblk"])
                bA, bB = 0, 1
                for cb in range(32):
                    q_, kq = sq[cb % 2], f"sq{cb % 2}"
                    kb.op("act", lambda e_, q_=q_, cb=cb: e_.activation(out=q_[:], in_=vblk[:, cb, :], func=AF.Square), reads=["vblk"], writes=[kq])
                    kb.op("pe", lambda e_, cb=cb: e_.matmul(self.bank(bA), lhsT=ones_f[:], rhs=vblk[:, cb, :], start=(cb == 0), stop=(cb == 31)),
                          reads=["vblk", "ones_f"], writes=[f"ps{bA}"], signal=(cb == 31))
                    kb.op("pe", lambda e_, cb=cb, q_=q_: e_.matmul(self.bank(bB), lhsT=ones_f[:], rhs=q_[:], start=(cb == 0), stop=(cb == 31)),
                          reads=[kq, "ones_f"], writes=[f"ps{bB}"], signal=True)
                kb.op("dve", lambda e_: e_.tensor_scalar(out=mean[:], in0=self.bank(bA), scalar1=1.0 / 4096, scalar2=None, op0=ALU.mult),
                      reads=[f"ps{bA}"], writes=["meanO"])
                kb.op("dve", lambda e_: e_.tensor_tensor(out=tmp[:], in0=mean[:], in1=mean[:], op=ALU.mult), reads=["meanO"], writes=["tmpO"])
                kb.op("dve", lambda e_: e_.scalar_tensor_tensor(out=tmp[:], in0=self.bank(bB), scalar=1.0 / 4096, in1=tmp[:],
                                                                op0=ALU.mult, op1=ALU.subtract), reads=[f"ps{bB}", "tmpO"], writes=["tmpO"])
                kb.op("act", lambda e_: e_.activation(out=tmp[:], in_=tmp[:], func=AF.Sqrt, bias=EPS, scale=1.0), reads=["tmpO"], writes=["tmpO"])
                kb.op("dve", lambda e_: e_.reciprocal(out=rstd[:], in_=tmp[:]), reads=["tmpO"], writes=["rstdO"])
                for cb in range(32):
                    g_, kg = sgt[cb % 3], f"sgt{cb % 3}"
                    t_, kt = tt_[cb % 2], f"tO{cb % 2}"
                    y_, ky = yb[cb % 3], f"ybO{cb % 3}"
                    kb.dma("sp", g_[:], S["SG2T"][cb * 128:(cb + 1) * 128, c0:c0 + 512], reads=["SG2T"], writes=[kg])
                    kb.op("dve", lambda e_, t_=t_, cb=cb: e_.tensor_tensor(out=t_[:], in0=vblk[:, cb, :], in1=mean[:], op=ALU.subtract),
                          reads=["vblk", "meanO"], writes=[kt])
                    kb.op("pool", lambda e_, t_=t_: e_.tensor_tensor(out=t_[:], in0=t_[:], in1=rstd[:], op=ALU.mult), reads=[kt, "rstdO"], writes=[kt])
                    kb.op("act", lambda e_, t_=t_, cb=cb: e_.activation(out=t_[:], in_=t_[:], func=AF.Silu, bias=pm[:, 2, cb:cb + 1],
                                                                     scale=pm[:, 1, cb:cb + 1]), reads=[kt, "pm31b"], writes=[kt])
                    kb.op("dve", lambda e_, t_=t_, g_=g_, y_=y_: e_.tensor_tensor(out=y_[:], in0=t_[:], in1=g_[:], op=ALU.mult),
                          reads=[kt, kg], writes=[ky])
                    kb.dma("pool", S["YOT"][cb * 128:(cb + 1) * 128, c0:c0 + 512], y_[:], reads=[ky], writes=["YOT"])


def _consts():
    c = {}
    c["c_ident"] = np.eye(128, dtype=np.float32)
    t = np.arange(128)
    tri = np.zeros((3, 128, 128), np.float32)
    tri[0] = (t[:, None] <= t[None, :])
    tri[1] = (t[:, None] < t[None, :])
    tri[2] = 1.0
    c["c_tri"] = tri
    m = np.zeros((2, 128, 512), np.float32)
    mf = np.where(t[:, None] <= t[None, :], 0.0, NEG)
    mb = np.where(t[:, None] >= t[None, :], 0.0, NEG)
    m[0] = np.tile(mf, (1, 4))
    m[1] = np.tile(mb, (1, 4))
    c["c_mask"] = m.astype(np.float32)
    cols = np.arange(64)
    cs = np.clip(cols - 8, 0, 48)
    valid = (cols[None, :] >= cs[:, None]) & (cols[None, :] < cs[:, None] + 16)
    mk = np.where(valid.T, 0.0, NEG).astype(np.float32)
    mk2 = np.concatenate([mk, mk], 0)
    c["c_namask"] = np.tile(mk2, (1, 4)).astype(np.float32)
    return c


def _rpb_table(rpb):
    cols = np.arange(64)
    coff = np.clip(cols[None, :] - cols[:, None], -15, 15) + 15
    out = np.zeros((2, 16, 8, 128, 256), np.float32)
    for pat in range(8):
        delta = -pat
        for c_ in range(4):
            for il in range(2):
                i = 2 * c_ + il
                roff = delta + i + 7
                g = rpb[:, :, roff, :][:, :, coff]
                out[:, :, pat, il * 64:(il + 1) * 64, c_ * 64:(c_ + 1) * 64] = np.transpose(g, (0, 1, 3, 2))
    return out


_NC_CACHE = {}


def kernel(x, p, ev_norm_w, ev_w_in, ev_conv_w, ev_conv_b, ev_dt_bias_f, ev_dt_bias_b,
           ev_a_log_f, ev_a_log_b, ev_d_skip, ev_gnorm_w, ev_rpb, ev_w_out,
           od_norm_w, od_w_in, od_dw_w, od_dw_b, od_ln_w, od_ln_b, od_w_out,
           ple_norm_w, ple_w_gate, ple_w_proj, final_norm_w):
    f = lambda a: np.ascontiguousarray(np.asarray(a, dtype=np.float32))
    if "nc" not in _NC_CACHE:
        _NC_CACHE["nc"] = Prog().build()
    nc = _NC_CACHE["nc"]
    shared = {
        "ev_norm_w": f(ev_norm_w), "ev_w_in": f(ev_w_in), "ev_conv_wT": f(np.transpose(f(ev_conv_w), (0, 2, 1))),
        "ev_conv_b": f(ev_conv_b), "ev_conv_bpm": f(np.transpose(f(ev_conv_b).reshape(2, 32, 128), (0, 2, 1))),
        "ev_dt_bias": f(np.concatenate([ev_dt_bias_f, ev_dt_bias_b], 1)),
        "ev_a_log": f(np.concatenate([ev_a_log_f, ev_a_log_b], 1)),
        "ev_d_skip": f(ev_d_skip), "ev_gnorm_w": f(ev_gnorm_w), "ev_rpbt": _rpb_table(f(ev_rpb)),
        "ev_w_out": f(ev_w_out), "od_norm_w": f(od_norm_w), "od_w_in": f(od_w_in), "od_dw_wT": f(np.transpose(f(od_dw_w), (0, 2, 1))),
        "od_pm": f(np.transpose(np.stack([f(od_dw_b), f(od_ln_w), f(od_ln_b)], 1).reshape(2, 3, 32, 128), (0, 1, 3, 2))),
        "od_w_out": f(od_w_out),
        "ple_norm_w": f(ple_norm_w), "ple_w_gate": f(ple_w_gate), "ple_w_proj": f(ple_w_proj),
        "final_norm_w": f(final_norm_w).reshape(1, D),
    }
    shared.update(_consts())
    x = f(x)
    p = f(p)
    in_maps = []
    for c in range(NCORES):
        b = c % 2
        m = dict(shared)
        m["x"] = np.ascontiguousarray(x[b, :L])
        m["p"] = np.ascontiguousarray(p[:, b, :L])
        in_maps.append(m)
    res = run_bass_kernel_spmd(nc, in_maps, core_ids=list(range(NCORES)))
    kernel.last = res
    return np.stack([res.results[0]["out"], res.results[1]["out"]], 0)
```

```python
import contextlib
import numpy as np
import ml_dtypes
import concourse.bass as bass
import concourse.mybir as mybir
from concourse.bass_utils import run_bass_kernel_spmd

F32 = mybir.dt.float32
BF16 = mybir.dt.bfloat16
AF = mybir.ActivationFunctionType
ALU = mybir.AluOpType
AX = mybir.AxisListType

L = 8192
D = 2048
TS = 512
TPS = TS // 128
NST = L // TS
NT = L // 128
EVEN_IN = 14400
ODD_IN = 12288
NEG = -30000.0
EPS = 1e-6
DEPTH = 4
EV_COLS = ([c for c in range(0, 2048, 512)] + [2048 + c for c in range(0, 4096, 512)] +
           [6208 + c for c in range(0, 8192, 512)])
NCORES = 2

DEBUG = {}
STOP_AFTER = None
NST_RUN = NST
PHASES = None
E_PARTS = "ACN"


def _set_L(n):
    global L, NST, NT, NST_RUN
    L = n
    NST = L // TS
    NT = L // 128
    NST_RUN = NST


class _Eng:
    def __init__(self, kb, name, h):
        self.kb = kb
        self.name = name
        self.h = h
        self.cnt = 0
        self.sem_ids = []
        self.waited = {}
        self.dsl = []
        self.drr = 0

    def event_for(self, cnt):
        cap = 30000
        idx = (cnt - 1) // cap
        while len(self.sem_ids) <= idx:
            self.sem_ids.append(self.kb.new_sem(f"{self.name}_c{len(self.sem_ids)}"))
        return (self.sem_ids[idx], (cnt - 1) % cap + 1)


class KB:
    NDSEM = 10

    def __init__(self, nc, es):
        self.nc = nc
        self.es = es
        self.sems = []
        self.state = {}
        self.eng = {}
        for name, h in (("pe", nc.tensor), ("act", nc.scalar), ("dve", nc.vector),
                        ("pool", nc.gpsimd), ("sp", nc.sync)):
            self.eng[name] = _Eng(self, name, h)
        self.ninst = 0

    def new_sem(self, name):
        s = self.es.enter_context(self.nc.semaphore(name))
        self.sems.append(s)
        return len(self.sems) - 1

    def _wait(self, e, ev):
        sid, val = ev
        if e.name == "pe" and sid in e.sem_ids:
            return
        if e.waited.get(sid, 0) < val:
            e.h.wait_ge(self.sems[sid], val)
            e.waited[sid] = val

    def _gather(self, reads, writes):
        evs = []
        for k in reads:
            st = self.state.get(k)
            if st is not None and st[0] is not None:
                evs.append(st[0])
        for k in writes:
            st = self.state.get(k)
            if st is not None:
                if st[0] is not None:
                    evs.append(st[0])
                evs.extend(st[1].items())
        return evs

    def _record(self, ev, reads, writes):
        for k in reads:
            st = self.state.setdefault(k, [None, {}])
            if st[1].get(ev[0], 0) < ev[1]:
                st[1][ev[0]] = ev[1]
        for k in writes:
            self.state[k] = [ev, {}]

    def op(self, engname, fn, reads=(), writes=(), signal=True):
        e = self.eng[engname]
        for ev in self._gather(reads, writes):
            self._wait(e, ev)
        inst = fn(e.h)
        self.ninst += 1
        if signal:
            e.cnt += 1
            ev = e.event_for(e.cnt)
            inst.then_inc(self.sems[ev[0]], 1)
        else:
            ev = e.event_for(e.cnt + 1)
        self._record(ev, reads, writes)
        return inst

    def dma(self, q, out, in_, reads=(), writes=()):
        e = self.eng[q]
        for ev in self._gather(reads, writes):
            self._wait(e, ev)
        if not e.dsl:
            e.dsl = [[self.new_sem(f"{q}_d{i}"), 0] for i in range(self.NDSEM)]
        slot = e.dsl[e.drr]
        e.drr = (e.drr + 1) % len(e.dsl)
        if slot[1] > 0:
            self._wait(e, (slot[0], slot[1] * 16))
        inst = e.h.dma_start(out=out, in_=in_)
        inst.then_inc(self.sems[slot[0]], 16)
        slot[1] += 1
        self.ninst += 1
        ev = (slot[0], slot[1] * 16)
        self._record(ev, reads, writes)
        return ev

    def barrier(self):
        evs = []
        for e in self.eng.values():
            if e.cnt > 0:
                evs.append(e.event_for(e.cnt))
            for slot in e.dsl:
                if slot[1] > 0:
                    evs.append((slot[0], slot[1] * 16))
        for e in self.eng.values():
            for ev in evs:
                sid, val = ev
                if e.waited.get(sid, 0) < val:
                    e.h.wait_ge(self.sems[sid], val)
                    e.waited[sid] = val
        self.state = {}


def _bc(ap2d, n):
    return ap2d.unsqueeze(2).broadcast_to([ap2d.shape[0], ap2d.shape[1], n])


class Prog:
    def __init__(self):
        self.nc = bass.Bass("TRN2", target_bir_lowering=False)
        self.nsb = 0

    def sb(self, es, shape, dt, name=None):
        self.nsb += 1
        return es.enter_context(self.nc.sbuf_tensor(f"{name or 'sb'}_{self.nsb}", list(shape), dt))

    def din(self, name, shape, dt=F32):
        return self.nc.dram_tensor(name, list(shape), dt, kind="ExternalInput").ap()

    def dscr(self, name, shape, dt):
        return self.nc.dram_tensor(name, list(shape), dt).ap()

    def build(self):
        nc = self.nc
        I = {}
        I["x"] = self.din("x", [L, D])
        I["p"] = self.din("p", [DEPTH, L, 256])
        I["ev_norm_w"] = self.din("ev_norm_w", [2, D])
        I["ev_w_in"] = self.din("ev_w_in", [2, D, EVEN_IN])
        I["ev_conv_wT"] = self.din("ev_conv_wT", [2, 4096, 5])
        I["ev_conv_b"] = self.din("ev_conv_b", [2, 4096])
        I["ev_conv_bpm"] = self.din("ev_conv_bpm", [2, 128, 32])
        I["ev_dt_bias"] = self.din("ev_dt_bias", [2, 64])
        I["ev_a_log"] = self.din("ev_a_log", [2, 64])
        I["ev_d_skip"] = self.din("ev_d_skip", [2, 32])
        I["ev_gnorm_w"] = self.din("ev_gnorm_w", [2, D])
        I["ev_rpbt"] = self.din("ev_rpbt", [2, 16, 8, 128, 256])
        I["ev_w_out"] = self.din("ev_w_out", [2, 4096, D])
        I["od_norm_w"] = self.din("od_norm_w", [2, D])
        I["od_w_in"] = self.din("od_w_in", [2, D, ODD_IN])
        I["od_dw_wT"] = self.din("od_dw_wT", [2, 4096, 31])
        I["od_pm"] = self.din("od_pm", [2, 3, 128, 32])
        I["od_w_out"] = self.din("od_w_out", [2, 4096, D])
        I["ple_norm_w"] = self.din("ple_norm_w", [DEPTH, D])
        I["ple_w_gate"] = self.din("ple_w_gate", [DEPTH, D, D])
        I["ple_w_proj"] = self.din("ple_w_proj", [DEPTH, 256, D])
        I["final_norm_w"] = self.din("final_norm_w", [1, D])
        I["c_ident"] = self.din("c_ident", [128, 128])
        I["c_tri"] = self.din("c_tri", [3, 128, 128])
        I["c_mask"] = self.din("c_mask", [2, 128, 512])
        I["c_namask"] = self.din("c_namask", [128, 256])
        self.I = I
        self.out = nc.dram_tensor("out", [L, D], F32, kind="ExternalOutput").ap()

        S = {}
        S["H"] = self.dscr("H", [L, D], F32)
        S["wk_ev_in"] = self.dscr("wk_ev_in", [2, 29, 128, 16, 512], BF16)
        S["wk_od_in"] = self.dscr("wk_od_in", [2, 24, 128, 16, 512], BF16)
        S["wk_ev_out"] = self.dscr("wk_ev_out", [2, 8, 128, 16, 512], BF16)
        S["wk_od_out"] = self.dscr("wk_od_out", [2, 8, 128, 16, 512], BF16)
        S["wk_gate"] = self.dscr("wk_gate", [DEPTH, 4, 128, 16, 512], BF16)
        S["wk_proj"] = self.dscr("wk_proj", [DEPTH, 4, 128, 2, 512], BF16)
        S["SZ"] = self.dscr("SZ", [L, D], F32)
        S["XBCT"] = self.dscr("XBCT", [4096, L + 4], BF16)
        S["DT"] = self.dscr("DT", [L, 64], F32)
        S["QT"] = self.dscr("QT", [D, L], BF16)
        S["KT"] = self.dscr("KT", [D, L], BF16)
        S["V"] = self.dscr("V", [L, D], BF16)
        S["SG"] = self.dscr("SG", [L, D], F32)
        S["YS"] = self.dscr("YS", [L, D], F32)
        S["YN"] = self.dscr("YN", [L, D], F32)
        S["XC"] = self.dscr("XC", [L, D], BF16)
        S["BTOK"] = self.dscr("BTOK", [L, 1024], BF16)
        S["BT"] = self.dscr("BT", [1024, L], BF16)
        S["CT"] = self.dscr("CT", [1024, L], BF16)
        S["SPF"] = self.dscr("SPF", [NT, 128, D], BF16)
        S["LSB"] = self.dscr("LSB", [NT, 128, D], F32)
        S["VT"] = self.dscr("VT", [4096, L + 30], BF16)
        S["VCT"] = self.dscr("VCT", [4096, L], F32)
        S["SG2T"] = self.dscr("SG2T", [4096, L], F32)
        S["YOT"] = self.dscr("YOT", [4096, L], BF16)
        self.S = S

        self.dbg = {}
        for name, (shape, _fn) in DEBUG.items():
            self.dbg[name] = nc.dram_tensor("dbg_" + name, list(shape), F32, kind="ExternalOutput").ap()

        with contextlib.ExitStack() as es:
            self.kb = KB(nc, es)
            kb = self.kb
            self.psum = es.enter_context(nc.psum_tensor("psum", [128, 4096], F32))
            self.psum_bf = self.psum[:].bitcast(BF16)
            self.ident_f = self.sb(es, [128, 128], F32, "ident_f")
            self.ident_b = self.sb(es, [128, 128], BF16, "ident_b")
            self.ones_b = self.sb(es, [128, 128], BF16, "ones_b")
            self.zero_b = self.sb(es, [128, 32], BF16, "zero_b")
            kb.dma("sp", self.ident_f[:], I["c_ident"][:, :], writes=["ident_f"])
            kb.op("dve", lambda e: e.tensor_copy(out=self.ident_b[:], in_=self.ident_f[:]),
                  reads=["ident_f"], writes=["ident_b"])
            kb.op("dve", lambda e: e.memset(self.ones_b[:], 1.0), writes=["ones_b"])
            kb.op("dve", lambda e: e.memset(self.zero_b[:], 0.0), writes=["zero_b"])
            self.psrr = 0

            self.phase_cast()
            kb.barrier()
            phases = [("P", None, 0), ("E", 0), ("P", 0, 1), ("O", 1), ("P", 1, 2), ("E", 2),
                      ("P", 2, 3), ("O", 3), ("P", 3, None)]
            if PHASES is not None:
                phases = PHASES
            for ph in phases:
                tag = "_".join(str(a) for a in ph)
                if ph[0] == "P":
                    self.phase_P(ph[1], ph[2])
                elif ph[0] == "E":
                    self.phase_E(ph[1])
                else:
                    self.phase_O(ph[1])
                kb.barrier()
                if STOP_AFTER == tag:
                    break
            self.dump_debug()
            kb.barrier()
        return nc

    def bank(self, b, n=512, off=0):
        return self.psum[:, b * 512 + off: b * 512 + off + n]

    def bank_bf(self, b, n=1024, off=0):
        return self.psum_bf[:, b * 1024 + off: b * 1024 + off + n]

    def next_bank(self, lo=0, hi=8):
        b = lo + self.psrr % (hi - lo)
        self.psrr += 1
        return b

    def mm(self, out_ap, pairs, reads, wkey, transpose=False):
        kb = self.kb
        n = len(pairs)
        for i, (l, r) in enumerate(pairs):
            kb.op("pe", lambda e, l=l, r=r, i=i: e.matmul(out_ap, lhsT=l, rhs=r, start=(i == 0), stop=(i == n - 1)),
                  reads=reads, writes=[wkey], signal=(i == n - 1))

    def dump_debug(self):
        kb = self.kb
        for name, ap in self.dbg.items():
            fn = DEBUG[name][1]
            if fn is None:
                continue
            kb.dma("pool", ap, fn(self), writes=["dbg_" + name])

    def phase_cast(self):
        kb, I, S = self.kb, self.I, self.S
        def blk(dst4, src2d, k0, c0, nkc, ncols):
            kb.dma("pool", dst4[:, 0:nkc, 0:ncols],
                   src2d[k0:k0 + nkc * 128, c0:c0 + ncols].rearrange("(kc p) c -> p kc c", p=128))
        for e in range(2):
            for i, c0 in enumerate(EV_COLS):
                blk(S["wk_ev_in"][e, i], I["ev_w_in"][e], 0, c0, 16, 512)
            blk(S["wk_ev_in"][e, 28], I["ev_w_in"][e], 0, 6144, 16, 64)
            for i in range(24):
                blk(S["wk_od_in"][e, i], I["od_w_in"][e], 0, i * 512, 16, 512)
            for nb in range(4):
                for hf in range(2):
                    blk(S["wk_ev_out"][e, nb * 2 + hf], I["ev_w_out"][e], hf * 2048, nb * 512, 16, 512)
                    blk(S["wk_od_out"][e, nb * 2 + hf], I["od_w_out"][e], hf * 2048, nb * 512, 16, 512)
        for l in range(DEPTH):
            for nb in range(4):
                blk(S["wk_gate"][l, nb], I["ple_w_gate"][l], 0, nb * 512, 16, 512)
                blk(S["wk_proj"][l, nb], I["ple_w_proj"][l], 0, nb * 512, 2, 512)
        zt = self.zero_b
        for cb in range(32):
            kb.dma("sp", S["XBCT"][cb * 128:(cb + 1) * 128, 0:2], zt[:, 0:2], reads=["zero_b"])
            kb.dma("sp", S["XBCT"][cb * 128:(cb + 1) * 128, L + 2:L + 4], zt[:, 0:2], reads=["zero_b"])
            kb.dma("sp", S["VT"][cb * 128:(cb + 1) * 128, 0:15], zt[:, 0:15], reads=["zero_b"])
            kb.dma("sp", S["VT"][cb * 128:(cb + 1) * 128, L + 15:L + 30], zt[:, 0:15], reads=["zero_b"])

    def norm_T(self, es_tiles, Hs, wrep, AT, sq_junk, hn_tiles, stat, hkey):
        kb = self.kb
        ss = stat
        for tt in range(TPS):
            kb.op("act", lambda e, tt=tt: e.activation(out=sq_junk[:], in_=Hs[:, tt, :], func=AF.Square,
                                                        accum_out=ss[:, tt:tt + 1]),
                  reads=[hkey], writes=["sq_junk", "stat"])
        kb.op("act", lambda e: e.activation(out=ss[:, TPS:2 * TPS], in_=ss[:, 0:TPS], func=AF.Sqrt,
                                            bias=EPS, scale=1.0 / D), reads=["stat"], writes=["stat"])
        kb.op("dve", lambda e: e.reciprocal(out=ss[:, 2 * TPS:3 * TPS], in_=ss[:, TPS:2 * TPS]),
              reads=["stat"], writes=["stat"])
        for tt in range(TPS):
            hn = hn_tiles[tt % len(hn_tiles)]
            hk = f"hn{tt % len(hn_tiles)}"
            kb.op("dve", lambda e, tt=tt, hn=hn: e.scalar_tensor_tensor(
                out=hn[:], in0=Hs[:, tt, :], scalar=ss[:, 2 * TPS + tt:2 * TPS + tt + 1], in1=wrep[:],
                op0=ALU.mult, op1=ALU.mult), reads=[hkey, "stat", "wrep"], writes=[hk])
            self.transpose_into(hn, 16, AT, tt, hk, "AT")

    def transpose_into(self, src, nkc, dstT, tt, skey, dkey):
        kb = self.kb
        for g0 in range(0, nkc, 8):
            b = self.next_bank()
            n = min(8, nkc - g0)
            for j in range(n):
                kc = g0 + j
                kb.op("pe", lambda e, kc=kc, j=j, b=b: e.transpose(self.bank_bf(b, 128, j * 128),
                                                                src[:, kc * 128:(kc + 1) * 128], self.ident_b[:]),
                      reads=[skey, "ident_b"], writes=[f"ps{b}"], signal=(j == n - 1))
            eng = "act" if (self.psrr % 2 == 0) else "dve"
            dst = dstT[:, g0:g0 + n, tt * 128:(tt + 1) * 128]
            srcp = self.bank_bf(b, n * 128).rearrange("p (a c) -> p a c", a=n)
            if eng == "act":
                kb.op("act", lambda e, dst=dst, srcp=srcp: e.activation(out=dst, in_=srcp, func=AF.Copy),
                      reads=[f"ps{b}"], writes=[dkey])
            else:
                kb.op("dve", lambda e, dst=dst, srcp=srcp: e.tensor_copy(out=dst, in_=srcp),
                      reads=[f"ps{b}"], writes=[dkey])

    def phase_P(self, lp, ln):
        kb, I, S, nc = self.kb, self.I, self.S, self.nc
        with contextlib.ExitStack() as es:
            Hs = self.sb(es, [128, TPS, D], F32, "Hs")
            AT = self.sb(es, [128, 16, TS], BF16, "AT")
            hn_tiles = [self.sb(es, [128, D], BF16, f"hn{i}") for i in range(2)]
            sq_junk = self.sb(es, [128, D], BF16, "sq_junk")
            stat = self.sb(es, [128, 3 * TPS], F32, "stat")
            wrep = self.sb(es, [128, D], F32, "wrep")
            wbuf = [self.sb(es, [128, 16, 512], BF16, f"wbuf{i}") for i in range(3)]
            stg = [self.sb(es, [128, 512], F32, f"stg{i}") for i in range(3)]
            stgb = [self.sb(es, [128, 512], BF16, f"stgb{i}") for i in range(3)]
            self.wrr = 0
            self.srr = 0
            if lp is not None:
                yT = self.sb(es, [128, 32, TS], BF16, "yT")
                ybf = self.sb(es, [128, 4096], BF16, "ybf")
                ld = [self.sb(es, [128, 1024], F32, f"ld{i}") for i in range(4)]
                gst = self.sb(es, [128, 16], F32, "gst")
                grep = self.sb(es, [128, D], F32, "grep")
                pf = self.sb(es, [128, TPS, 256], F32, "pf")
                pb = self.sb(es, [128, TPS, 256], BF16, "pb")
                pT = self.sb(es, [128, 2, TS], BF16, "pT")
                wpj = [self.sb(es, [128, 2, 512], BF16, f"wpj{i}") for i in range(2)]
                sig = [self.sb(es, [128, 512], F32, f"sig{i}") for i in range(2)]
                if lp % 2 == 0:
                    kb.dma("sp", grep[:], I["ev_gnorm_w"][lp // 2:lp // 2 + 1, :].partition_broadcast(128),
                           writes=["grep"])

            def load_w(blk4, ncols=512, nkc=16):
                i = self.wrr % len(wbuf)
                self.wrr += 1
                t = wbuf[i]
                kb.dma("sp", t[:, 0:nkc, 0:ncols], blk4[:, 0:nkc, 0:ncols], writes=[f"wbuf{i}"])
                return t, f"wbuf{i}"

            def next_stg(bf=False):
                i = self.srr % 3
                self.srr += 1
                return (stgb[i], f"stgb{i}") if bf else (stg[i], f"stg{i}")

            for st in range(NST_RUN):
                t0 = st * TS
                srcH = I["x"] if lp is None else S["H"]
                kb.dma("sp", Hs[:], srcH[t0:t0 + TS, :].rearrange("(a p) d -> p a d", p=128), writes=["Hs"])
                if lp is not None:
                    e_idx = lp // 2
                    if lp % 2 == 1:
                        kb.dma("sp", yT[:], S["YOT"][:, t0:t0 + TS].rearrange("(kc p) t -> p kc t", p=128), writes=["yT"])
                    for tt in range(TPS if lp % 2 == 0 else 0):
                        r0 = t0 + tt * 128
                        if lp % 2 == 0:
                            for hf in range(2):
                                c0 = hf * 1024
                                a, b_, c_, d_ = ld[0:4]
                                ka, kb_, kc_, kd_ = [f"ld{j}" for j in range(4)]
                                kb.dma("sp", a[:], S["YS"][r0:r0 + 128, c0:c0 + 1024], writes=[ka])
                                kb.dma("sp", b_[:], S["SZ"][r0:r0 + 128, c0:c0 + 1024], writes=[kb_])
                                kb.dma("sp", c_[:], S["YN"][r0:r0 + 128, c0:c0 + 1024], writes=[kc_])
                                kb.dma("sp", d_[:], S["SG"][r0:r0 + 128, c0:c0 + 1024], writes=[kd_])
                                kb.op("dve", lambda e, a=a, b_=b_: e.tensor_tensor(out=a[:], in0=a[:], in1=b_[:], op=ALU.mult),
                                      reads=[ka, kb_], writes=[ka])
                                kb.op("pool", lambda e, a=a, b_=b_: e.tensor_tensor(out=b_[:], in0=a[:], in1=a[:], op=ALU.mult),
                                      reads=[ka], writes=[kb_])
                                kb.op("dve", lambda e, b_=b_, hf=hf: e.tensor_reduce(
                                    out=gst[:, hf * 4:hf * 4 + 4], in_=b_[:].rearrange("p (g c) -> p g c", g=4),
                                    axis=AX.X, op=ALU.add), reads=[kb_], writes=["gst"])
                                kb.op("act", lambda e, hf=hf: e.activation(out=gst[:, 8 + hf * 4:8 + hf * 4 + 4],
                                                                          in_=gst[:, hf * 4:hf * 4 + 4], func=AF.Sqrt,
                                                                          bias=EPS, scale=1.0 / 256), reads=["gst"], writes=["gst"])
                                kb.op("dve", lambda e, hf=hf: e.reciprocal(out=gst[:, hf * 4:hf * 4 + 4],
                                                                          in_=gst[:, 8 + hf * 4:8 + hf * 4 + 4]),
                                      reads=["gst"], writes=["gst"])
                                kb.op("dve", lambda e, a=a, hf=hf: e.tensor_tensor(
                                    out=a[:].rearrange("p (g c) -> p g c", g=4), in0=a[:].rearrange("p (g c) -> p g c", g=4),
                                    in1=_bc(gst[:, hf * 4:hf * 4 + 4], 256), op=ALU.mult), reads=[ka, "gst"], writes=[ka])
                                kb.op("dve", lambda e, a=a, c0=c0: e.tensor_tensor(out=ybf[:, c0:c0 + 1024], in0=a[:],
                                                                                 in1=grep[:, c0:c0 + 1024], op=ALU.mult),
                                      reads=[ka, "grep"], writes=["ybf"])
                                kb.op("pool", lambda e, c_=c_, d_=d_, c0=c0: e.tensor_tensor(
                                    out=ybf[:, 2048 + c0:2048 + c0 + 1024], in0=c_[:], in1=d_[:], op=ALU.mult),
                                    reads=[kc_, kd_], writes=["ybf"])
                        self.transpose_into(ybf, 32, yT, tt, "ybf", "yT")
                    wo = S["wk_ev_out"][e_idx] if lp % 2 == 0 else S["wk_od_out"][e_idx]
                    for nb in range(4):
                        wa, kwa = load_w(wo[nb * 2])
                        wb_, kwb = load_w(wo[nb * 2 + 1])
                        for tt in range(TPS):
                            b = self.next_bank()
                            pairs = [(yT[:, kc, tt * 128:(tt + 1) * 128], (wa if kc < 16 else wb_)[:, kc % 16, :])
                                     for kc in range(32)]
                            self.mm(self.bank(b), pairs, ["yT", kwa, kwb], f"ps{b}")
                            kb.op("dve", lambda e, tt=tt, nb=nb, b=b: e.tensor_tensor(
                                out=Hs[:, tt, nb * 512:(nb + 1) * 512], in0=self.bank(b),
                                in1=Hs[:, tt, nb * 512:(nb + 1) * 512], op=ALU.add),
                                reads=[f"ps{b}", "Hs"], writes=["Hs"])
                    if "hmix" in self.dbg and lp == 0:
                        kb.dma("pool", self.dbg["hmix"][t0:t0 + TS, :].rearrange("(a p) d -> p a d", p=128), Hs[:],
                               reads=["Hs"], writes=["dbg_hmix"])
                    kb.dma("sp", wrep[:], I["ple_norm_w"][lp:lp + 1, :].partition_broadcast(128), writes=["wrep"])
                    self.norm_T(es, Hs, wrep, AT, sq_junk, hn_tiles, stat, "Hs")
                    kb.dma("sp", pf[:], I["p"][lp, t0:t0 + TS, :].rearrange("(a p) d -> p a d", p=128), writes=["pf"])
                    kb.op("pool", lambda e: e.tensor_copy(out=pb[:], in_=pf[:]), reads=["pf"], writes=["pb"])
                    for tt in range(TPS):
                        b = self.next_bank()
                        for j in range(2):
                            kb.op("pe", lambda e, tt=tt, j=j, b=b: e.transpose(self.bank_bf(b, 128, j * 128),
                                                                           pb[:, tt, j * 128:(j + 1) * 128], self.ident_b[:]),
                                  reads=["pb", "ident_b"], writes=[f"ps{b}"], signal=(j == 1))
                        kb.op("act", lambda e, tt=tt, b=b: e.activation(
                            out=pT[:, :, tt * 128:(tt + 1) * 128],
                            in_=self.bank_bf(b, 256).rearrange("p (a c) -> p a c", a=2), func=AF.Copy),
                            reads=[f"ps{b}"], writes=["pT"])
                    for nb in range(4):
                        wg, kwg = load_w(S["wk_gate"][lp, nb])
                        wp = wpj[nb % 2]
                        kwp = f"wpj{nb % 2}"
                        kb.dma("sp", wp[:], S["wk_proj"][lp, nb], writes=[kwp])
                        for tt in range(TPS):
                            b = self.next_bank()
                            b2 = self.next_bank()
                            self.mm(self.bank(b), [(AT[:, kc, tt * 128:(tt + 1) * 128], wg[:, kc, :]) for kc in range(16)],
                                    ["AT", kwg], f"ps{b}")
                            self.mm(self.bank(b2), [(pT[:, kc, tt * 128:(tt + 1) * 128], wp[:, kc, :]) for kc in range(2)],
                                    ["pT", kwp], f"ps{b2}")
                            sg_ = sig[tt % 2]
                            ks = f"sig{tt % 2}"
                            kb.op("act", lambda e, sg_=sg_, b=b: e.activation(out=sg_[:], in_=self.bank(b), func=AF.Sigmoid),
                                  reads=[f"ps{b}"], writes=[ks])
                            kb.op("dve", lambda e, sg_=sg_, b2=b2: e.tensor_tensor(out=sg_[:], in0=self.bank(b2), in1=sg_[:], op=ALU.mult),
                                  reads=[f"ps{b2}", ks], writes=[ks])
                            kb.op("pool", lambda e, sg_=sg_, tt=tt, nb=nb: e.tensor_tensor(
                                out=Hs[:, tt, nb * 512:(nb + 1) * 512], in0=Hs[:, tt, nb * 512:(nb + 1) * 512],
                                in1=sg_[:], op=ALU.add), reads=[ks, "Hs"], writes=["Hs"])
                if ln is not None:
                    kb.dma("pool", S["H"][t0:t0 + TS, :].rearrange("(a p) d -> p a d", p=128), Hs[:], reads=["Hs"], writes=["Hd"])
                    if "hple" in self.dbg and lp == 0:
                        kb.dma("pool", self.dbg["hple"][t0:t0 + TS, :].rearrange("(a p) d -> p a d", p=128), Hs[:],
                               reads=["Hs"], writes=["dbg_hple"])
                else:
                    kb.dma("sp", wrep[:], I["final_norm_w"][0:1, :].partition_broadcast(128), writes=["wrep"])
                    ss = stat
                    for tt in range(TPS):
                        kb.op("act", lambda e, tt=tt: e.activation(out=sq_junk[:], in_=Hs[:, tt, :], func=AF.Square,
                                                                    accum_out=ss[:, tt:tt + 1]),
                              reads=["Hs"], writes=["sq_junk", "stat"])
                    kb.op("act", lambda e: e.activation(out=ss[:, TPS:2 * TPS], in_=ss[:, 0:TPS], func=AF.Sqrt,
                                                        bias=EPS, scale=1.0 / D), reads=["stat"], writes=["stat"])
                    kb.op("dve", lambda e: e.reciprocal(out=ss[:, 2 * TPS:3 * TPS], in_=ss[:, TPS:2 * TPS]),
                          reads=["stat"], writes=["stat"])
                    for tt in range(TPS):
                        kb.op("dve", lambda e, tt=tt: e.scalar_tensor_tensor(
                            out=Hs[:, tt, :], in0=Hs[:, tt, :], scalar=ss[:, 2 * TPS + tt:2 * TPS + tt + 1], in1=wrep[:],
                            op0=ALU.mult, op1=ALU.mult), reads=["Hs", "stat", "wrep"], writes=["Hs"])
                    kb.dma("pool", self.out[t0:t0 + TS, :].rearrange("(a p) d -> p a d", p=128), Hs[:], reads=["Hs"], writes=["outd"])
                    continue
                e2 = ln // 2
                nw = I["ev_norm_w"] if ln % 2 == 0 else I["od_norm_w"]
                kb.dma("sp", wrep[:], nw[e2:e2 + 1, :].partition_broadcast(128), writes=["wrep"])
                self.norm_T(es, Hs, wrep, AT, sq_junk, hn_tiles, stat, "Hs")
                if ln % 2 == 0:
                    self.inproj_even(e2, t0, AT, load_w, next_stg)
                else:
                    self.inproj_odd(e2, t0, AT, load_w, next_stg)

    def tok_block(self, AT, w, kw, tt, ncols=512):
        b = self.next_bank()
        self.mm(self.bank(b, ncols), [(AT[:, kc, tt * 128:(tt + 1) * 128], w[:, kc, 0:ncols]) for kc in range(16)],
                ["AT", kw], f"ps{b}")
        return b

    def feat_block(self, AT, w, kw, cl):
        b = self.next_bank()
        self.mm(self.bank(b), [(w[:, kc, cl * 128:(cl + 1) * 128], AT[:, kc, :]) for kc in range(16)],
                ["AT", kw], f"ps{b}")
        return b

    def inproj_even(self, e, t0, AT, load_w, next_stg):
        kb, S = self.kb, self.S
        W = S["wk_ev_in"][e]
        ci = {c: i for i, c in enumerate(EV_COLS)}
        for (c_base, dst, mode) in ((0, "SZ", "silu"), (12352, "SG", "silu"), (10304, "V", "bf")):
            for nb in range(4):
                w, kw = load_w(W[ci[c_base + nb * 512]])
                for tt in range(TPS):
                    b = self.tok_block(AT, w, kw, tt)
                    r0 = t0 + tt * 128
                    if mode == "silu":
                        s, ks = next_stg()
                        kb.op("act", lambda e_, s=s, b=b: e_.activation(out=s[:], in_=self.bank(b), func=AF.Silu),
                              reads=[f"ps{b}"], writes=[ks])
                    else:
                        s, ks = next_stg(bf=True)
                        kb.op("dve", lambda e_, s=s, b=b: e_.tensor_copy(out=s[:], in_=self.bank(b)),
                              reads=[f"ps{b}"], writes=[ks])
                    kb.dma("pool", S[dst][r0:r0 + 128, nb * 512:(nb + 1) * 512], s[:], reads=[ks], writes=[dst])
        for (c_base, nblk, dst, doff) in ((2048, 8, "XBCT", 2), (6208, 4, "QT", 0), (8256, 4, "KT", 0)):
            for nb in range(nblk):
                w, kw = load_w(W[ci[c_base + nb * 512]])
                for cl in range(4):
                    b = self.feat_block(AT, w, kw, cl)
                    s, ks = next_stg(bf=True)
                    eng = "act" if cl % 2 == 0 else "dve"
                    if eng == "act":
                        kb.op("act", lambda e_, s=s, b=b: e_.activation(out=s[:], in_=self.bank(b), func=AF.Copy),
                              reads=[f"ps{b}"], writes=[ks])
                    else:
                        kb.op("dve", lambda e_, s=s, b=b: e_.tensor_copy(out=s[:], in_=self.bank(b)),
                              reads=[f"ps{b}"], writes=[ks])
                    ch0 = (nb * 4 + cl) * 128
                    kb.dma("pool", S[dst][ch0:ch0 + 128, doff + t0:doff + t0 + TS], s[:], reads=[ks], writes=[dst])
        w, kw = load_w(W[28], 64)
        s, ks = next_stg()
        for tt in range(TPS):
            b = self.tok_block(AT, w, kw, tt, 64)
            kb.op("dve", lambda e_, s=s, b=b, tt=tt: e_.tensor_copy(out=s[:, tt * 64:(tt + 1) * 64], in_=self.bank(b, 64)),
                  reads=[f"ps{b}"], writes=[ks])
        kb.dma("pool", S["DT"][t0:t0 + TS, :].rearrange("(a p) c -> p a c", p=128),
               s[:, 0:TPS * 64].rearrange("p (a c) -> p a c", a=TPS), reads=[ks], writes=["DT"])

    def inproj_odd(self, e, t0, AT, load_w, next_stg):
        kb, S = self.kb, self.S
        W = S["wk_od_in"][e]
        for nb in range(8):
            wa, kwa = load_w(W[nb])
            wg, kwg = load_w(W[8 + nb])
            for cl in range(4):
                ba = self.feat_block(AT, wa, kwa, cl)
                bg = self.feat_block(AT, wg, kwg, cl)
                s, ks = next_stg()
                sb_, ksb = next_stg(bf=True)
                kb.op("act", lambda e_, s=s, bg=bg: e_.activation(out=s[:], in_=self.bank(bg), func=AF.Sigmoid),
                      reads=[f"ps{bg}"], writes=[ks])
                kb.op("dve", lambda e_, s=s, sb_=sb_, ba=ba: e_.tensor_tensor(out=sb_[:], in0=self.bank(ba), in1=s[:], op=ALU.mult),
                      reads=[f"ps{ba}", ks], writes=[ksb])
                ch0 = (nb * 4 + cl) * 128
                kb.dma("pool", S["VT"][ch0:ch0 + 128, 15 + t0:15 + t0 + TS], sb_[:], reads=[ksb], writes=["VT"])
        for nb in range(8):
            w, kw = load_w(W[16 + nb])
            for cl in range(4):
                b = self.feat_block(AT, w, kw, cl)
                s, ks = next_stg()
                kb.op("act", lambda e_, s=s, b=b: e_.activation(out=s[:], in_=self.bank(b), func=AF.Silu),
                      reads=[f"ps{b}"], writes=[ks])
                ch0 = (nb * 4 + cl) * 128
                kb.dma("pool", S["SG2T"][ch0:ch0 + 128, t0:t0 + TS], s[:], reads=[ks], writes=["SG2T"])

    def phase_E(self, layer):
        kb, I, S = self.kb, self.I, self.S
        e = layer // 2
        with contextlib.ExitStack() as es:
            C = {}
            C["tri"] = self.sb(es, [128, 3, 128], F32, "tri")
            C["maskb"] = self.sb(es, [128, 2, 512], BF16, "maskb")
            C["dtb"] = self.sb(es, [128, 64], F32, "dtb")
            C["arep"] = self.sb(es, [128, 64], F32, "arep")
            C["dsk"] = self.sb(es, [128, 32], F32, "dsk")
            kb.dma("sp", C["tri"][:], I["c_tri"].rearrange("a p c -> p a c"), writes=["tri"])
            kb.dma("pool", C["maskb"][:], I["c_mask"].rearrange("a p c -> p a c"), writes=["maskb"])
            kb.dma("sp", C["dtb"][:], I["ev_dt_bias"][e:e + 1, :].partition_broadcast(128), writes=["dtb"])
            kb.dma("sp", C["arep"][:], I["ev_a_log"][e:e + 1, :].partition_broadcast(128), writes=["arep"])
            kb.dma("sp", C["dsk"][:], I["ev_d_skip"][e:e + 1, :].partition_broadcast(128), writes=["dsk"])
            kb.op("act", lambda e_: e_.activation(out=C["arep"][:], in_=C["arep"][:], func=AF.Exp), reads=["arep"], writes=["arep"])
            kb.op("dve", lambda e_: e_.tensor_scalar(out=C["arep"][:], in0=C["arep"][:], scalar1=-1.0, scalar2=None, op0=ALU.mult),
                  reads=["arep"], writes=["arep"])
            for nm, shp in (("dtr", [128, 64]), ("x1", [128, 64]), ("dtv", [128, 64]), ("lndt", [128, 64]), ("da", [128, 64]),
                            ("nda", [128, 32]), ("cums", [128, 192]), ("Ein", [128, 6, 32]), ("Eout", [128, 6, 32]), ("biasf", [128, 32])):
                C[nm] = self.sb(es, shp, F32, nm)
            self.C = C
            if "A" in E_PARTS:
                self.ssd_pass_A(e)
                kb.barrier()
            if "C" in E_PARTS:
                self.ssd_pass_C(e)
                kb.barrier()
        if "N" in E_PARTS:
            self.na(e)

    def dtq(self, c, fixed=None):
        kb, S, C = self.kb, self.S, self.C
        K_ = ["dtq"]
        kb.dma("sp", C["dtr"][:], S["DT"][c * 128:(c + 1) * 128, :], reads=["DT"], writes=["dtr"])
        kb.op("dve", lambda e: e.tensor_tensor(out=C["x1"][:], in0=C["dtr"][:], in1=C["dtb"][:], op=ALU.add),
              reads=["dtr", "dtb"], writes=K_)
        kb.op("act", lambda e: e.activation(out=C["x1"][:], in_=C["x1"][:], func=AF.Exp), reads=K_, writes=K_)
        kb.op("act", lambda e: e.activation(out=C["dtv"][:], in_=C["x1"][:], func=AF.Ln, bias=1.0, scale=1.0), reads=K_, writes=K_)
        kb.op("act", lambda e: e.activation(out=C["lndt"][:], in_=C["dtv"][:], func=AF.Ln), reads=K_, writes=K_)
        kb.op("dve", lambda e: e.tensor_tensor(out=C["da"][:], in0=C["dtv"][:], in1=C["arep"][:], op=ALU.mult),
              reads=K_ + ["arep"], writes=K_)
        kb.op("dve", lambda e: e.tensor_scalar(out=C["nda"][:], in0=C["da"][:, 32:64], scalar1=-1.0, scalar2=None, op0=ALU.mult),
              reads=K_, writes=K_)
        if fixed is None:
            b = self.next_bank()
            off, pk = 0, f"ps{b}"
        else:
            b, off, pk = fixed
        tri = C["tri"]
        for j in range(3):
            kb.op("pe", lambda e, j=j: e.matmul(self.bank(b, 64, off + j * 64), lhsT=tri[:, j, :], rhs=C["da"][:], start=True, stop=True),
                  reads=K_ + ["tri"], writes=[pk], signal=(j == 2))
        cums = C["cums"]
        kb.op("act", lambda e: e.activation(out=cums[:], in_=self.bank(b, 192, off), func=AF.Copy), reads=[pk], writes=K_)
        cI, cE, tot = cums[:, 0:64], cums[:, 64:128], cums[:, 128:192]
        Ein, Eout = C["Ein"], C["Eout"]
        R = K_
        kb.op("dve", lambda e: e.tensor_copy(out=Ein[:, 2, :], in_=cI[:, 0:32]), reads=R, writes=K_)
        kb.op("dve", lambda e: e.tensor_copy(out=Ein[:, 4:6, :], in_=tot.rearrange("p (a c) -> p a c", a=2)), reads=R, writes=K_)
        kb.op("dve", lambda e: e.tensor_tensor(out=Ein[:, 0, :], in0=tot[:, 0:32], in1=Ein[:, 2, :], op=ALU.subtract), reads=R, writes=K_)
        kb.op("dve", lambda e: e.tensor_tensor(out=Ein[:, 0, :], in0=Ein[:, 0, :], in1=C["lndt"][:, 0:32], op=ALU.add), reads=R, writes=K_)
        kb.op("dve", lambda e: e.tensor_tensor(out=Ein[:, 1, :], in0=cE[:, 32:64], in1=C["lndt"][:, 32:64], op=ALU.add), reads=R, writes=K_)
        kb.op("dve", lambda e: e.tensor_tensor(out=Ein[:, 3, :], in0=tot[:, 32:64], in1=cE[:, 32:64], op=ALU.subtract), reads=R, writes=K_)
        kb.op("dve", lambda e: e.tensor_tensor(out=C["biasf"][:], in0=C["lndt"][:, 0:32], in1=Ein[:, 2, :], op=ALU.subtract), reads=R, writes=K_)
        kb.op("act", lambda e: e.activation(out=Eout[:], in_=Ein[:], func=AF.Exp), reads=K_, writes=K_)

    def ssd_pass_A(self, e):
        kb, I, S, C = self.kb, self.I, self.S, self.C
        with contextlib.ExitStack() as es:
            wT = self.sb(es, [128, 32, 5], F32, "wT")
            diag = self.sb(es, [128, 32, 5, 128], BF16, "diag")
            cbias = self.sb(es, [128, 32], F32, "cbias")
            brow = self.sb(es, [1, 3072], BF16, "brow")
            xin = [self.sb(es, [128, 32, 516], BF16, f"xin{i}") for i in range(2)]
            xc = self.sb(es, [128, 4, 2048], BF16, "xc")
            btok = self.sb(es, [128, 4, 1024], BF16, "btok")
            bT = self.sb(es, [128, 8, 512], BF16, "bT")
            cT = self.sb(es, [128, 8, 512], BF16, "cT")
            Sf = self.sb(es, [128, 2048], F32, "Sf")
            xw = [self.sb(es, [128, 2048], BF16, f"xw{i}") for i in range(2)]
            stgb = self.sb(es, [128, 2048], BF16, "stgb")
            stgf = self.sb(es, [128, 2048], F32, "stgf")
            kb.dma("sp", wT[:], I["ev_conv_wT"][e].rearrange("(cb p) k -> p cb k", p=128), writes=["wT"])
            kb.dma("sp", cbias[:], I["ev_conv_bpm"][e], writes=["cbias"])
            kb.dma("pool", brow[:], I["ev_conv_b"][e:e + 1, 0:3072], writes=["brow"])
            for cb in range(32):
                for k in range(5):
                    kb.op("dve", lambda e_, cb=cb, k=k: e_.tensor_scalar(out=diag[:, cb, k, :], in0=self.ident_f[:],
                                                                       scalar1=wT[:, cb, k:k + 1], scalar2=None, op0=ALU.mult),
                          reads=["wT", "ident_f"], writes=["diag"])
            kb.op("dve", lambda e_: e_.memset(Sf[:], 0.0), writes=["Sf"])
            for blk in range(NT // 4):
                t0 = blk * 512
                xi = xin[blk % 2]
                kx = f"xin{blk % 2}"
                kb.dma("sp", xi[:], S["XBCT"][:, t0:t0 + 516].rearrange("(cb p) t -> p cb t", p=128), reads=["XBCT"], writes=[kx])
                for ci in range(4):
                    for cbg in range(6):
                        b = self.next_bank()
                        for j in range(4):
                            cb = cbg * 4 + j
                            pairs = [(xi[:, cb, ci * 128 + k:ci * 128 + k + 128], diag[:, cb, k, :]) for k in range(5)]
                            pairs.append((self.ones_b[0:1, :], brow[0:1, cb * 128:(cb + 1) * 128]))
                            self.mm(self.bank(b, 128, j * 128), pairs, [kx, "diag", "brow", "ones_b"], f"ps{b}")
                        dst = xc[:, ci, cbg * 512:(cbg + 1) * 512] if cbg < 4 else btok[:, ci, (cbg - 4) * 512:(cbg - 3) * 512]
                        kb.op("act", lambda e_, dst=dst, b=b: e_.activation(out=dst, in_=self.bank(b), func=AF.Silu),
                              reads=[f"ps{b}"], writes=["xc" if cbg < 4 else "btok"])
                for cb in range(16, 32):
                    b = self.next_bank()
                    pairs = [(diag[:, cb, k, :], xi[:, cb, k:k + 512]) for k in range(5)]
                    self.mm(self.bank(b), pairs, [kx, "diag"], f"ps{b}")
                    dst = bT[:, cb - 16, :] if cb < 24 else cT[:, cb - 24, :]
                    kb.op("act", lambda e_, dst=dst, b=b, cb=cb: e_.activation(out=dst, in_=self.bank(b), func=AF.Silu,
                                                                           bias=cbias[:, cb:cb + 1]),
                          reads=[f"ps{b}", "cbias"], writes=["bT" if cb < 24 else "cT"])
                kb.dma("pool", S["XC"][t0:t0 + 512, :].rearrange("(a p) d -> p a d", p=128), xc[:], reads=["xc"], writes=["XC"])
                kb.dma("pool", S["BT"][:, t0:t0 + 512].rearrange("(g p) t -> p g t", p=128), bT[:], reads=["bT"], writes=["BT"])
                kb.dma("pool", S["CT"][:, t0:t0 + 512].rearrange("(g p) t -> p g t", p=128), cT[:], reads=["cT"], writes=["CT"])
                for ci in range(4):
                    c = blk * 4 + ci
                    self.dtq(c)
                    Eout = C["Eout"]
                    for d_ in range(2):
                        xw_ = xw[d_]
                        kxw = f"xw{d_}"
                        kb.op("dve", lambda e_, xw_=xw_, ci=ci, d_=d_: e_.tensor_tensor(
                            out=xw_[:].rearrange("p (h c) -> p h c", h=32), in0=xc[:, ci, :].rearrange("p (h c) -> p h c", h=32),
                            in1=_bc(Eout[:, d_, :], 64), op=ALU.mult), reads=["xc", "dtq"], writes=[kxw])
                        banks = [self.next_bank() for _ in range(4)]
                        for g in range(8):
                            bk = banks[g // 2]
                            self.mm(self.bank(bk, 256, (g % 2) * 256), [(btok[:, ci, g * 128:(g + 1) * 128], xw_[:, g * 256:(g + 1) * 256])],
                                    ["btok", kxw], f"ps{bk}")
                        if d_ == 0:
                            kb.op("act", lambda e_: e_.activation(out=stgb[:], in_=Sf[:], func=AF.Copy), reads=["Sf"], writes=["stgb"])
                            kb.dma("pool", S["SPF"][c], stgb[:], reads=["stgb"], writes=["SPF"])
                            kb.op("dve", lambda e_: e_.tensor_tensor(out=Sf[:].rearrange("p (h c) -> p h c", h=32),
                                                                    in0=Sf[:].rearrange("p (h c) -> p h c", h=32),
                                                                    in1=_bc(Eout[:, 4, :], 64), op=ALU.mult),
                                  reads=["Sf", "dtq", "stgb"], writes=["Sf"])
                            for j, bk in enumerate(banks):
                                kb.op("dve", lambda e_, j=j, bk=bk: e_.tensor_tensor(out=Sf[:, j * 512:(j + 1) * 512], in0=self.bank(bk),
                                                                                 in1=Sf[:, j * 512:(j + 1) * 512], op=ALU.add),
                                      reads=[f"ps{bk}", "Sf"], writes=["Sf"])
                        else:
                            for j, bk in enumerate(banks):
                                kb.op("act", lambda e_, j=j, bk=bk: e_.activation(out=stgf[:, j * 512:(j + 1) * 512], in_=self.bank(bk), func=AF.Copy),
                                      reads=[f"ps{bk}"], writes=["stgf"])
                            kb.dma("pool", S["LSB"][c], stgf[:], reads=["stgf"], writes=["LSB"])

    def ssd_pass_C(self, e):
        kb, I, S, C = self.kb, self.I, self.S, self.C
        with contextlib.ExitStack() as es:
            xc = self.sb(es, [128, 2048], BF16, "xcC")
            bT = self.sb(es, [128, 8, 128], BF16, "bTC")
            cT = self.sb(es, [128, 8, 128], BF16, "cTC")
            spf = self.sb(es, [128, 2048], BF16, "spf")
            lsb = self.sb(es, [128, 2048], F32, "lsb")
            Sb = self.sb(es, [128, 2048], F32, "Sb")
            Sbb = self.sb(es, [128, 2048], BF16, "Sbb")
            Xf = self.sb(es, [128, 32, 128], F32, "Xf")
            Xb = self.sb(es, [128, 32, 128], F32, "Xb")
            G = [self.sb(es, [128, 4, 128], F32, f"G{i}") for i in range(4)]
            Wt = [self.sb(es, [128, 4, 128], BF16, f"Wt{i}") for i in range(2)]
            ys = self.sb(es, [128, 2048], F32, "ysC")
            t1 = self.sb(es, [128, 512], F32, "t1")
            t2 = self.sb(es, [128, 512], F32, "t2")
            xd = self.sb(es, [128, 2048], F32, "xd")
            ones_f = C["tri"][:, 2, :]
            kb.op("dve", lambda e_: e_.memset(Sb[:], 0.0), writes=["Sb"])
            kb.op("dve", lambda e_: e_.memset(Sbb[:], 0.0), writes=["Sbb"])
            for c in range(NT - 1, -1, -1):
                r0 = c * 128
                kb.dma("sp", xc[:], S["XC"][r0:r0 + 128, :], reads=["XC"], writes=["xcC"])
                kb.dma("sp", bT[:], S["BT"][:, r0:r0 + 128].rearrange("(g p) t -> p g t", p=128), reads=["BT"], writes=["bTC"])
                kb.dma("sp", cT[:], S["CT"][:, r0:r0 + 128].rearrange("(g p) t -> p g t", p=128), reads=["CT"], writes=["cTC"])
                kb.dma("sp", spf[:], S["SPF"][c], reads=["SPF"], writes=["spf"])
                kb.dma("sp", lsb[:], S["LSB"][c], reads=["LSB"], writes=["lsb"])
                self.dtq(c)
                Ein, Eout = C["Ein"], C["Eout"]
                kb.op("dve", lambda e_: e_.tensor_tensor(out=Xf[:], in0=_bc(C["da"][:, 0:32], 128),
                                                        in1=C["tri"][:, 0, :].unsqueeze(1).broadcast_to([128, 32, 128]), op=ALU.mult),
                      reads=["dtq", "tri"], writes=["Xf"])
                kb.op("dve", lambda e_: e_.tensor_tensor(out=Xb[:], in0=_bc(C["nda"][:, 0:32], 128),
                                                        in1=C["tri"][:, 1, :].unsqueeze(1).broadcast_to([128, 32, 128]), op=ALU.mult),
                      reads=["dtq", "tri"], writes=["Xb"])
                bY, bF, bB, bS = 0, 1, 2, 3

                def st_a(g):
                    gi = g % 2
                    self.mm(self.bank(bS, 128, gi * 128), [(bT[:, g, :], cT[:, g, :])], ["bTC", "cTC"], f"ps{bS}")
                    for d_ in range(2):
                        bR = 4 + (self.psrr % 4)
                        self.psrr += 1
                        X_ = Xf if d_ == 0 else Xb
                        pairs = [(ones_f, X_[:, g * 4:(g + 1) * 4, :].rearrange("p a c -> p (a c)")),
                                 (self.ident_b[:], C["maskb"][:, d_, :])]
                        self.mm(self.bank(bR), pairs, ["Xf" if d_ == 0 else "Xb", "tri", "ident_b", "maskb"], f"ps{bR}")
                        Gd = G[gi * 2 + d_]
                        kG = f"G{gi * 2 + d_}"
                        for h in range(4):
                            bias = C["biasf"][:, g * 4 + h:g * 4 + h + 1] if d_ == 0 else Ein[:, 1, g * 4 + h:g * 4 + h + 1]
                            kb.op("act", lambda e_, Gd=Gd, h=h, bR=bR, bias=bias: e_.activation(
                                out=Gd[:, h, :], in_=self.bank(bR, 128, h * 128), func=AF.Exp, bias=bias, scale=1.0),
                                reads=[f"ps{bR}", "dtq"], writes=[kG])
                    Gf, Gb = G[gi * 2], G[gi * 2 + 1]
                    kb.op("pool", lambda e_, Gf=Gf, Gb=Gb: e_.tensor_tensor(out=Gf[:], in0=Gf[:], in1=Gb[:], op=ALU.add),
                          reads=[f"G{gi * 2}", f"G{gi * 2 + 1}"], writes=[f"G{gi * 2}"])
                    W_ = Wt[gi]
                    kb.op("dve", lambda e_, W_=W_, Gf=Gf, gi=gi: e_.tensor_tensor(
                        out=W_[:], in0=Gf[:], in1=self.bank(bS, 128, gi * 128).unsqueeze(1).broadcast_to([128, 4, 128]), op=ALU.mult),
                        reads=[f"G{gi * 2}", f"ps{bS}"], writes=[f"Wt{gi}"])

                def st_b(g):
                    gi = g % 2
                    W_ = Wt[gi]
                    for h in range(4):
                        hh = g * 4 + h
                        self.mm(self.bank(bY, 64, (gi * 4 + h) * 64), [(W_[:, h, :], xc[:, hh * 64:(hh + 1) * 64])],
                                [f"Wt{gi}", "xcC"], f"ps{bY}")
                    self.mm(self.bank(bF, 256, gi * 256), [(cT[:, g, :], spf[:, g * 256:(g + 1) * 256])], ["cTC", "spf"], f"ps{bF}")
                    self.mm(self.bank(bB, 256, gi * 256), [(cT[:, g, :], Sbb[:, g * 256:(g + 1) * 256])], ["cTC", "Sbb"], f"ps{bB}")

                def epi(q):
                    hs = slice(q * 8, q * 8 + 8)
                    kb.op("dve", lambda e_, hs=hs: e_.tensor_tensor(out=t1[:].rearrange("p (h c) -> p h c", h=8),
                                                                  in0=self.bank(bF).rearrange("p (h c) -> p h c", h=8),
                                                                  in1=_bc(Eout[:, 2, hs], 64), op=ALU.mult),
                          reads=[f"ps{bF}", "dtq"], writes=["t1"])
                    kb.op("dve", lambda e_, hs=hs: e_.tensor_tensor(out=t2[:].rearrange("p (h c) -> p h c", h=8),
                                                                  in0=self.bank(bB).rearrange("p (h c) -> p h c", h=8),
                                                                  in1=_bc(Eout[:, 3, hs], 64), op=ALU.mult),
                          reads=[f"ps{bB}", "dtq"], writes=["t2"])
                    kb.op("pool", lambda e_: e_.tensor_tensor(out=t1[:], in0=t1[:], in1=t2[:], op=ALU.add), reads=["t1", "t2"], writes=["t1"])
                    kb.op("dve", lambda e_, q=q: e_.tensor_tensor(out=ys[:, q * 512:(q + 1) * 512], in0=self.bank(bY), in1=t1[:], op=ALU.add),
                          reads=[f"ps{bY}", "t1"], writes=["ysC"])

                st_a(0)
                for g in range(8):
                    if g + 1 < 8:
                        st_a(g + 1)
                    st_b(g)
                    if g % 2 == 1:
                        epi(g // 2)
                kb.op("pool", lambda e_: e_.tensor_tensor(out=xd[:].rearrange("p (h c) -> p h c", h=32),
                                                         in0=xc[:].rearrange("p (h c) -> p h c", h=32),
                                                         in1=_bc(C["dsk"][:, :], 64), op=ALU.mult), reads=["xcC", "dsk"], writes=["xd"])
                kb.op("dve", lambda e_: e_.tensor_tensor(out=ys[:], in0=ys[:], in1=xd[:], op=ALU.add), reads=["ysC", "xd"], writes=["ysC"])
                kb.dma("pool", S["YS"][r0:r0 + 128, :], ys[:], reads=["ysC"], writes=["YS"])
                kb.op("dve", lambda e_: e_.tensor_tensor(out=Sb[:].rearrange("p (h c) -> p h c", h=32),
                                                        in0=Sb[:].rearrange("p (h c) -> p h c", h=32),
                                                        in1=_bc(Eout[:, 5, :], 64), op=ALU.mult), reads=["Sb", "dtq"], writes=["Sb"])
                kb.op("dve", lambda e_: e_.tensor_tensor(out=Sb[:], in0=Sb[:], in1=lsb[:], op=ALU.add), reads=["Sb", "lsb"], writes=["Sb"])
                kb.op("act", lambda e_: e_.activation(out=Sbb[:], in_=Sb[:], func=AF.Copy), reads=["Sb"], writes=["Sbb"])

    def na(self, e):
        kb, I, S = self.kb, self.I, self.S
        rows = L // 64
        scale = 128 ** -0.5
        RB = 16
        with contextlib.ExitStack() as es:
            KTh = self.sb(es, [128, L], BF16, "KTh")
            QTh = self.sb(es, [128, L], BF16, "QTh")
            V0 = self.sb(es, [128, NT, 132], BF16, "V0")
            V1 = self.sb(es, [128, NT, 132], BF16, "V1")
            bt = self.sb(es, [128, 8, 256], F32, "bt")
            nam = self.sb(es, [128, 256], F32, "nam")
            NBUF = 4
            s2 = [self.sb(es, [128, 256], F32, f"s2_{i}") for i in range(NBUF)]
            pT = [self.sb(es, [128, 256], BF16, f"pT_{i}") for i in range(NBUF)]
            rinv = [self.sb(es, [64, 1], F32, f"rinv{i}") for i in range(NBUF)]
            obuf = [self.sb(es, [64, RB, 128], F32, f"obuf{i}") for i in range(2)]
            kb.dma("sp", nam[:], I["c_namask"], writes=["nam"])
            kb.op("dve", lambda e_: e_.memset(V0[:, :, 128:129], 1.0), writes=["V0"])
            kb.op("dve", lambda e_: e_.memset(V1[:, :, 128:129], 1.0), writes=["V1"])
            for h in range(16):
                kb.dma("sp", KTh[:], S["KT"][h * 128:(h + 1) * 128, :], reads=["KT"], writes=["KTh"])
                kb.dma("sp", QTh[:], S["QT"][h * 128:(h + 1) * 128, :], reads=["QT"], writes=["QTh"])
                kb.dma("sp", V0[:, :, 0:128], S["V"][:, h * 128:(h + 1) * 128].rearrange("(j p) d -> p j d", p=128), reads=["V"], writes=["V0"])
                kb.dma("sp", V1[:, 0:NT - 1, 0:128], S["V"][64:L - 64, h * 128:(h + 1) * 128].rearrange("(j p) d -> p j d", p=128),
                       reads=["V"], writes=["V1"])
                kb.dma("sp", bt[:], I["ev_rpbt"][e, h].rearrange("a p c -> p a c"), writes=["bt"])
                kb.op("dve", lambda e_: e_.tensor_tensor(out=bt[:], in0=bt[:], in1=nam[:].unsqueeze(1).broadcast_to([128, 8, 256]), op=ALU.add),
                      reads=["bt", "nam"], writes=["bt"])
                def stage_a(r):
                    rs = min(max(r - 4, 0), rows - 8)
                    pat = r - rs
                    ks = rs * 64
                    i2 = r % NBUF
                    b = self.next_bank(0, 4)
                    for c_ in range(4):
                        self.mm(self.bank(b, 64, c_ * 64), [(KTh[:, ks + c_ * 128:ks + (c_ + 1) * 128], QTh[:, r * 64:(r + 1) * 64])],
                                ["KTh", "QTh"], f"ps{b}")
                    kb.op("dve", lambda e_, i2=i2, b=b, pat=pat: e_.scalar_tensor_tensor(
                        out=s2[i2][:], in0=self.bank(b, 256), scalar=scale, in1=bt[:, pat, :], op0=ALU.mult, op1=ALU.add),
                        reads=[f"ps{b}", "bt"], writes=[f"s2_{i2}"])
                    kb.op("act", lambda e_, i2=i2: e_.activation(out=pT[i2][:], in_=s2[i2][:], func=AF.Exp),
                          reads=[f"s2_{i2}"], writes=[f"pT_{i2}"])

                def stage_b(r):
                    rs = min(max(r - 4, 0), rows - 8)
                    ks = rs * 64
                    i2 = r % NBUF
                    b2 = self.next_bank(4, 8)
                    pairs = []
                    for c_ in range(4):
                        tk = ks + c_ * 128
                        vsel = V0[:, tk // 128, 0:129] if tk % 128 == 0 else V1[:, (tk - 64) // 128, 0:129]
                        pairs.append((pT[i2][:, c_ * 64:(c_ + 1) * 64], vsel))
                    self.mm(self.psum[0:64, b2 * 512:b2 * 512 + 129], pairs, [f"pT_{i2}", "V0", "V1"], f"ps{b2}")
                    kb.op("dve", lambda e_, i2=i2, b2=b2: e_.reciprocal(out=rinv[i2][:], in_=self.psum[0:64, b2 * 512 + 128:b2 * 512 + 129]),
                          reads=[f"ps{b2}"], writes=[f"rinv{i2}"])
                    ob = obuf[(r // RB) % 2]
                    kob = f"obuf{(r // RB) % 2}"
                    kb.op("act", lambda e_, ob=ob, r=r, i2=i2, b2=b2: e_.activation(
                        out=ob[:, r % RB, :], in_=self.psum[0:64, b2 * 512:b2 * 512 + 128], func=AF.Copy, scale=rinv[i2][:]),
                        reads=[f"ps{b2}", f"rinv{i2}"], writes=[kob])
                    if r % RB == RB - 1:
                        rr0 = (r - RB + 1) * 64
                        kb.dma("pool", S["YN"][rr0:rr0 + RB * 64, h * 128:(h + 1) * 128].rearrange("(r q) d -> q r d", q=64), ob[:],
                               reads=[kob], writes=["YN"])

                SK = 2
                for r in range(rows + SK):
                    if r < rows:
                        stage_a(r)
                    if r >= SK:
                        stage_b(r - SK)

    def phase_O(self, layer):
        kb, I, S = self.kb, self.I, self.S
        e = layer // 2
        NB = L // 512
        with contextlib.ExitStack() as es:
            wT = self.sb(es, [128, 32, 31], F32, "wT31")
            pm = self.sb(es, [128, 3, 32], F32, "pm31")
            dgs = [self.sb(es, [128, 31, 128], BF16, f"dg{i}") for i in range(2)]
            vins = [self.sb(es, [128, L + 30], BF16, f"vin{i}") for i in range(2)]
            stg = [self.sb(es, [128, 512], F32, f"stgO{i}") for i in range(3)]
            kb.dma("sp", wT[:], I["od_dw_wT"][e].rearrange("(cb p) k -> p cb k", p=128), writes=["wT31"])
            kb.dma("sp", pm[:], I["od_pm"][e].rearrange("a p c -> p a c"), writes=["pm31"])
            si = 0
            for cb in range(32):
                dg, kd = dgs[cb % 2], f"dg{cb % 2}"
                vi, kv = vins[cb % 2], f"vin{cb % 2}"
                for k in range(31):
                    kb.op("dve", lambda e_, k=k, dg=dg, cb=cb: e_.tensor_scalar(
                        out=dg[:, k, :], in0=self.ident_f[:], scalar1=wT[:, cb, k:k + 1], scalar2=None, op0=ALU.mult),
                        reads=["wT31", "ident_f"], writes=[kd])
                kb.dma("sp", vi[:], S["VT"][cb * 128:(cb + 1) * 128, :], reads=["VT"], writes=[kv])
                for tb in range(NB):
                    b = self.next_bank()
                    self.mm(self.bank(b), [(dg[:, k, :], vi[:, tb * 512 + k:tb * 512 + k + 512]) for k in range(31)],
                            [kd, kv], f"ps{b}")
                    s_, ks_ = stg[si % 3], f"stgO{si % 3}"
                    si += 1
                    kb.op("act", lambda e_, s_=s_, b=b, cb=cb: e_.activation(out=s_[:], in_=self.bank(b), func=AF.Identity,
                                                                         bias=pm[:, 0, cb:cb + 1], scale=1.0),
                          reads=[f"ps{b}", "pm31"], writes=[ks_])
                    kb.dma("pool", S["VCT"][cb * 128:(cb + 1) * 128, tb * 512:(tb + 1) * 512], s_[:], reads=[ks_], writes=["VCT"])
        kb.barrier()
        with contextlib.ExitStack() as es:
            pm = self.sb(es, [128, 3, 32], F32, "pm31b")
            ones_f = self.sb(es, [128, 128], F32, "ones_f")
            vblk = self.sb(es, [128, 32, 512], F32, "vblk")
            sq = [self.sb(es, [128, 512], F32, f"sq{i}") for i in range(2)]
            sgt = [self.sb(es, [128, 512], F32, f"sgt{i}") for i in range(3)]
            tt_ = [self.sb(es, [128, 512], F32, f"tO{i}") for i in range(2)]
            yb = [self.sb(es, [128, 512], BF16, f"ybO{i}") for i in range(3)]
            mean = self.sb(es, [128, 512], F32, "meanO")
            rstd = self.sb(es, [128, 512], F32, "rstdO")
            tmp = self.sb(es, [128, 512], F32, "tmpO")
            kb.dma("sp", pm[:], I["od_pm"][e].rearrange("a p c -> p a c"), writes=["pm31b"])
            kb.dma("sp", ones_f[:], I["c_tri"][2], writes=["ones_f"])
            for tb in range(NB):
                c0 = tb * 512
                kb.dma("sp", vblk[:], S["VCT"][:, c0:c0 + 512].rearrange("(cb p) t -> p cb t", p=128), reads=["VCT"], writes=["vblk"])
                bA, bB = 0, 1
                for cb in range(32):
                    q_, kq = sq[cb % 2], f"sq{cb % 2}"
                    kb.op("act", lambda e_, q_=q_, cb=cb: e_.activation(out=q_[:], in_=vblk[:, cb, :], func=AF.Square), reads=["vblk"], writes=[kq])
                    kb.op("pe", lambda e_, cb=cb: e_.matmul(self.bank(bA), lhsT=ones_f[:], rhs=vblk[:, cb, :], start=(cb == 0), stop=(cb == 31)),
                          reads=["vblk", "ones_f"], writes=[f"ps{bA}"], signal=(cb == 31))
                    kb.op("pe", lambda e_, cb=cb, q_=q_: e_.matmul(self.bank(bB), lhsT=ones_f[:], rhs=q_[:], start=(cb == 0), stop=(cb == 31)),
                          reads=[kq, "ones_f"], writes=[f"ps{bB}"], signal=True)
                kb.op("dve", lambda e_: e_.tensor_scalar(out=mean[:], in0=self.bank(bA), scalar1=1.0 / 4096, scalar2=None, op0=ALU.mult),
                      reads=[f"ps{bA}"], writes=["meanO"])
                kb.op("dve", lambda e_: e_.tensor_tensor(out=tmp[:], in0=mean[:], in1=mean[:], op=ALU.mult), reads=["meanO"], writes=["tmpO"])
                kb.op("dve", lambda e_: e_.scalar_tensor_tensor(out=tmp[:], in0=self.bank(bB), scalar=1.0 / 4096, in1=tmp[:],
                                                                op0=ALU.mult, op1=ALU.subtract), reads=[f"ps{bB}", "tmpO"], writes=["tmpO"])
                kb.op("act", lambda e_: e_.activation(out=tmp[:], in_=tmp[:], func=AF.Sqrt, bias=EPS, scale=1.0), reads=["tmpO"], writes=["tmpO"])
                kb.op("dve", lambda e_: e_.reciprocal(out=rstd[:], in_=tmp[:]), reads=["tmpO"], writes=["rstdO"])
                for cb in range(32):
                    g_, kg = sgt[cb % 3], f"sgt{cb % 3}"
                    t_, kt = tt_[cb % 2], f"tO{cb % 2}"
                    y_, ky = yb[cb % 3], f"ybO{cb % 3}"
                    kb.dma("sp", g_[:], S["SG2T"][cb * 128:(cb + 1) * 128, c0:c0 + 512], reads=["SG2T"], writes=[kg])
                    kb.op("dve", lambda e_, t_=t_, cb=cb: e_.tensor_tensor(out=t_[:], in0=vblk[:, cb, :], in1=mean[:], op=ALU.subtract),
                          reads=["vblk", "meanO"], writes=[kt])
                    kb.op("pool", lambda e_, t_=t_: e_.tensor_tensor(out=t_[:], in0=t_[:], in1=rstd[:], op=ALU.mult), reads=[kt, "rstdO"], writes=[kt])
                    kb.op("act", lambda e_, t_=t_, cb=cb: e_.activation(out=t_[:], in_=t_[:], func=AF.Silu, bias=pm[:, 2, cb:cb + 1],
                                                                     scale=pm[:, 1, cb:cb + 1]), reads=[kt, "pm31b"], writes=[kt])
                    kb.op("dve", lambda e_, t_=t_, g_=g_, y_=y_: e_.tensor_tensor(out=y_[:], in0=t_[:], in1=g_[:], op=ALU.mult),
                          reads=[kt, kg], writes=[ky])
                    kb.dma("pool", S["YOT"][cb * 128:(cb + 1) * 128, c0:c0 + 512], y_[:], reads=[ky], writes=["YOT"])


def _consts():
    c = {}
    c["c_ident"] = np.eye(128, dtype=np.float32)
    t = np.arange(128)
    tri = np.zeros((3, 128, 128), np.float32)
    tri[0] = (t[:, None] <= t[None, :])
    tri[1] = (t[:, None] < t[None, :])
    tri[2] = 1.0
    c["c_tri"] = tri
    m = np.zeros((2, 128, 512), np.float32)
    mf = np.where(t[:, None] <= t[None, :], 0.0, NEG)
    mb = np.where(t[:, None] >= t[None, :], 0.0, NEG)
    m[0] = np.tile(mf, (1, 4))
    m[1] = np.tile(mb, (1, 4))
    c["c_mask"] = m.astype(np.float32)
    cols = np.arange(64)
    cs = np.clip(cols - 8, 0, 48)
    valid = (cols[None, :] >= cs[:, None]) & (cols[None, :] < cs[:, None] + 16)
    mk = np.where(valid.T, 0.0, NEG).astype(np.float32)
    mk2 = np.concatenate([mk, mk], 0)
    c["c_namask"] = np.tile(mk2, (1, 4)).astype(np.float32)
    return c


def _rpb_table(rpb):
    cols = np.arange(64)
    coff = np.clip(cols[None, :] - cols[:, None], -15, 15) + 15
    out = np.zeros((2, 16, 8, 128, 256), np.float32)
    for pat in range(8):
        delta = -pat
        for c_ in range(4):
            for il in range(2):
                i = 2 * c_ + il
                roff = delta + i + 7
                g = rpb[:, :, roff, :][:, :, coff]
                out[:, :, pat, il * 64:(il + 1) * 64, c_ * 64:(c_ + 1) * 64] = np.transpose(g, (0, 1, 3, 2))
    return out


_NC_CACHE = {}


def kernel(x, p, ev_norm_w, ev_w_in, ev_conv_w, ev_conv_b, ev_dt_bias_f, ev_dt_bias_b,
           ev_a_log_f, ev_a_log_b, ev_d_skip, ev_gnorm_w, ev_rpb, ev_w_out,
           od_norm_w, od_w_in, od_dw_w, od_dw_b, od_ln_w, od_ln_b, od_w_out,
           ple_norm_w, ple_w_gate, ple_w_proj, final_norm_w):
    f = lambda a: np.ascontiguousarray(np.asarray(a, dtype=np.float32))
    if "nc" not in _NC_CACHE:
        _NC_CACHE["nc"] = Prog().build()
    nc = _NC_CACHE["nc"]
    shared = {
        "ev_norm_w": f(ev_norm_w), "ev_w_in": f(ev_w_in), "ev_conv_wT": f(np.transpose(f(ev_conv_w), (0, 2, 1))),
        "ev_conv_b": f(ev_conv_b), "ev_conv_bpm": f(np.transpose(f(ev_conv_b).reshape(2, 32, 128), (0, 2, 1))),
        "ev_dt_bias": f(np.concatenate([ev_dt_bias_f, ev_dt_bias_b], 1)),
        "ev_a_log": f(np.concatenate([ev_a_log_f, ev_a_log_b], 1)),
        "ev_d_skip": f(ev_d_skip), "ev_gnorm_w": f(ev_gnorm_w), "ev_rpbt": _rpb_table(f(ev_rpb)),
        "ev_w_out": f(ev_w_out), "od_norm_w": f(od_norm_w), "od_w_in": f(od_w_in), "od_dw_wT": f(np.transpose(f(od_dw_w), (0, 2, 1))),
        "od_pm": f(np.transpose(np.stack([f(od_dw_b), f(od_ln_w), f(od_ln_b)], 1).reshape(2, 3, 32, 128), (0, 1, 3, 2))),
        "od_w_out": f(od_w_out),
        "ple_norm_w": f(ple_norm_w), "ple_w_gate": f(ple_w_gate), "ple_w_proj": f(ple_w_proj),
        "final_norm_w": f(final_norm_w).reshape(1, D),
    }
    shared.update(_consts())
    x = f(x)
    p = f(p)
    in_maps = []
    for c in range(NCORES):
        b = c % 2
        m = dict(shared)
        m["x"] = np.ascontiguousarray(x[b, :L])
        m["p"] = np.ascontiguousarray(p[:, b, :L])
        in_maps.append(m)
    res = run_bass_kernel_spmd(nc, in_maps, core_ids=list(range(NCORES)))
    kernel.last = res
    return np.stack([res.results[0]["out"], res.results[1]["out"]], 0)
```

```python
import contextlib
import numpy as np
import ml_dtypes
import concourse.bass as bass
import concourse.mybir as mybir
from concourse.bass_utils import run_bass_kernel_spmd

F32 = mybir.dt.float32
BF16 = mybir.dt.bfloat16
AF = mybir.ActivationFunctionType
ALU = mybir.AluOpType
AX = mybir.AxisListType

L = 8192
D = 2048
TS = 512
TPS = TS // 128
NST = L // TS
NT = L // 128
EVEN_IN = 14400
ODD_IN = 12288
NEG = -30000.0
EPS = 1e-6
DEPTH = 4
EV_COLS = ([c for c in range(0, 2048, 512)] + [2048 + c for c in range(0, 4096, 512)] +
           [6208 + c for c in range(0, 8192, 512)])
NCORES = 2

DEBUG = {}
STOP_AFTER = None
NST_RUN = NST
PHASES = None
E_PARTS = "ACN"


def _set_L(n):
    global L, NST, NT, NST_RUN
    L = n
    NST = L // TS
    NT = L // 128
    NST_RUN = NST


class _Eng:
    def __init__(self, kb, name, h):
        self.kb = kb
        self.name = name
        self.h = h
        self.cnt = 0
        self.sem_ids = []
        self.waited = {}
        self.dsl = []
        self.drr = 0

    def event_for(self, cnt):
        cap = 30000
        idx = (cnt - 1) // cap
        while len(self.sem_ids) <= idx:
            self.sem_ids.append(self.kb.new_sem(f"{self.name}_c{len(self.sem_ids)}"))
        return (self.sem_ids[idx], (cnt - 1) % cap + 1)


class KB:
    NDSEM = 10

    def __init__(self, nc, es):
        self.nc = nc
        self.es = es
        self.sems = []
        self.state = {}
        self.eng = {}
        for name, h in (("pe", nc.tensor), ("act", nc.scalar), ("dve", nc.vector),
                        ("pool", nc.gpsimd), ("sp", nc.sync)):
            self.eng[name] = _Eng(self, name, h)
        self.ninst = 0

    def new_sem(self, name):
        s = self.es.enter_context(self.nc.semaphore(name))
        self.sems.append(s)
        return len(self.sems) - 1

    def _wait(self, e, ev):
        sid, val = ev
        if e.name == "pe" and sid in e.sem_ids:
            return
        if e.waited.get(sid, 0) < val:
            e.h.wait_ge(self.sems[sid], val)
            e.waited[sid] = val

    def _gather(self, reads, writes):
        evs = []
        for k in reads:
            st = self.state.get(k)
            if st is not None and st[0] is not None:
                evs.append(st[0])
        for k in writes:
            st = self.state.get(k)
            if st is not None:
                if st[0] is not None:
                    evs.append(st[0])
                evs.extend(st[1].items())
        return evs

    def _record(self, ev, reads, writes):
        for k in reads:
            st = self.state.setdefault(k, [None, {}])
            if st[1].get(ev[0], 0) < ev[1]:
                st[1][ev[0]] = ev[1]
        for k in writes:
            self.state[k] = [ev, {}]

    def op(self, engname, fn, reads=(), writes=(), signal=True):
        e = self.eng[engname]
        for ev in self._gather(reads, writes):
            self._wait(e, ev)
        inst = fn(e.h)
        self.ninst += 1
        if signal:
            e.cnt += 1
            ev = e.event_for(e.cnt)
            inst.then_inc(self.sems[ev[0]], 1)
        else:
            ev = e.event_for(e.cnt + 1)
        self._record(ev, reads, writes)
        return inst

    def dma(self, q, out, in_, reads=(), writes=()):
        e = self.eng[q]
        for ev in self._gather(reads, writes):
            self._wait(e, ev)
        if not e.dsl:
            e.dsl = [[self.new_sem(f"{q}_d{i}"), 0] for i in range(self.NDSEM)]
        slot = e.dsl[e.drr]
        e.drr = (e.drr + 1) % len(e.dsl)
        if slot[1] > 0:
            self._wait(e, (slot[0], slot[1] * 16))
        inst = e.h.dma_start(out=out, in_=in_)
        inst.then_inc(self.sems[slot[0]], 16)
        slot[1] += 1
        self.ninst += 1
        ev = (slot[0], slot[1] * 16)
        self._record(ev, reads, writes)
        return ev

    def barrier(self):
        evs = []
        for e in self.eng.values():
            if e.cnt > 0:
                evs.append(e.event_for(e.cnt))
            for slot in e.dsl:
                if slot[1] > 0:
                    evs.append((slot[0], slot[1] * 16))
        for e in self.eng.values():
            for ev in evs:
                sid, val = ev
                if e.waited.get(sid, 0) < val:
                    e.h.wait_ge(self.sems[sid], val)
                    e.waited[sid] = val
        self.state = {}


def _bc(ap2d, n):
    return ap2d.unsqueeze(2).broadcast_to([ap2d.shape[0], ap2d.shape[1], n])


class Prog:
    def __init__(self):
        self.nc = bass.Bass("TRN2", target_bir_lowering=False)
        self.nsb = 0

    def sb(self, es, shape, dt, name=None):
        self.nsb += 1
        return es.enter_context(self.nc.sbuf_tensor(f"{name or 'sb'}_{self.nsb}", list(shape), dt))

    def din(self, name, shape, dt=F32):
        return self.nc.dram_tensor(name, list(shape), dt, kind="ExternalInput").ap()

    def dscr(self, name, shape, dt):
        return self.nc.dram_tensor(name, list(shape), dt).ap()

    def build(self):
        nc = self.nc
        I = {}
        I["x"] = self.din("x", [L, D])
        I["p"] = self.din("p", [DEPTH, L, 256])
        I["ev_norm_w"] = self.din("ev_norm_w", [2, D])
        I["ev_w_in"] = self.din("ev_w_in", [2, D, EVEN_IN])
        I["ev_conv_wT"] = self.din("ev_conv_wT", [2, 4096, 5])
        I["ev_conv_b"] = self.din("ev_conv_b", [2, 4096])
        I["ev_conv_bpm"] = self.din("ev_conv_bpm", [2, 128, 32])
        I["ev_dt_bias"] = self.din("ev_dt_bias", [2, 64])
        I["ev_a_log"] = self.din("ev_a_log", [2, 64])
        I["ev_d_skip"] = self.din("ev_d_skip", [2, 32])
        I["ev_gnorm_w"] = self.din("ev_gnorm_w", [2, D])
        I["ev_rpbt"] = self.din("ev_rpbt", [2, 16, 8, 128, 256])
        I["ev_w_out"] = self.din("ev_w_out", [2, 4096, D])
        I["od_norm_w"] = self.din("od_norm_w", [2, D])
        I["od_w_in"] = self.din("od_w_in", [2, D, ODD_IN])
        I["od_dw_wT"] = self.din("od_dw_wT", [2, 4096, 31])
        I["od_pm"] = self.din("od_pm", [2, 3, 128, 32])
        I["od_w_out"] = self.din("od_w_out", [2, 4096, D])
        I["ple_norm_w"] = self.din("ple_norm_w", [DEPTH, D])
        I["ple_w_gate"] = self.din("ple_w_gate", [DEPTH, D, D])
        I["ple_w_proj"] = self.din("ple_w_proj", [DEPTH, 256, D])
        I["final_norm_w"] = self.din("final_norm_w", [1, D])
        I["c_ident"] = self.din("c_ident", [128, 128])
        I["c_tri"] = self.din("c_tri", [3, 128, 128])
        I["c_mask"] = self.din("c_mask", [2, 128, 512])
        I["c_namask"] = self.din("c_namask", [128, 256])
        self.I = I
        self.out = nc.dram_tensor("out", [L, D], F32, kind="ExternalOutput").ap()

        S = {}
        S["H"] = self.dscr("H", [L, D], F32)
        S["wk_ev_in"] = self.dscr("wk_ev_in", [2, 29, 128, 16, 512], BF16)
        S["wk_od_in"] = self.dscr("wk_od_in", [2, 24, 128, 16, 512], BF16)
        S["wk_ev_out"] = self.dscr("wk_ev_out", [2, 8, 128, 16, 512], BF16)
        S["wk_od_out"] = self.dscr("wk_od_out", [2, 8, 128, 16, 512], BF16)
        S["wk_gate"] = self.dscr("wk_gate", [DEPTH, 4, 128, 16, 512], BF16)
        S["wk_proj"] = self.dscr("wk_proj", [DEPTH, 4, 128, 2, 512], BF16)
        S["ATD"] = self.dscr("ATD", [L // TS, 128, 16, TS], BF16)
        S["SZ"] = self.dscr("SZ", [L, D], F32)
        S["XBCT"] = self.dscr("XBCT", [4096, L + 4], BF16)
        S["DT"] = self.dscr("DT", [L, 64], F32)
        S["QT"] = self.dscr("QT", [D, L], BF16)
        S["KT"] = self.dscr("KT", [D, L], BF16)
        S["V"] = self.dscr("V", [L, D], BF16)
        S["SG"] = self.dscr("SG", [L, D], F32)
        S["YS"] = self.dscr("YS", [L, D], F32)
        S["YN"] = self.dscr("YN", [L, D], F32)
        S["XC"] = self.dscr("XC", [L, D], BF16)
        S["BTOK"] = self.dscr("BTOK", [L, 1024], BF16)
        S["BT"] = self.dscr("BT", [1024, L], BF16)
        S["CT"] = self.dscr("CT", [1024, L], BF16)
        S["SPF"] = self.dscr("SPF", [NT, 128, D], BF16)
        S["LSB"] = self.dscr("LSB", [NT, 128, D], F32)
        S["VT"] = self.dscr("VT", [4096, L + 30], BF16)
        S["VCT"] = self.dscr("VCT", [4096, L], F32)
        S["SG2T"] = self.dscr("SG2T", [4096, L], F32)
        S["YOT"] = self.dscr("YOT", [4096, L], BF16)
        self.S = S

        self.dbg = {}
        for name, (shape, _fn) in DEBUG.items():
            self.dbg[name] = nc.dram_tensor("dbg_" + name, list(shape), F32, kind="ExternalOutput").ap()

        with contextlib.ExitStack() as es:
            self.kb = KB(nc, es)
            kb = self.kb
            self.psum = es.enter_context(nc.psum_tensor("psum", [128, 4096], F32))
            self.psum_bf = self.psum[:].bitcast(BF16)
            self.ident_f = self.sb(es, [128, 128], F32, "ident_f")
            self.ident_b = self.sb(es, [128, 128], BF16, "ident_b")
            self.ones_b = self.sb(es, [128, 128], BF16, "ones_b")
            self.zero_b = self.sb(es, [128, 32], BF16, "zero_b")
            kb.dma("sp", self.ident_f[:], I["c_ident"][:, :], writes=["ident_f"])
            kb.op("dve", lambda e: e.tensor_copy(out=self.ident_b[:], in_=self.ident_f[:]),
                  reads=["ident_f"], writes=["ident_b"])
            kb.op("dve", lambda e: e.memset(self.ones_b[:], 1.0), writes=["ones_b"])
            kb.op("dve", lambda e: e.memset(self.zero_b[:], 0.0), writes=["zero_b"])
            self.psrr = 0

            self.phase_cast()
            kb.barrier()
            phases = [("P", None, 0), ("I", 0), ("E", 0), ("P", 0, 1), ("I", 1), ("O", 1), ("P", 1, 2), ("I", 2), ("E", 2),
                      ("P", 2, 3), ("I", 3), ("O", 3), ("P", 3, None)]
            if PHASES is not None:
                phases = PHASES
            for ph in phases:
                tag = "_".join(str(a) for a in ph)
                if ph[0] == "P":
                    self.phase_P(ph[1], ph[2])
                elif ph[0] == "E":
                    self.phase_E(ph[1])
                elif ph[0] == "I":
                    self.phase_I(ph[1])
                else:
                    self.phase_O(ph[1])
                kb.barrier()
                if STOP_AFTER == tag:
                    break
            self.dump_debug()
            kb.barrier()
        return nc

    def bank(self, b, n=512, off=0):
        return self.psum[:, b * 512 + off: b * 512 + off + n]

    def bank_bf(self, b, n=1024, off=0):
        return self.psum_bf[:, b * 1024 + off: b * 1024 + off + n]

    def next_bank(self, lo=0, hi=8):
        b = lo + self.psrr % (hi - lo)
        self.psrr += 1
        return b

    def mm(self, out_ap, pairs, reads, wkey, transpose=False):
        kb = self.kb
        n = len(pairs)
        for i, (l, r) in enumerate(pairs):
            kb.op("pe", lambda e, l=l, r=r, i=i: e.matmul(out_ap, lhsT=l, rhs=r, start=(i == 0), stop=(i == n - 1)),
                  reads=reads, writes=[wkey], signal=(i == n - 1))

    def dump_debug(self):
        kb = self.kb
        for name, ap in self.dbg.items():
            fn = DEBUG[name][1]
            if fn is None:
                continue
            kb.dma("pool", ap, fn(self), writes=["dbg_" + name])

    def phase_cast(self):
        kb, I, S = self.kb, self.I, self.S
        def blk(dst4, src2d, k0, c0, nkc, ncols):
            kb.dma("pool", dst4[:, 0:nkc, 0:ncols],
                   src2d[k0:k0 + nkc * 128, c0:c0 + ncols].rearrange("(kc p) c -> p kc c", p=128))
        for e in range(2):
            for i, c0 in enumerate(EV_COLS):
                blk(S["wk_ev_in"][e, i], I["ev_w_in"][e], 0, c0, 16, 512)
            blk(S["wk_ev_in"][e, 28], I["ev_w_in"][e], 0, 6144, 16, 64)
            for i in range(24):
                blk(S["wk_od_in"][e, i], I["od_w_in"][e], 0, i * 512, 16, 512)
            for nb in range(4):
                for hf in range(2):
                    blk(S["wk_ev_out"][e, nb * 2 + hf], I["ev_w_out"][e], hf * 2048, nb * 512, 16, 512)
                    blk(S["wk_od_out"][e, nb * 2 + hf], I["od_w_out"][e], hf * 2048, nb * 512, 16, 512)
        for l in range(DEPTH):
            for nb in range(4):
                blk(S["wk_gate"][l, nb], I["ple_w_gate"][l], 0, nb * 512, 16, 512)
                blk(S["wk_proj"][l, nb], I["ple_w_proj"][l], 0, nb * 512, 2, 512)
        zt = self.zero_b
        for cb in range(32):
            kb.dma("sp", S["XBCT"][cb * 128:(cb + 1) * 128, 0:2], zt[:, 0:2], reads=["zero_b"])
            kb.dma("sp", S["XBCT"][cb * 128:(cb + 1) * 128, L + 2:L + 4], zt[:, 0:2], reads=["zero_b"])
            kb.dma("sp", S["VT"][cb * 128:(cb + 1) * 128, 0:15], zt[:, 0:15], reads=["zero_b"])
            kb.dma("sp", S["VT"][cb * 128:(cb + 1) * 128, L + 15:L + 30], zt[:, 0:15], reads=["zero_b"])

    def norm_T(self, es_tiles, Hs, wrep, AT, sq_junk, hn_tiles, stat, hkey):
        kb = self.kb
        ss = stat
        for tt in range(TPS):
            kb.op("act", lambda e, tt=tt: e.activation(out=sq_junk[:], in_=Hs[:, tt, :], func=AF.Square,
                                                        accum_out=ss[:, tt:tt + 1]),
                  reads=[hkey], writes=["sq_junk", "stat"])
        kb.op("act", lambda e: e.activation(out=ss[:, TPS:2 * TPS], in_=ss[:, 0:TPS], func=AF.Sqrt,
                                            bias=EPS, scale=1.0 / D), reads=["stat"], writes=["stat"])
        kb.op("dve", lambda e: e.reciprocal(out=ss[:, 2 * TPS:3 * TPS], in_=ss[:, TPS:2 * TPS]),
              reads=["stat"], writes=["stat"])
        for tt in range(TPS):
            hn = hn_tiles[tt % len(hn_tiles)]
            hk = f"hn{tt % len(hn_tiles)}"
            kb.op("dve", lambda e, tt=tt, hn=hn: e.scalar_tensor_tensor(
                out=hn[:], in0=Hs[:, tt, :], scalar=ss[:, 2 * TPS + tt:2 * TPS + tt + 1], in1=wrep[:],
                op0=ALU.mult, op1=ALU.mult), reads=[hkey, "stat", "wrep"], writes=[hk])
            self.transpose_into(hn, 16, AT, tt, hk, "AT")

    def transpose_into(self, src, nkc, dstT, tt, skey, dkey):
        kb = self.kb
        for g0 in range(0, nkc, 8):
            b = self.next_bank()
            n = min(8, nkc - g0)
            for j in range(n):
                kc = g0 + j
                kb.op("pe", lambda e, kc=kc, j=j, b=b: e.transpose(self.bank_bf(b, 128, j * 128),
                                                                src[:, kc * 128:(kc + 1) * 128], self.ident_b[:]),
                      reads=[skey, "ident_b"], writes=[f"ps{b}"], signal=(j == n - 1))
            eng = "act" if (self.psrr % 2 == 0) else "dve"
            dst = dstT[:, g0:g0 + n, tt * 128:(tt + 1) * 128]
            srcp = self.bank_bf(b, n * 128).rearrange("p (a c) -> p a c", a=n)
            if eng == "act":
                kb.op("act", lambda e, dst=dst, srcp=srcp: e.activation(out=dst, in_=srcp, func=AF.Copy),
                      reads=[f"ps{b}"], writes=[dkey])
            else:
                kb.op("dve", lambda e, dst=dst, srcp=srcp: e.tensor_copy(out=dst, in_=srcp),
                      reads=[f"ps{b}"], writes=[dkey])

    def phase_P(self, lp, ln):
        kb, I, S, nc = self.kb, self.I, self.S, self.nc
        with contextlib.ExitStack() as es:
            Hs = self.sb(es, [128, TPS, D], F32, "Hs")
            AT = self.sb(es, [128, 16, TS], BF16, "AT")
            hn_tiles = [self.sb(es, [128, D], BF16, f"hn{i}") for i in range(2)]
            sq_junk = self.sb(es, [128, D], BF16, "sq_junk")
            stat = self.sb(es, [128, 3 * TPS], F32, "stat")
            wrep = self.sb(es, [128, D], F32, "wrep")
            wbuf = [self.sb(es, [128, 16, 512], BF16, f"wbuf{i}") for i in range(3)]
            stg = [self.sb(es, [128, 512], F32, f"stg{i}") for i in range(3)]
            stgb = [self.sb(es, [128, 512], BF16, f"stgb{i}") for i in range(3)]
            self.wrr = 0
            self.srr = 0
            if lp is not None:
                yT = self.sb(es, [128, 32, TS], BF16, "yT")
                ybf = self.sb(es, [128, 4096], BF16, "ybf")
                ld = [self.sb(es, [128, 1024], F32, f"ld{i}") for i in range(4)]
                gst = self.sb(es, [128, 16], F32, "gst")
                grep = self.sb(es, [128, D], F32, "grep")
                pf = self.sb(es, [128, TPS, 256], F32, "pf")
                pb = self.sb(es, [128, TPS, 256], BF16, "pb")
                pT = self.sb(es, [128, 2, TS], BF16, "pT")
                wpj = [self.sb(es, [128, 2, 512], BF16, f"wpj{i}") for i in range(2)]
                sig = [self.sb(es, [128, 512], F32, f"sig{i}") for i in range(2)]
                if lp % 2 == 0:
                    kb.dma("sp", grep[:], I["ev_gnorm_w"][lp // 2:lp // 2 + 1, :].partition_broadcast(128),
                           writes=["grep"])

            def load_w(blk4, ncols=512, nkc=16):
                i = self.wrr % len(wbuf)
                self.wrr += 1
                t = wbuf[i]
                kb.dma("sp", t[:, 0:nkc, 0:ncols], blk4[:, 0:nkc, 0:ncols], writes=[f"wbuf{i}"])
                return t, f"wbuf{i}"

            def next_stg(bf=False):
                i = self.srr % 3
                self.srr += 1
                return (stgb[i], f"stgb{i}") if bf else (stg[i], f"stg{i}")

            for st in range(NST_RUN):
                t0 = st * TS
                srcH = I["x"] if lp is None else S["H"]
                kb.dma("sp", Hs[:], srcH[t0:t0 + TS, :].rearrange("(a p) d -> p a d", p=128), writes=["Hs"])
                if lp is not None:
                    e_idx = lp // 2
                    if lp % 2 == 1:
                        kb.dma("sp", yT[:], S["YOT"][:, t0:t0 + TS].rearrange("(kc p) t -> p kc t", p=128), writes=["yT"])
                    for tt in range(TPS if lp % 2 == 0 else 0):
                        r0 = t0 + tt * 128
                        if lp % 2 == 0:
                            for hf in range(2):
                                c0 = hf * 1024
                                a, b_, c_, d_ = ld[0:4]
                                ka, kb_, kc_, kd_ = [f"ld{j}" for j in range(4)]
                                kb.dma("sp", a[:], S["YS"][r0:r0 + 128, c0:c0 + 1024], writes=[ka])
                                kb.dma("sp", b_[:], S["SZ"][r0:r0 + 128, c0:c0 + 1024], writes=[kb_])
                                kb.dma("sp", c_[:], S["YN"][r0:r0 + 128, c0:c0 + 1024], writes=[kc_])
                                kb.dma("sp", d_[:], S["SG"][r0:r0 + 128, c0:c0 + 1024], writes=[kd_])
                                kb.op("dve", lambda e, a=a, b_=b_: e.tensor_tensor(out=a[:], in0=a[:], in1=b_[:], op=ALU.mult),
                                      reads=[ka, kb_], writes=[ka])
                                kb.op("pool", lambda e, a=a, b_=b_: e.tensor_tensor(out=b_[:], in0=a[:], in1=a[:], op=ALU.mult),
                                      reads=[ka], writes=[kb_])
                                kb.op("dve", lambda e, b_=b_, hf=hf: e.tensor_reduce(
                                    out=gst[:, hf * 4:hf * 4 + 4], in_=b_[:].rearrange("p (g c) -> p g c", g=4),
                                    axis=AX.X, op=ALU.add), reads=[kb_], writes=["gst"])
                                kb.op("act", lambda e, hf=hf: e.activation(out=gst[:, 8 + hf * 4:8 + hf * 4 + 4],
                                                                          in_=gst[:, hf * 4:hf * 4 + 4], func=AF.Sqrt,
                                                                          bias=EPS, scale=1.0 / 256), reads=["gst"], writes=["gst"])
                                kb.op("dve", lambda e, hf=hf: e.reciprocal(out=gst[:, hf * 4:hf * 4 + 4],
                                                                          in_=gst[:, 8 + hf * 4:8 + hf * 4 + 4]),
                                      reads=["gst"], writes=["gst"])
                                kb.op("dve", lambda e, a=a, hf=hf: e.tensor_tensor(
                                    out=a[:].rearrange("p (g c) -> p g c", g=4), in0=a[:].rearrange("p (g c) -> p g c", g=4),
                                    in1=_bc(gst[:, hf * 4:hf * 4 + 4], 256), op=ALU.mult), reads=[ka, "gst"], writes=[ka])
                                kb.op("dve", lambda e, a=a, c0=c0: e.tensor_tensor(out=ybf[:, c0:c0 + 1024], in0=a[:],
                                                                                 in1=grep[:, c0:c0 + 1024], op=ALU.mult),
                                      reads=[ka, "grep"], writes=["ybf"])
                                kb.op("pool", lambda e, c_=c_, d_=d_, c0=c0: e.tensor_tensor(
                                    out=ybf[:, 2048 + c0:2048 + c0 + 1024], in0=c_[:], in1=d_[:], op=ALU.mult),
                                    reads=[kc_, kd_], writes=["ybf"])
                        self.transpose_into(ybf, 32, yT, tt, "ybf", "yT")
                    wo = S["wk_ev_out"][e_idx] if lp % 2 == 0 else S["wk_od_out"][e_idx]
                    for nb in range(4):
                        wa, kwa = load_w(wo[nb * 2])
                        wb_, kwb = load_w(wo[nb * 2 + 1])
                        for tt in range(TPS):
                            b = self.next_bank()
                            pairs = [(yT[:, kc, tt * 128:(tt + 1) * 128], (wa if kc < 16 else wb_)[:, kc % 16, :])
                                     for kc in range(32)]
                            self.mm(self.bank(b), pairs, ["yT", kwa, kwb], f"ps{b}")
                            kb.op("dve", lambda e, tt=tt, nb=nb, b=b: e.tensor_tensor(
                                out=Hs[:, tt, nb * 512:(nb + 1) * 512], in0=self.bank(b),
                                in1=Hs[:, tt, nb * 512:(nb + 1) * 512], op=ALU.add),
                                reads=[f"ps{b}", "Hs"], writes=["Hs"])
                    if "hmix" in self.dbg and lp == 0:
                        kb.dma("pool", self.dbg["hmix"][t0:t0 + TS, :].rearrange("(a p) d -> p a d", p=128), Hs[:],
                               reads=["Hs"], writes=["dbg_hmix"])
                    kb.dma("sp", wrep[:], I["ple_norm_w"][lp:lp + 1, :].partition_broadcast(128), writes=["wrep"])
                    self.norm_T(es, Hs, wrep, AT, sq_junk, hn_tiles, stat, "Hs")
                    kb.dma("sp", pf[:], I["p"][lp, t0:t0 + TS, :].rearrange("(a p) d -> p a d", p=128), writes=["pf"])
                    kb.op("pool", lambda e: e.tensor_copy(out=pb[:], in_=pf[:]), reads=["pf"], writes=["pb"])
                    for tt in range(TPS):
                        b = self.next_bank()
                        for j in range(2):
                            kb.op("pe", lambda e, tt=tt, j=j, b=b: e.transpose(self.bank_bf(b, 128, j * 128),
                                                                           pb[:, tt, j * 128:(j + 1) * 128], self.ident_b[:]),
                                  reads=["pb", "ident_b"], writes=[f"ps{b}"], signal=(j == 1))
                        kb.op("act", lambda e, tt=tt, b=b: e.activation(
                            out=pT[:, :, tt * 128:(tt + 1) * 128],
                            in_=self.bank_bf(b, 256).rearrange("p (a c) -> p a c", a=2), func=AF.Copy),
                            reads=[f"ps{b}"], writes=["pT"])
                    for nb in range(4):
                        wg, kwg = load_w(S["wk_gate"][lp, nb])
                        wp = wpj[nb % 2]
                        kwp = f"wpj{nb % 2}"
                        kb.dma("sp", wp[:], S["wk_proj"][lp, nb], writes=[kwp])
                        for tt in range(TPS):
                            b = self.next_bank()
                            b2 = self.next_bank()
                            self.mm(self.bank(b), [(AT[:, kc, tt * 128:(tt + 1) * 128], wg[:, kc, :]) for kc in range(16)],
                                    ["AT", kwg], f"ps{b}")
                            self.mm(self.bank(b2), [(pT[:, kc, tt * 128:(tt + 1) * 128], wp[:, kc, :]) for kc in range(2)],
                                    ["pT", kwp], f"ps{b2}")
                            sg_ = sig[tt % 2]
                            ks = f"sig{tt % 2}"
                            kb.op("act", lambda e, sg_=sg_, b=b: e.activation(out=sg_[:], in_=self.bank(b), func=AF.Sigmoid),
                                  reads=[f"ps{b}"], writes=[ks])
                            kb.op("dve", lambda e, sg_=sg_, b2=b2: e.tensor_tensor(out=sg_[:], in0=self.bank(b2), in1=sg_[:], op=ALU.mult),
                                  reads=[f"ps{b2}", ks], writes=[ks])
                            kb.op("pool", lambda e, sg_=sg_, tt=tt, nb=nb: e.tensor_tensor(
                                out=Hs[:, tt, nb * 512:(nb + 1) * 512], in0=Hs[:, tt, nb * 512:(nb + 1) * 512],
                                in1=sg_[:], op=ALU.add), reads=[ks, "Hs"], writes=["Hs"])
                if ln is not None:
                    kb.dma("pool", S["H"][t0:t0 + TS, :].rearrange("(a p) d -> p a d", p=128), Hs[:], reads=["Hs"], writes=["Hd"])
                    if "hple" in self.dbg and lp == 0:
                        kb.dma("pool", self.dbg["hple"][t0:t0 + TS, :].rearrange("(a p) d -> p a d", p=128), Hs[:],
                               reads=["Hs"], writes=["dbg_hple"])
                else:
                    kb.dma("sp", wrep[:], I["final_norm_w"][0:1, :].partition_broadcast(128), writes=["wrep"])
                    ss = stat
                    for tt in range(TPS):
                        kb.op("act", lambda e, tt=tt: e.activation(out=sq_junk[:], in_=Hs[:, tt, :], func=AF.Square,
                                                                    accum_out=ss[:, tt:tt + 1]),
                              reads=["Hs"], writes=["sq_junk", "stat"])
                    kb.op("act", lambda e: e.activation(out=ss[:, TPS:2 * TPS], in_=ss[:, 0:TPS], func=AF.Sqrt,
                                                        bias=EPS, scale=1.0 / D), reads=["stat"], writes=["stat"])
                    kb.op("dve", lambda e: e.reciprocal(out=ss[:, 2 * TPS:3 * TPS], in_=ss[:, TPS:2 * TPS]),
                          reads=["stat"], writes=["stat"])
                    for tt in range(TPS):
                        kb.op("dve", lambda e, tt=tt: e.scalar_tensor_tensor(
                            out=Hs[:, tt, :], in0=Hs[:, tt, :], scalar=ss[:, 2 * TPS + tt:2 * TPS + tt + 1], in1=wrep[:],
                            op0=ALU.mult, op1=ALU.mult), reads=["Hs", "stat", "wrep"], writes=["Hs"])
                    kb.dma("pool", self.out[t0:t0 + TS, :].rearrange("(a p) d -> p a d", p=128), Hs[:], reads=["Hs"], writes=["outd"])
                    continue
                e2 = ln // 2
                nw = I["ev_norm_w"] if ln % 2 == 0 else I["od_norm_w"]
                kb.dma("sp", wrep[:], nw[e2:e2 + 1, :].partition_broadcast(128), writes=["wrep"])
                self.norm_T(es, Hs, wrep, AT, sq_junk, hn_tiles, stat, "Hs")
                kb.dma("pool", S["ATD"][st], AT[:], reads=["AT"], writes=["ATD"])

    def phase_I(self, ln):
        kb, S = self.kb, self.S
        with contextlib.ExitStack() as es:
            NW = 8
            wbuf = [self.sb(es, [128, 16, 512], BF16, f"iw{i}") for i in range(NW)]
            ATb = [self.sb(es, [128, 16, TS], BF16, f"iAT{i}") for i in range(3)]
            stg = [self.sb(es, [128, 512], F32, f"istg{i}") for i in range(4)]
            stgb = [self.sb(es, [128, 512], BF16, f"istgb{i}") for i in range(4)]
            self.srr = 0

            def next_stg(bf=False):
                i = self.srr % 4
                self.srr += 1
                return (stgb[i], f"istgb{i}") if bf else (stg[i], f"istg{i}")

            blocks = self.blocks_even(ln // 2, next_stg) if ln % 2 == 0 else self.blocks_odd(ln // 2, next_stg)
            groups, cur, n = [], [], 0
            for bl in blocks:
                if n + len(bl[0]) > NW:
                    groups.append(cur)
                    cur, n = [], 0
                cur.append(bl)
                n += len(bl[0])
            if cur:
                groups.append(cur)
            ati = 0
            for grp in groups:
                wi = 0
                loaded = []
                for (wl, body) in grp:
                    ws = []
                    for (blk4, ncols) in wl:
                        t, k_ = wbuf[wi], f"iw{wi}"
                        wi += 1
                        kb.dma("sp", t[:, :, 0:ncols], blk4[:, :, 0:ncols], writes=[k_])
                        ws.append((t, k_))
                    loaded.append((ws, body))
                for tb in range(L // TS):
                    A_, kA = ATb[ati % 3], f"iAT{ati % 3}"
                    ati += 1
                    kb.dma("sp", A_[:], S["ATD"][tb], reads=["ATD"], writes=[kA])
                    for (ws, body) in loaded:
                        body(ws, A_, kA, tb * TS)

    def tok_block(self, AT, kA, w, kw, tt, ncols=512):
        b = self.next_bank()
        self.mm(self.bank(b, ncols), [(AT[:, kc, tt * 128:(tt + 1) * 128], w[:, kc, 0:ncols]) for kc in range(16)],
                [kA, kw], f"ps{b}")
        return b

    def feat_block(self, AT, kA, w, kw, cl):
        b = self.next_bank()
        self.mm(self.bank(b), [(w[:, kc, cl * 128:(cl + 1) * 128], AT[:, kc, :]) for kc in range(16)],
                [kA, kw], f"ps{b}")
        return b

    def blocks_even(self, e, next_stg):
        kb, S = self.kb, self.S
        W = S["wk_ev_in"][e]
        ci = {c: i for i, c in enumerate(EV_COLS)}
        out = []

        def tok_body(dst, mode, nb):
            def body(ws, AT, kA, t0):
                (w, kw), = ws
                for tt in range(TPS):
                    b = self.tok_block(AT, kA, w, kw, tt)
                    r0 = t0 + tt * 128
                    if mode == "silu":
                        s, ks = next_stg()
                        kb.op("act", lambda e_, s=s, b=b: e_.activation(out=s[:], in_=self.bank(b), func=AF.Silu),
                              reads=[f"ps{b}"], writes=[ks])
                    else:
                        s, ks = next_stg(bf=True)
                        kb.op("dve", lambda e_, s=s, b=b: e_.tensor_copy(out=s[:], in_=self.bank(b)),
                              reads=[f"ps{b}"], writes=[ks])
                    kb.dma("pool", S[dst][r0:r0 + 128, nb * 512:(nb + 1) * 512], s[:], reads=[ks], writes=[dst])
            return body

        def feat_body(dst, doff, nb):
            def body(ws, AT, kA, t0):
                (w, kw), = ws
                for cl in range(4):
                    b = self.feat_block(AT, kA, w, kw, cl)
                    s, ks = next_stg(bf=True)
                    if cl % 2 == 0:
                        kb.op("act", lambda e_, s=s, b=b: e_.activation(out=s[:], in_=self.bank(b), func=AF.Copy),
                              reads=[f"ps{b}"], writes=[ks])
                    else:
                        kb.op("dve", lambda e_, s=s, b=b: e_.tensor_copy(out=s[:], in_=self.bank(b)),
                              reads=[f"ps{b}"], writes=[ks])
                    ch0 = (nb * 4 + cl) * 128
                    kb.dma("pool", S[dst][ch0:ch0 + 128, doff + t0:doff + t0 + TS], s[:], reads=[ks], writes=[dst])
            return body

        def dt_body(ws, AT, kA, t0):
            (w, kw), = ws
            s, ks = next_stg()
            for tt in range(TPS):
                b = self.tok_block(AT, kA, w, kw, tt, 64)
                kb.op("dve", lambda e_, s=s, b=b, tt=tt: e_.tensor_copy(out=s[:, tt * 64:(tt + 1) * 64], in_=self.bank(b, 64)),
                      reads=[f"ps{b}"], writes=[ks])
            kb.dma("pool", S["DT"][t0:t0 + TS, :].rearrange("(a p) c -> p a c", p=128),
                   s[:, 0:TPS * 64].rearrange("p (a c) -> p a c", a=TPS), reads=[ks], writes=["DT"])

        for (c_base, dst, mode) in ((0, "SZ", "silu"), (12352, "SG", "silu"), (10304, "V", "bf")):
            for nb in range(4):
                out.append(([(W[ci[c_base + nb * 512]], 512)], tok_body(dst, mode, nb)))
        for (c_base, nblk, dst, doff) in ((2048, 8, "XBCT", 2), (6208, 4, "QT", 0), (8256, 4, "KT", 0)):
            for nb in range(nblk):
                out.append(([(W[ci[c_base + nb * 512]], 512)], feat_body(dst, doff, nb)))
        out.append(([(W[28], 64)], dt_body))
        return out

    def blocks_odd(self, e, next_stg):
        kb, S = self.kb, self.S
        W = S["wk_od_in"][e]
        out = []

        def v_body(nb):
            def body(ws, AT, kA, t0):
                (wa, kwa), (wg, kwg) = ws
                for cl in range(4):
                    ba = self.feat_block(AT, kA, wa, kwa, cl)
                    bg = self.feat_block(AT, kA, wg, kwg, cl)
                    s, ks = next_stg()
                    sb_, ksb = next_stg(bf=True)
                    kb.op("act", lambda e_, s=s, bg=bg: e_.activation(out=s[:], in_=self.bank(bg), func=AF.Sigmoid),
                          reads=[f"ps{bg}"], writes=[ks])
                    kb.op("dve", lambda e_, s=s, sb_=sb_, ba=ba: e_.tensor_tensor(out=sb_[:], in0=self.bank(ba), in1=s[:], op=ALU.mult),
                          reads=[f"ps{ba}", ks], writes=[ksb])
                    ch0 = (nb * 4 + cl) * 128
                    kb.dma("pool", S["VT"][ch0:ch0 + 128, 15 + t0:15 + t0 + TS], sb_[:], reads=[ksb], writes=["VT"])
            return body

        def g_body(nb):
            def body(ws, AT, kA, t0):
                (w, kw), = ws
                for cl in range(4):
                    b = self.feat_block(AT, kA, w, kw, cl)
                    s, ks = next_stg()
                    kb.op("act", lambda e_, s=s, b=b: e_.activation(out=s[:], in_=self.bank(b), func=AF.Silu),
                          reads=[f"ps{b}"], writes=[ks])
                    ch0 = (nb * 4 + cl) * 128
                    kb.dma("pool", S["SG2T"][ch0:ch0 + 128, t0:t0 + TS], s[:], reads=[ks], writes=["SG2T"])
            return body

        for nb in range(8):
            out.append(([(W[nb], 512), (W[8 + nb], 512)], v_body(nb)))
        for nb in range(8):
            out.append(([(W[16 + nb], 512)], g_body(nb)))
        return out

    def phase_E(self, layer):
        kb, I, S = self.kb, self.I, self.S
        e = layer // 2
        with contextlib.ExitStack() as es:
            C = {}
            C["tri"] = self.sb(es, [128, 3, 128], F32, "tri")
            C["maskb"] = self.sb(es, [128, 2, 512], BF16, "maskb")
            C["dtb"] = self.sb(es, [128, 64], F32, "dtb")
            C["arep"] = self.sb(es, [128, 64], F32, "arep")
            C["dsk"] = self.sb(es, [128, 32], F32, "dsk")
            kb.dma("sp", C["tri"][:], I["c_tri"].rearrange("a p c -> p a c"), writes=["tri"])
            kb.dma("pool", C["maskb"][:], I["c_mask"].rearrange("a p c -> p a c"), writes=["maskb"])
            kb.dma("sp", C["dtb"][:], I["ev_dt_bias"][e:e + 1, :].partition_broadcast(128), writes=["dtb"])
            kb.dma("sp", C["arep"][:], I["ev_a_log"][e:e + 1, :].partition_broadcast(128), writes=["arep"])
            kb.dma("sp", C["dsk"][:], I["ev_d_skip"][e:e + 1, :].partition_broadcast(128), writes=["dsk"])
            kb.op("act", lambda e_: e_.activation(out=C["arep"][:], in_=C["arep"][:], func=AF.Exp), reads=["arep"], writes=["arep"])
            kb.op("dve", lambda e_: e_.tensor_scalar(out=C["arep"][:], in0=C["arep"][:], scalar1=-1.0, scalar2=None, op0=ALU.mult),
                  reads=["arep"], writes=["arep"])
            for nm, shp in (("dtr", [128, 64]), ("x1", [128, 64]), ("dtv", [128, 64]), ("lndt", [128, 64]), ("da", [128, 64]),
                            ("nda", [128, 32]), ("cums", [128, 192]), ("Ein", [128, 6, 32]), ("Eout", [128, 6, 32]), ("biasf", [128, 32])):
                C[nm] = self.sb(es, shp, F32, nm)
            self.C = C
            if "A" in E_PARTS:
                self.ssd_pass_A(e)
                kb.barrier()
            if "C" in E_PARTS:
                self.ssd_pass_C(e)
                kb.barrier()
        if "N" in E_PARTS:
            self.na(e)

    def dtq(self, c, fixed=None):
        kb, S, C = self.kb, self.S, self.C
        K_ = ["dtq"]
        kb.dma("sp", C["dtr"][:], S["DT"][c * 128:(c + 1) * 128, :], reads=["DT"], writes=["dtr"])
        kb.op("dve", lambda e: e.tensor_tensor(out=C["x1"][:], in0=C["dtr"][:], in1=C["dtb"][:], op=ALU.add),
              reads=["dtr", "dtb"], writes=K_)
        kb.op("act", lambda e: e.activation(out=C["x1"][:], in_=C["x1"][:], func=AF.Exp), reads=K_, writes=K_)
        kb.op("act", lambda e: e.activation(out=C["dtv"][:], in_=C["x1"][:], func=AF.Ln, bias=1.0, scale=1.0), reads=K_, writes=K_)
        kb.op("act", lambda e: e.activation(out=C["lndt"][:], in_=C["dtv"][:], func=AF.Ln), reads=K_, writes=K_)
        kb.op("dve", lambda e: e.tensor_tensor(out=C["da"][:], in0=C["dtv"][:], in1=C["arep"][:], op=ALU.mult),
              reads=K_ + ["arep"], writes=K_)
        kb.op("dve", lambda e: e.tensor_scalar(out=C["nda"][:], in0=C["da"][:, 32:64], scalar1=-1.0, scalar2=None, op0=ALU.mult),
              reads=K_, writes=K_)
        if fixed is None:
            b = self.next_bank()
            off, pk = 0, f"ps{b}"
        else:
            b, off, pk = fixed
        tri = C["tri"]
        for j in range(3):
            kb.op("pe", lambda e, j=j: e.matmul(self.bank(b, 64, off + j * 64), lhsT=tri[:, j, :], rhs=C["da"][:], start=True, stop=True),
                  reads=K_ + ["tri"], writes=[pk], signal=(j == 2))
        cums = C["cums"]
        kb.op("act", lambda e: e.activation(out=cums[:], in_=self.bank(b, 192, off), func=AF.Copy), reads=[pk], writes=K_)
        cI, cE, tot = cums[:, 0:64], cums[:, 64:128], cums[:, 128:192]
        Ein, Eout = C["Ein"], C["Eout"]
        R = K_
        kb.op("dve", lambda e: e.tensor_copy(out=Ein[:, 2, :], in_=cI[:, 0:32]), reads=R, writes=K_)
        kb.op("dve", lambda e: e.tensor_copy(out=Ein[:, 4:6, :], in_=tot.rearrange("p (a c) -> p a c", a=2)), reads=R, writes=K_)
        kb.op("dve", lambda e: e.tensor_tensor(out=Ein[:, 0, :], in0=tot[:, 0:32], in1=Ein[:, 2, :], op=ALU.subtract), reads=R, writes=K_)
        kb.op("dve", lambda e: e.tensor_tensor(out=Ein[:, 0, :], in0=Ein[:, 0, :], in1=C["lndt"][:, 0:32], op=ALU.add), reads=R, writes=K_)
        kb.op("dve", lambda e: e.tensor_tensor(out=Ein[:, 1, :], in0=cE[:, 32:64], in1=C["lndt"][:, 32:64], op=ALU.add), reads=R, writes=K_)
        kb.op("dve", lambda e: e.tensor_tensor(out=Ein[:, 3, :], in0=tot[:, 32:64], in1=cE[:, 32:64], op=ALU.subtract), reads=R, writes=K_)
        kb.op("dve", lambda e: e.tensor_tensor(out=C["biasf"][:], in0=C["lndt"][:, 0:32], in1=Ein[:, 2, :], op=ALU.subtract), reads=R, writes=K_)
        kb.op("act", lambda e: e.activation(out=Eout[:], in_=Ein[:], func=AF.Exp), reads=K_, writes=K_)

    def ssd_pass_A(self, e):
        kb, I, S, C = self.kb, self.I, self.S, self.C
        with contextlib.ExitStack() as es:
            wT = self.sb(es, [128, 32, 5], F32, "wT")
            diag = self.sb(es, [128, 32, 5, 128], BF16, "diag")
            cbias = self.sb(es, [128, 32], F32, "cbias")
            brow = self.sb(es, [1, 3072], BF16, "brow")
            xin = [self.sb(es, [128, 32, 516], BF16, f"xin{i}") for i in range(2)]
            xc = self.sb(es, [128, 4, 2048], BF16, "xc")
            btok = self.sb(es, [128, 4, 1024], BF16, "btok")
            bT = self.sb(es, [128, 8, 512], BF16, "bT")
            cT = self.sb(es, [128, 8, 512], BF16, "cT")
            Sf = self.sb(es, [128, 2048], F32, "Sf")
            xw = [self.sb(es, [128, 2048], BF16, f"xw{i}") for i in range(2)]
            stgb = self.sb(es, [128, 2048], BF16, "stgb")
            stgf = self.sb(es, [128, 2048], F32, "stgf")
            kb.dma("sp", wT[:], I["ev_conv_wT"][e].rearrange("(cb p) k -> p cb k", p=128), writes=["wT"])
            kb.dma("sp", cbias[:], I["ev_conv_bpm"][e], writes=["cbias"])
            kb.dma("pool", brow[:], I["ev_conv_b"][e:e + 1, 0:3072], writes=["brow"])
            for cb in range(32):
                for k in range(5):
                    kb.op("dve", lambda e_, cb=cb, k=k: e_.tensor_scalar(out=diag[:, cb, k, :], in0=self.ident_f[:],
                                                                       scalar1=wT[:, cb, k:k + 1], scalar2=None, op0=ALU.mult),
                          reads=["wT", "ident_f"], writes=["diag"])
            kb.op("dve", lambda e_: e_.memset(Sf[:], 0.0), writes=["Sf"])
            for blk in range(NT // 4):
                t0 = blk * 512
                xi = xin[blk % 2]
                kx = f"xin{blk % 2}"
                kb.dma("sp", xi[:], S["XBCT"][:, t0:t0 + 516].rearrange("(cb p) t -> p cb t", p=128), reads=["XBCT"], writes=[kx])
                for ci in range(4):
                    for cbg in range(6):
                        b = self.next_bank()
                        for j in range(4):
                            cb = cbg * 4 + j
                            pairs = [(xi[:, cb, ci * 128 + k:ci * 128 + k + 128], diag[:, cb, k, :]) for k in range(5)]
                            pairs.append((self.ones_b[0:1, :], brow[0:1, cb * 128:(cb + 1) * 128]))
                            self.mm(self.bank(b, 128, j * 128), pairs, [kx, "diag", "brow", "ones_b"], f"ps{b}")
                        dst = xc[:, ci, cbg * 512:(cbg + 1) * 512] if cbg < 4 else btok[:, ci, (cbg - 4) * 512:(cbg - 3) * 512]
                        kb.op("act", lambda e_, dst=dst, b=b: e_.activation(out=dst, in_=self.bank(b), func=AF.Silu),
                              reads=[f"ps{b}"], writes=["xc" if cbg < 4 else "btok"])
                for cb in range(16, 32):
                    b = self.next_bank()
                    pairs = [(diag[:, cb, k, :], xi[:, cb, k:k + 512]) for k in range(5)]
                    self.mm(self.bank(b), pairs, [kx, "diag"], f"ps{b}")
                    dst = bT[:, cb - 16, :] if cb < 24 else cT[:, cb - 24, :]
                    kb.op("act", lambda e_, dst=dst, b=b, cb=cb: e_.activation(out=dst, in_=self.bank(b), func=AF.Silu,
                                                                           bias=cbias[:, cb:cb + 1]),
                          reads=[f"ps{b}", "cbias"], writes=["bT" if cb < 24 else "cT"])
                kb.dma("pool", S["XC"][t0:t0 + 512, :].rearrange("(a p) d -> p a d", p=128), xc[:], reads=["xc"], writes=["XC"])
                kb.dma("pool", S["BT"][:, t0:t0 + 512].rearrange("(g p) t -> p g t", p=128), bT[:], reads=["bT"], writes=["BT"])
                kb.dma("pool", S["CT"][:, t0:t0 + 512].rearrange("(g p) t -> p g t", p=128), cT[:], reads=["cT"], writes=["CT"])
                for ci in range(4):
                    c = blk * 4 + ci
                    self.dtq(c)
                    Eout = C["Eout"]
                    for d_ in range(2):
                        xw_ = xw[d_]
                        kxw = f"xw{d_}"
                        kb.op("dve", lambda e_, xw_=xw_, ci=ci, d_=d_: e_.tensor_tensor(
                            out=xw_[:].rearrange("p (h c) -> p h c", h=32), in0=xc[:, ci, :].rearrange("p (h c) -> p h c", h=32),
                            in1=_bc(Eout[:, d_, :], 64), op=ALU.mult), reads=["xc", "dtq"], writes=[kxw])
                        banks = [self.next_bank() for _ in range(4)]
                        for g in range(8):
                            bk = banks[g // 2]
                            self.mm(self.bank(bk, 256, (g % 2) * 256), [(btok[:, ci, g * 128:(g + 1) * 128], xw_[:, g * 256:(g + 1) * 256])],
                                    ["btok", kxw], f"ps{bk}")
                        if d_ == 0:
                            kb.op("act", lambda e_: e_.activation(out=stgb[:], in_=Sf[:], func=AF.Copy), reads=["Sf"], writes=["stgb"])
                            kb.dma("pool", S["SPF"][c], stgb[:], reads=["stgb"], writes=["SPF"])
                            kb.op("dve", lambda e_: e_.tensor_tensor(out=Sf[:].rearrange("p (h c) -> p h c", h=32),
                                                                    in0=Sf[:].rearrange("p (h c) -> p h c", h=32),
                                                                    in1=_bc(Eout[:, 4, :], 64), op=ALU.mult),
                                  reads=["Sf", "dtq", "stgb"], writes=["Sf"])
                            for j, bk in enumerate(banks):
                                kb.op("dve", lambda e_, j=j, bk=bk: e_.tensor_tensor(out=Sf[:, j * 512:(j + 1) * 512], in0=self.bank(bk),
                                                                                 in1=Sf[:, j * 512:(j + 1) * 512], op=ALU.add),
                                      reads=[f"ps{bk}", "Sf"], writes=["Sf"])
                        else:
                            for j, bk in enumerate(banks):
                                kb.op("act", lambda e_, j=j, bk=bk: e_.activation(out=stgf[:, j * 512:(j + 1) * 512], in_=self.bank(bk), func=AF.Copy),
                                      reads=[f"ps{bk}"], writes=["stgf"])
                            kb.dma("pool", S["LSB"][c], stgf[:], reads=["stgf"], writes=["LSB"])

    def ssd_pass_C(self, e):
        kb, I, S, C = self.kb, self.I, self.S, self.C
        with contextlib.ExitStack() as es:
            xc = self.sb(es, [128, 2048], BF16, "xcC")
            bT = self.sb(es, [128, 8, 128], BF16, "bTC")
            cT = self.sb(es, [128, 8, 128], BF16, "cTC")
            spf = self.sb(es, [128, 2048], BF16, "spf")
            lsb = self.sb(es, [128, 2048], F32, "lsb")
            Sb = self.sb(es, [128, 2048], F32, "Sb")
            Sbb = self.sb(es, [128, 2048], BF16, "Sbb")
            Xf = self.sb(es, [128, 32, 128], F32, "Xf")
            Xb = self.sb(es, [128, 32, 128], F32, "Xb")
            G = [self.sb(es, [128, 4, 128], F32, f"G{i}") for i in range(4)]
            Wt = [self.sb(es, [128, 4, 128], BF16, f"Wt{i}") for i in range(2)]
            ys = self.sb(es, [128, 2048], F32, "ysC")
            t1 = self.sb(es, [128, 512], F32, "t1")
            t2 = self.sb(es, [128, 512], F32, "t2")
            xd = self.sb(es, [128, 2048], F32, "xd")
            ones_f = C["tri"][:, 2, :]
            kb.op("dve", lambda e_: e_.memset(Sb[:], 0.0), writes=["Sb"])
            kb.op("dve", lambda e_: e_.memset(Sbb[:], 0.0), writes=["Sbb"])
            for c in range(NT - 1, -1, -1):
                r0 = c * 128
                kb.dma("sp", xc[:], S["XC"][r0:r0 + 128, :], reads=["XC"], writes=["xcC"])
                kb.dma("sp", bT[:], S["BT"][:, r0:r0 + 128].rearrange("(g p) t -> p g t", p=128), reads=["BT"], writes=["bTC"])
                kb.dma("sp", cT[:], S["CT"][:, r0:r0 + 128].rearrange("(g p) t -> p g t", p=128), reads=["CT"], writes=["cTC"])
                kb.dma("sp", spf[:], S["SPF"][c], reads=["SPF"], writes=["spf"])
                kb.dma("sp", lsb[:], S["LSB"][c], reads=["LSB"], writes=["lsb"])
                self.dtq(c)
                Ein, Eout = C["Ein"], C["Eout"]
                kb.op("dve", lambda e_: e_.tensor_tensor(out=Xf[:], in0=_bc(C["da"][:, 0:32], 128),
                                                        in1=C["tri"][:, 0, :].unsqueeze(1).broadcast_to([128, 32, 128]), op=ALU.mult),
                      reads=["dtq", "tri"], writes=["Xf"])
                kb.op("dve", lambda e_: e_.tensor_tensor(out=Xb[:], in0=_bc(C["nda"][:, 0:32], 128),
                                                        in1=C["tri"][:, 1, :].unsqueeze(1).broadcast_to([128, 32, 128]), op=ALU.mult),
                      reads=["dtq", "tri"], writes=["Xb"])
                bY, bF, bB, bS = 0, 1, 2, 3

                def st_a(g):
                    gi = g % 2
                    self.mm(self.bank(bS, 128, gi * 128), [(bT[:, g, :], cT[:, g, :])], ["bTC", "cTC"], f"ps{bS}")
                    for d_ in range(2):
                        bR = 4 + (self.psrr % 4)
                        self.psrr += 1
                        X_ = Xf if d_ == 0 else Xb
                        pairs = [(ones_f, X_[:, g * 4:(g + 1) * 4, :].rearrange("p a c -> p (a c)")),
                                 (self.ident_b[:], C["maskb"][:, d_, :])]
                        self.mm(self.bank(bR), pairs, ["Xf" if d_ == 0 else "Xb", "tri", "ident_b", "maskb"], f"ps{bR}")
                        Gd = G[gi * 2 + d_]
                        kG = f"G{gi * 2 + d_}"
                        for h in range(4):
                            bias = C["biasf"][:, g * 4 + h:g * 4 + h + 1] if d_ == 0 else Ein[:, 1, g * 4 + h:g * 4 + h + 1]
                            kb.op("act", lambda e_, Gd=Gd, h=h, bR=bR, bias=bias: e_.activation(
                                out=Gd[:, h, :], in_=self.bank(bR, 128, h * 128), func=AF.Exp, bias=bias, scale=1.0),
                                reads=[f"ps{bR}", "dtq"], writes=[kG])
                    Gf, Gb = G[gi * 2], G[gi * 2 + 1]
                    kb.op("pool", lambda e_, Gf=Gf, Gb=Gb: e_.tensor_tensor(out=Gf[:], in0=Gf[:], in1=Gb[:], op=ALU.add),
                          reads=[f"G{gi * 2}", f"G{gi * 2 + 1}"], writes=[f"G{gi * 2}"])
                    W_ = Wt[gi]
                    kb.op("dve", lambda e_, W_=W_, Gf=Gf, gi=gi: e_.tensor_tensor(
                        out=W_[:], in0=Gf[:], in1=self.bank(bS, 128, gi * 128).unsqueeze(1).broadcast_to([128, 4, 128]), op=ALU.mult),
                        reads=[f"G{gi * 2}", f"ps{bS}"], writes=[f"Wt{gi}"])

                def st_b(g):
                    gi = g % 2
                    W_ = Wt[gi]
                    for h in range(4):
                        hh = g * 4 + h
                        self.mm(self.bank(bY, 64, (gi * 4 + h) * 64), [(W_[:, h, :], xc[:, hh * 64:(hh + 1) * 64])],
                                [f"Wt{gi}", "xcC"], f"ps{bY}")
                    self.mm(self.bank(bF, 256, gi * 256), [(cT[:, g, :], spf[:, g * 256:(g + 1) * 256])], ["cTC", "spf"], f"ps{bF}")
                    self.mm(self.bank(bB, 256, gi * 256), [(cT[:, g, :], Sbb[:, g * 256:(g + 1) * 256])], ["cTC", "Sbb"], f"ps{bB}")

                def epi(q):
                    hs = slice(q * 8, q * 8 + 8)
                    kb.op("dve", lambda e_, hs=hs: e_.tensor_tensor(out=t1[:].rearrange("p (h c) -> p h c", h=8),
                                                                  in0=self.bank(bF).rearrange("p (h c) -> p h c", h=8),
                                                                  in1=_bc(Eout[:, 2, hs], 64), op=ALU.mult),
                          reads=[f"ps{bF}", "dtq"], writes=["t1"])
                    kb.op("dve", lambda e_, hs=hs: e_.tensor_tensor(out=t2[:].rearrange("p (h c) -> p h c", h=8),
                                                                  in0=self.bank(bB).rearrange("p (h c) -> p h c", h=8),
                                                                  in1=_bc(Eout[:, 3, hs], 64), op=ALU.mult),
                          reads=[f"ps{bB}", "dtq"], writes=["t2"])
                    kb.op("pool", lambda e_: e_.tensor_tensor(out=t1[:], in0=t1[:], in1=t2[:], op=ALU.add), reads=["t1", "t2"], writes=["t1"])
                    kb.op("dve", lambda e_, q=q: e_.tensor_tensor(out=ys[:, q * 512:(q + 1) * 512], in0=self.bank(bY), in1=t1[:], op=ALU.add),
                          reads=[f"ps{bY}", "t1"], writes=["ysC"])

                st_a(0)
                for g in range(8):
                    if g + 1 < 8:
                        st_a(g + 1)
                    st_b(g)
                    if g % 2 == 1:
                        epi(g // 2)
                kb.op("pool", lambda e_: e_.tensor_tensor(out=xd[:].rearrange("p (h c) -> p h c", h=32),
                                                         in0=xc[:].rearrange("p (h c) -> p h c", h=32),
                                                         in1=_bc(C["dsk"][:, :], 64), op=ALU.mult), reads=["xcC", "dsk"], writes=["xd"])
                kb.op("dve", lambda e_: e_.tensor_tensor(out=ys[:], in0=ys[:], in1=xd[:], op=ALU.add), reads=["ysC", "xd"], writes=["ysC"])
                kb.dma("pool", S["YS"][r0:r0 + 128, :], ys[:], reads=["ysC"], writes=["YS"])
                kb.op("dve", lambda e_: e_.tensor_tensor(out=Sb[:].rearrange("p (h c) -> p h c", h=32),
                                                        in0=Sb[:].rearrange("p (h c) -> p h c", h=32),
                                                        in1=_bc(Eout[:, 5, :], 64), op=ALU.mult), reads=["Sb", "dtq"], writes=["Sb"])
                kb.op("dve", lambda e_: e_.tensor_tensor(out=Sb[:], in0=Sb[:], in1=lsb[:], op=ALU.add), reads=["Sb", "lsb"], writes=["Sb"])
                kb.op("act", lambda e_: e_.activation(out=Sbb[:], in_=Sb[:], func=AF.Copy), reads=["Sb"], writes=["Sbb"])

    def na(self, e):
        kb, I, S = self.kb, self.I, self.S
        rows = L // 64
        scale = 128 ** -0.5
        RB = 16
        with contextlib.ExitStack() as es:
            KTh = self.sb(es, [128, L], BF16, "KTh")
            QTh = self.sb(es, [128, L], BF16, "QTh")
            V0 = self.sb(es, [128, NT, 132], BF16, "V0")
            V1 = self.sb(es, [128, NT, 132], BF16, "V1")
            bt = self.sb(es, [128, 8, 256], F32, "bt")
            nam = self.sb(es, [128, 256], F32, "nam")
            NBUF = 4
            s2 = [self.sb(es, [128, 256], F32, f"s2_{i}") for i in range(NBUF)]
            pT = [self.sb(es, [128, 256], BF16, f"pT_{i}") for i in range(NBUF)]
            rinv = [self.sb(es, [64, 1], F32, f"rinv{i}") for i in range(NBUF)]
            obuf = [self.sb(es, [64, RB, 128], F32, f"obuf{i}") for i in range(2)]
            kb.dma("sp", nam[:], I["c_namask"], writes=["nam"])
            kb.op("dve", lambda e_: e_.memset(V0[:, :, 128:129], 1.0), writes=["V0"])
            kb.op("dve", lambda e_: e_.memset(V1[:, :, 128:129], 1.0), writes=["V1"])
            for h in range(16):
                kb.dma("sp", KTh[:], S["KT"][h * 128:(h + 1) * 128, :], reads=["KT"], writes=["KTh"])
                kb.dma("sp", QTh[:], S["QT"][h * 128:(h + 1) * 128, :], reads=["QT"], writes=["QTh"])
                kb.dma("sp", V0[:, :, 0:128], S["V"][:, h * 128:(h + 1) * 128].rearrange("(j p) d -> p j d", p=128), reads=["V"], writes=["V0"])
                kb.dma("sp", V1[:, 0:NT - 1, 0:128], S["V"][64:L - 64, h * 128:(h + 1) * 128].rearrange("(j p) d -> p j d", p=128),
                       reads=["V"], writes=["V1"])
                kb.dma("sp", bt[:], I["ev_rpbt"][e, h].rearrange("a p c -> p a c"), writes=["bt"])
                kb.op("dve", lambda e_: e_.tensor_tensor(out=bt[:], in0=bt[:], in1=nam[:].unsqueeze(1).broadcast_to([128, 8, 256]), op=ALU.add),
                      reads=["bt", "nam"], writes=["bt"])
                def stage_a(r):
                    rs = min(max(r - 4, 0), rows - 8)
                    pat = r - rs
                    ks = rs * 64
                    i2 = r % NBUF
                    b = self.next_bank(0, 4)
                    for c_ in range(4):
                        self.mm(self.bank(b, 64, c_ * 64), [(KTh[:, ks + c_ * 128:ks + (c_ + 1) * 128], QTh[:, r * 64:(r + 1) * 64])],
                                ["KTh", "QTh"], f"ps{b}")
                    kb.op("dve", lambda e_, i2=i2, b=b, pat=pat: e_.scalar_tensor_tensor(
                        out=s2[i2][:], in0=self.bank(b, 256), scalar=scale, in1=bt[:, pat, :], op0=ALU.mult, op1=ALU.add),
                        reads=[f"ps{b}", "bt"], writes=[f"s2_{i2}"])
                    kb.op("act", lambda e_, i2=i2: e_.activation(out=pT[i2][:], in_=s2[i2][:], func=AF.Exp),
                          reads=[f"s2_{i2}"], writes=[f"pT_{i2}"])

                def stage_b(r):
                    rs = min(max(r - 4, 0), rows - 8)
                    ks = rs * 64
                    i2 = r % NBUF
                    b2 = self.next_bank(4, 8)
                    pairs = []
                    for c_ in range(4):
                        tk = ks + c_ * 128
                        vsel = V0[:, tk // 128, 0:129] if tk % 128 == 0 else V1[:, (tk - 64) // 128, 0:129]
                        pairs.append((pT[i2][:, c_ * 64:(c_ + 1) * 64], vsel))
                    self.mm(self.psum[0:64, b2 * 512:b2 * 512 + 129], pairs, [f"pT_{i2}", "V0", "V1"], f"ps{b2}")
                    kb.op("dve", lambda e_, i2=i2, b2=b2: e_.reciprocal(out=rinv[i2][:], in_=self.psum[0:64, b2 * 512 + 128:b2 * 512 + 129]),
                          reads=[f"ps{b2}"], writes=[f"rinv{i2}"])
                    ob = obuf[(r // RB) % 2]
                    kob = f"obuf{(r // RB) % 2}"
                    kb.op("act", lambda e_, ob=ob, r=r, i2=i2, b2=b2: e_.activation(
                        out=ob[:, r % RB, :], in_=self.psum[0:64, b2 * 512:b2 * 512 + 128], func=AF.Copy, scale=rinv[i2][:]),
                        reads=[f"ps{b2}", f"rinv{i2}"], writes=[kob])
                    if r % RB == RB - 1:
                        rr0 = (r - RB + 1) * 64
                        kb.dma("pool", S["YN"][rr0:rr0 + RB * 64, h * 128:(h + 1) * 128].rearrange("(r q) d -> q r d", q=64), ob[:],
                               reads=[kob], writes=["YN"])

                SK = 2
                for r in range(rows + SK):
                    if r < rows:
                        stage_a(r)
                    if r >= SK:
                        stage_b(r - SK)

    def phase_O(self, layer):
        kb, I, S = self.kb, self.I, self.S
        e = layer // 2
        NB = L // 512
        with contextlib.ExitStack() as es:
            wT = self.sb(es, [128, 32, 31], F32, "wT31")
            pm = self.sb(es, [128, 3, 32], F32, "pm31")
            dgs = [self.sb(es, [128, 31, 128], BF16, f"dg{i}") for i in range(2)]
            vins = [self.sb(es, [128, L + 30], BF16, f"vin{i}") for i in range(2)]
            stg = [self.sb(es, [128, 512], F32, f"stgO{i}") for i in range(3)]
            kb.dma("sp", wT[:], I["od_dw_wT"][e].rearrange("(cb p) k -> p cb k", p=128), writes=["wT31"])
            kb.dma("sp", pm[:], I["od_pm"][e].rearrange("a p c -> p a c"), writes=["pm31"])
            si = 0
            for cb in range(32):
                dg, kd = dgs[cb % 2], f"dg{cb % 2}"
                vi, kv = vins[cb % 2], f"vin{cb % 2}"
                for k in range(31):
                    kb.op("dve", lambda e_, k=k, dg=dg, cb=cb: e_.tensor_scalar(
                        out=dg[:, k, :], in0=self.ident_f[:], scalar1=wT[:, cb, k:k + 1], scalar2=None, op0=ALU.mult),
                        reads=["wT31", "ident_f"], writes=[kd])
                kb.dma("sp", vi[:], S["VT"][cb * 128:(cb + 1) * 128, :], reads=["VT"], writes=[kv])
                for tb in range(NB):
                    b = self.next_bank()
                    self.mm(self.bank(b), [(dg[:, k, :], vi[:, tb * 512 + k:tb * 512 + k + 512]) for k in range(31)],
                            [kd, kv], f"ps{b}")
                    s_, ks_ = stg[si % 3], f"stgO{si % 3}"
                    si += 1
                    kb.op("act", lambda e_, s_=s_, b=b, cb=cb: e_.activation(out=s_[:], in_=self.bank(b), func=AF.Identity,
                                                                         bias=pm[:, 0, cb:cb + 1], scale=1.0),
                          reads=[f"ps{b}", "pm31"], writes=[ks_])
                    kb.dma("pool", S["VCT"][cb * 128:(cb + 1) * 128, tb * 512:(tb + 1) * 512], s_[:], reads=[ks_], writes=["VCT"])
        kb.barrier()
        with contextlib.ExitStack() as es:
            pm = self.sb(es, [128, 3, 32], F32, "pm31b")
            ones_f = self.sb(es, [128, 128], F32, "ones_f")
            vblk = self.sb(es, [128, 32, 512], F32, "vblk")
            sq = [self.sb(es, [128, 512], F32, f"sq{i}") for i in range(2)]
            sgt = [self.sb(es, [128, 512], F32, f"sgt{i}") for i in range(3)]
            tt_ = [self.sb(es, [128, 512], F32, f"tO{i}") for i in range(2)]
            yb = [self.sb(es, [128, 512], BF16, f"ybO{i}") for i in range(3)]
            mean = self.sb(es, [128, 512], F32, "meanO")
            rstd = self.sb(es, [128, 512], F32, "rstdO")
            tmp = self.sb(es, [128, 512], F32, "tmpO")
            kb.dma("sp", pm[:], I["od_pm"][e].rearrange("a p c -> p a c"), writes=["pm31b"])
            kb.dma("sp", ones_f[:], I["c_tri"][2], writes=["ones_f"])
            for tb in range(NB):
                c0 = tb * 512
                kb.dma("sp", vblk[:], S["VCT"][:, c0:c0 + 512].rearrange("(cb p) t -> p cb t", p=128), reads=["VCT"], writes=["vblk"])
                bA, bB = 0, 1
                for cb in range(32):
                    q_, kq = sq[cb % 2], f"sq{cb % 2}"
                    kb.op("act", lambda e_, q_=q_, cb=cb: e_.activation(out=q_[:], in_=vblk[:, cb, :], func=AF.Square), reads=["vblk"], writes=[kq])
                    kb.op("pe", lambda e_, cb=cb: e_.matmul(self.bank(bA), lhsT=ones_f[:], rhs=vblk[:, cb, :], start=(cb == 0), stop=(cb == 31)),
                          reads=["vblk", "ones_f"], writes=[f"ps{bA}"], signal=(cb == 31))
                    kb.op("pe", lambda e_, cb=cb, q_=q_: e_.matmul(self.bank(bB), lhsT=ones_f[:], rhs=q_[:], start=(cb == 0), stop=(cb == 31)),
                          reads=[kq, "ones_f"], writes=[f"ps{bB}"], signal=True)
                kb.op("dve", lambda e_: e_.tensor_scalar(out=mean[:], in0=self.bank(bA), scalar1=1.0 / 4096, scalar2=None, op0=ALU.mult),
                      reads=[f"ps{bA}"], writes=["meanO"])
                kb.op("dve", lambda e_: e_.tensor_tensor(out=tmp[:], in0=mean[:], in1=mean[:], op=ALU.mult), reads=["meanO"], writes=["tmpO"])
                kb.op("dve", lambda e_: e_.scalar_tensor_tensor(out=tmp[:], in0=self.bank(bB), scalar=1.0 / 4096, in1=tmp[:],
                                                                op0=ALU.mult, op1=ALU.subtract), reads=[f"ps{bB}", "tmpO"], writes=["tmpO"])
                kb.op("act", lambda e_: e_.activation(out=tmp[:], in_=tmp[:], func=AF.Sqrt, bias=EPS, scale=1.0), reads=["tmpO"], writes=["tmpO"])
                kb.op("dve", lambda e_: e_.reciprocal(out=rstd[:], in_=tmp[:]), reads=["tmpO"], writes=["rstdO"])
                for cb in range(32):
                    g_, kg = sgt[cb % 3], f"sgt{cb % 3}"
                    t_, kt = tt_[cb % 2], f"tO{cb % 2}"
                    y_, ky = yb[cb % 3], f"ybO{cb % 3}"
                    kb.dma("sp", g_[:], S["SG2T"][cb * 128:(cb + 1) * 128, c0:c0 + 512], reads=["SG2T"], writes=[kg])
                    kb.op("dve", lambda e_, t_=t_, cb=cb: e_.tensor_tensor(out=t_[:], in0=vblk[:, cb, :], in1=mean[:], op=ALU.subtract),
                          reads=["vblk", "meanO"], writes=[kt])
                    kb.op("pool", lambda e_, t_=t_: e_.tensor_tensor(out=t_[:], in0=t_[:], in1=rstd[:], op=ALU.mult), reads=[kt, "rstdO"], writes=[kt])
                    kb.op("act", lambda e_, t_=t_, cb=cb: e_.activation(out=t_[:], in_=t_[:], func=AF.Silu, bias=pm[:, 2, cb:cb + 1],
                                                                     scale=pm[:, 1, cb:cb + 1]), reads=[kt, "pm31b"], writes=[kt])
                    kb.op("dve", lambda e_, t_=t_, g_=g_, y_=y_: e_.tensor_tensor(out=y_[:], in0=t_[:], in1=g_[:], op=ALU.mult),
                          reads=[kt, kg], writes=[ky])
                    kb.dma("pool", S["YOT"][cb * 128:(cb + 1) * 128, c0:c0 + 512], y_[:], reads=[ky], writes=["YOT"])


def _consts():
    c = {}
    c["c_ident"] = np.eye(128, dtype=np.float32)
    t = np.arange(128)
    tri = np.zeros((3, 128, 128), np.float32)
    tri[0] = (t[:, None] <= t[None, :])
    tri[1] = (t[:, None] < t[None, :])
    tri[2] = 1.0
    c["c_tri"] = tri
    m = np.zeros((2, 128, 512), np.float32)
    mf = np.where(t[:, None] <= t[None, :], 0.0, NEG)
    mb = np.where(t[:, None] >= t[None, :], 0.0, NEG)
    m[0] = np.tile(mf, (1, 4))
    m[1] = np.tile(mb, (1, 4))
    c["c_mask"] = m.astype(np.float32)
    cols = np.arange(64)
    cs = np.clip(cols - 8, 0, 48)
    valid = (cols[None, :] >= cs[:, None]) & (cols[None, :] < cs[:, None] + 16)
    mk = np.where(valid.T, 0.0, NEG).astype(np.float32)
    mk2 = np.concatenate([mk, mk], 0)
    c["c_namask"] = np.tile(mk2, (1, 4)).astype(np.float32)
    return c


def _rpb_table(rpb):
    cols = np.arange(64)
    coff = np.clip(cols[None, :] - cols[:, None], -15, 15) + 15
    out = np.zeros((2, 16, 8, 128, 256), np.float32)
    for pat in range(8):
        delta = -pat
        for c_ in range(4):
            for il in range(2):
                i = 2 * c_ + il
                roff = delta + i + 7
                g = rpb[:, :, roff, :][:, :, coff]
                out[:, :, pat, il * 64:(il + 1) * 64, c_ * 64:(c_ + 1) * 64] = np.transpose(g, (0, 1, 3, 2))
    return out


_NC_CACHE = {}


def kernel(x, p, ev_norm_w, ev_w_in, ev_conv_w, ev_conv_b, ev_dt_bias_f, ev_dt_bias_b,
           ev_a_log_f, ev_a_log_b, ev_d_skip, ev_gnorm_w, ev_rpb, ev_w_out,
           od_norm_w, od_w_in, od_dw_w, od_dw_b, od_ln_w, od_ln_b, od_w_out,
           ple_norm_w, ple_w_gate, ple_w_proj, final_norm_w):
    f = lambda a: np.ascontiguousarray(np.asarray(a, dtype=np.float32))
    if "nc" not in _NC_CACHE:
        _NC_CACHE["nc"] = Prog().build()
    nc = _NC_CACHE["nc"]
    shared = {
        "ev_norm_w": f(ev_norm_w), "ev_w_in": f(ev_w_in), "ev_conv_wT": f(np.transpose(f(ev_conv_w), (0, 2, 1))),
        "ev_conv_b": f(ev_conv_b), "ev_conv_bpm": f(np.transpose(f(ev_conv_b).reshape(2, 32, 128), (0, 2, 1))),
        "ev_dt_bias": f(np.concatenate([ev_dt_bias_f, ev_dt_bias_b], 1)),
        "ev_a_log": f(np.concatenate([ev_a_log_f, ev_a_log_b], 1)),
        "ev_d_skip": f(ev_d_skip), "ev_gnorm_w": f(ev_gnorm_w), "ev_rpbt": _rpb_table(f(ev_rpb)),
        "ev_w_out": f(ev_w_out), "od_norm_w": f(od_norm_w), "od_w_in": f(od_w_in), "od_dw_wT": f(np.transpose(f(od_dw_w), (0, 2, 1))),
        "od_pm": f(np.transpose(np.stack([f(od_dw_b), f(od_ln_w), f(od_ln_b)], 1).reshape(2, 3, 32, 128), (0, 1, 3, 2))),
        "od_w_out": f(od_w_out),
        "ple_norm_w": f(ple_norm_w), "ple_w_gate": f(ple_w_gate), "ple_w_proj": f(ple_w_proj),
        "final_norm_w": f(final_norm_w).reshape(1, D),
    }
    shared.update(_consts())
    x = f(x)
    p = f(p)
    in_maps = []
    for c in range(NCORES):
        b = c % 2
        m = dict(shared)
        m["x"] = np.ascontiguousarray(x[b, :L])
        m["p"] = np.ascontiguousarray(p[:, b, :L])
        in_maps.append(m)
    res = run_bass_kernel_spmd(nc, in_maps, core_ids=list(range(NCORES)))
    kernel.last = res
    return np.stack([res.results[0]["out"], res.results[1]["out"]], 0)
```

```python
import contextlib
import numpy as np
import ml_dtypes
import concourse.bass as bass
import concourse.mybir as mybir
from concourse.bass_utils import run_bass_kernel_spmd

F32 = mybir.dt.float32
BF16 = mybir.dt.bfloat16
AF = mybir.ActivationFunctionType
ALU = mybir.AluOpType
AX = mybir.AxisListType

L = 8192
D = 2048
TS = 512
TPS = TS // 128
NST = L // TS
NT = L // 128
EVEN_IN = 14400
ODD_IN = 12288
NEG = -30000.0
EPS = 1e-6
DEPTH = 4
EV_COLS = ([c for c in range(0, 2048, 512)] + [2048 + c for c in range(0, 4096, 512)] +
           [6208 + c for c in range(0, 8192, 512)])
NCORES = 2

DEBUG = {}
STOP_AFTER = None
NST_RUN = NST
PHASES = None
E_PARTS = "ACN"


def _set_L(n):
    global L, NST, NT, NST_RUN
    L = n
    NST = L // TS
    NT = L // 128
    NST_RUN = NST


class _Eng:
    def __init__(self, kb, name, h):
        self.kb = kb
        self.name = name
        self.h = h
        self.cnt = 0
        self.sem_ids = []
        self.waited = {}
        self.dsl = []
        self.drr = 0

    def event_for(self, cnt):
        cap = 30000
        idx = (cnt - 1) // cap
        while len(self.sem_ids) <= idx:
            self.sem_ids.append(self.kb.new_sem(f"{self.name}_c{len(self.sem_ids)}"))
        return (self.sem_ids[idx], (cnt - 1) % cap + 1)


class KB:
    NDSEM = 10

    def __init__(self, nc, es):
        self.nc = nc
        self.es = es
        self.sems = []
        self.state = {}
        self.eng = {}
        for name, h in (("pe", nc.tensor), ("act", nc.scalar), ("dve", nc.vector),
                        ("pool", nc.gpsimd), ("sp", nc.sync)):
            self.eng[name] = _Eng(self, name, h)
        self.ninst = 0

    def new_sem(self, name):
        s = self.es.enter_context(self.nc.semaphore(name))
        self.sems.append(s)
        return len(self.sems) - 1

    def _wait(self, e, ev):
        sid, val = ev
        if e.name == "pe" and sid in e.sem_ids:
            return
        if e.waited.get(sid, 0) < val:
            e.h.wait_ge(self.sems[sid], val)
            e.waited[sid] = val

    def _gather(self, reads, writes):
        evs = []
        for k in reads:
            st = self.state.get(k)
            if st is not None and st[0] is not None:
                evs.append(st[0])
        for k in writes:
            st = self.state.get(k)
            if st is not None:
                if st[0] is not None:
                    evs.append(st[0])
                evs.extend(st[1].items())
        return evs

    def _record(self, ev, reads, writes):
        for k in reads:
            st = self.state.setdefault(k, [None, {}])
            if st[1].get(ev[0], 0) < ev[1]:
                st[1][ev[0]] = ev[1]
        for k in writes:
            self.state[k] = [ev, {}]

    def op(self, engname, fn, reads=(), writes=(), signal=True):
        e = self.eng[engname]
        for ev in self._gather(reads, writes):
            self._wait(e, ev)
        inst = fn(e.h)
        self.ninst += 1
        if signal:
            e.cnt += 1
            ev = e.event_for(e.cnt)
            inst.then_inc(self.sems[ev[0]], 1)
        else:
            ev = e.event_for(e.cnt + 1)
        self._record(ev, reads, writes)
        return inst

    def dma(self, q, out, in_, reads=(), writes=()):
        e = self.eng[q]
        for ev in self._gather(reads, writes):
            self._wait(e, ev)
        if not e.dsl:
            e.dsl = [[self.new_sem(f"{q}_d{i}"), 0] for i in range(self.NDSEM)]
        slot = e.dsl[e.drr]
        e.drr = (e.drr + 1) % len(e.dsl)
        if slot[1] > 0:
            self._wait(e, (slot[0], slot[1] * 16))
        inst = e.h.dma_start(out=out, in_=in_)
        inst.then_inc(self.sems[slot[0]], 16)
        slot[1] += 1
        self.ninst += 1
        ev = (slot[0], slot[1] * 16)
        self._record(ev, reads, writes)
        return ev

    def barrier(self):
        evs = []
        for e in self.eng.values():
            if e.cnt > 0:
                evs.append(e.event_for(e.cnt))
            for slot in e.dsl:
                if slot[1] > 0:
                    evs.append((slot[0], slot[1] * 16))
        for e in self.eng.values():
            for ev in evs:
                sid, val = ev
                if e.waited.get(sid, 0) < val:
                    e.h.wait_ge(self.sems[sid], val)
                    e.waited[sid] = val
        self.state = {}


def _bc(ap2d, n):
    return ap2d.unsqueeze(2).broadcast_to([ap2d.shape[0], ap2d.shape[1], n])


class Prog:
    def __init__(self):
        self.nc = bass.Bass("TRN2", target_bir_lowering=False)
        self.nsb = 0

    def sb(self, es, shape, dt, name=None):
        self.nsb += 1
        return es.enter_context(self.nc.sbuf_tensor(f"{name or 'sb'}_{self.nsb}", list(shape), dt))

    def din(self, name, shape, dt=F32):
        return self.nc.dram_tensor(name, list(shape), dt, kind="ExternalInput").ap()

    def dscr(self, name, shape, dt):
        return self.nc.dram_tensor(name, list(shape), dt).ap()

    def build(self):
        nc = self.nc
        I = {}
        I["x"] = self.din("x", [L, D])
        I["p"] = self.din("p", [DEPTH, L, 256])
        I["ev_norm_w"] = self.din("ev_norm_w", [2, D])
        I["ev_w_in"] = self.din("ev_w_in", [2, D, EVEN_IN])
        I["ev_conv_wT"] = self.din("ev_conv_wT", [2, 4096, 5])
        I["ev_conv_b"] = self.din("ev_conv_b", [2, 4096])
        I["ev_conv_bpm"] = self.din("ev_conv_bpm", [2, 128, 32])
        I["ev_dt_bias"] = self.din("ev_dt_bias", [2, 64])
        I["ev_a_log"] = self.din("ev_a_log", [2, 64])
        I["ev_d_skip"] = self.din("ev_d_skip", [2, 32])
        I["ev_gnorm_w"] = self.din("ev_gnorm_w", [2, D])
        I["ev_rpbt"] = self.din("ev_rpbt", [2, 16, 8, 128, 256])
        I["ev_w_out"] = self.din("ev_w_out", [2, 4096, D])
        I["od_norm_w"] = self.din("od_norm_w", [2, D])
        I["od_w_in"] = self.din("od_w_in", [2, D, ODD_IN])
        I["od_dw_wT"] = self.din("od_dw_wT", [2, 4096, 31])
        I["od_pm"] = self.din("od_pm", [2, 3, 128, 32])
        I["od_w_out"] = self.din("od_w_out", [2, 4096, D])
        I["ple_norm_w"] = self.din("ple_norm_w", [DEPTH, D])
        I["ple_w_gate"] = self.din("ple_w_gate", [DEPTH, D, D])
        I["ple_w_proj"] = self.din("ple_w_proj", [DEPTH, 256, D])
        I["final_norm_w"] = self.din("final_norm_w", [1, D])
        I["c_ident"] = self.din("c_ident", [128, 128])
        I["c_tri"] = self.din("c_tri", [3, 128, 128])
        I["c_mask"] = self.din("c_mask", [2, 128, 512])
        I["c_namask"] = self.din("c_namask", [128, 256])
        self.I = I
        self.out = nc.dram_tensor("out", [L, D], F32, kind="ExternalOutput").ap()

        S = {}
        S["H"] = self.dscr("H", [L, D], F32)
        S["wk_ev_in"] = self.dscr("wk_ev_in", [2, 29, 128, 16, 512], BF16)
        S["wk_od_in"] = self.dscr("wk_od_in", [2, 24, 128, 16, 512], BF16)
        S["wk_ev_out"] = self.dscr("wk_ev_out", [2, 8, 128, 16, 512], BF16)
        S["wk_od_out"] = self.dscr("wk_od_out", [2, 8, 128, 16, 512], BF16)
        S["wk_gate"] = self.dscr("wk_gate", [DEPTH, 4, 128, 16, 512], BF16)
        S["wk_proj"] = self.dscr("wk_proj", [DEPTH, 4, 128, 2, 512], BF16)
        S["ATD"] = self.dscr("ATD", [L // TS, 128, 16, TS], BF16)
        S["SZ"] = self.dscr("SZ", [L, D], F32)
        S["XBCT"] = self.dscr("XBCT", [4096, L + 4], BF16)
        S["DT"] = self.dscr("DT", [L, 64], F32)
        S["QT"] = self.dscr("QT", [D, L], BF16)
        S["KT"] = self.dscr("KT", [D, L], BF16)
        S["V"] = self.dscr("V", [L, D], BF16)
        S["SG"] = self.dscr("SG", [L, D], F32)
        S["YS"] = self.dscr("YS", [L, D], F32)
        S["YN"] = self.dscr("YN", [L, D], F32)
        S["XC"] = self.dscr("XC", [L, D], BF16)
        S["BTOK"] = self.dscr("BTOK", [L, 1024], BF16)
        S["BT"] = self.dscr("BT", [1024, L], BF16)
        S["CT"] = self.dscr("CT", [1024, L], BF16)
        S["SPF"] = self.dscr("SPF", [NT, 128, D], BF16)
        S["LSB"] = self.dscr("LSB", [NT, 128, D], F32)
        S["VT"] = self.dscr("VT", [4096, L + 30], BF16)
        S["VCT"] = self.dscr("VCT", [4096, L], F32)
        S["SG2T"] = self.dscr("SG2T", [4096, L], F32)
        S["YOT"] = self.dscr("YOT", [4096, L], BF16)
        self.S = S

        self.dbg = {}
        for name, (shape, _fn) in DEBUG.items():
            self.dbg[name] = nc.dram_tensor("dbg_" + name, list(shape), F32, kind="ExternalOutput").ap()

        with contextlib.ExitStack() as es:
            self.kb = KB(nc, es)
            kb = self.kb
            self.psum = es.enter_context(nc.psum_tensor("psum", [128, 4096], F32))
            self.psum_bf = self.psum[:].bitcast(BF16)
            self.ident_f = self.sb(es, [128, 128], F32, "ident_f")
            self.ident_b = self.sb(es, [128, 128], BF16, "ident_b")
            self.ones_b = self.sb(es, [128, 128], BF16, "ones_b")
            self.zero_b = self.sb(es, [128, 32], BF16, "zero_b")
            kb.dma("sp", self.ident_f[:], I["c_ident"][:, :], writes=["ident_f"])
            kb.op("dve", lambda e: e.tensor_copy(out=self.ident_b[:], in_=self.ident_f[:]),
                  reads=["ident_f"], writes=["ident_b"])
            kb.op("dve", lambda e: e.memset(self.ones_b[:], 1.0), writes=["ones_b"])
            kb.op("dve", lambda e: e.memset(self.zero_b[:], 0.0), writes=["zero_b"])
            self.psrr = 0

            self.phase_cast()
            phases = [("P", None, 0), ("I", 0), ("E", 0), ("P", 0, 1), ("I", 1), ("O", 1), ("P", 1, 2), ("I", 2), ("E", 2),
                      ("P", 2, 3), ("I", 3), ("O", 3), ("P", 3, None)]
            if PHASES is not None:
                phases = PHASES
            for ph in phases:
                tag = "_".join(str(a) for a in ph)
                if ph[0] == "P":
                    self.phase_P(ph[1], ph[2])
                elif ph[0] == "E":
                    self.phase_E(ph[1])
                elif ph[0] == "I":
                    self.phase_I(ph[1])
                else:
                    self.phase_O(ph[1])
                kb.barrier()
                if STOP_AFTER == tag:
                    break
            self.dump_debug()
            kb.barrier()
        return nc

    def bank(self, b, n=512, off=0):
        return self.psum[:, b * 512 + off: b * 512 + off + n]

    def bank_bf(self, b, n=1024, off=0):
        return self.psum_bf[:, b * 1024 + off: b * 1024 + off + n]

    def next_bank(self, lo=0, hi=8):
        b = lo + self.psrr % (hi - lo)
        self.psrr += 1
        return b

    def mm(self, out_ap, pairs, reads, wkey, transpose=False):
        kb = self.kb
        n = len(pairs)
        for i, (l, r) in enumerate(pairs):
            kb.op("pe", lambda e, l=l, r=r, i=i: e.matmul(out_ap, lhsT=l, rhs=r, start=(i == 0), stop=(i == n - 1)),
                  reads=reads, writes=[wkey], signal=(i == n - 1))

    def dump_debug(self):
        kb = self.kb
        for name, ap in self.dbg.items():
            fn = DEBUG[name][1]
            if fn is None:
                continue
            kb.dma("pool", ap, fn(self), writes=["dbg_" + name])

    def phase_cast(self):
        kb, I, S = self.kb, self.I, self.S
        def blk(dst4, src2d, k0, c0, nkc, ncols):
            kb.dma("pool", dst4[:, 0:nkc, 0:ncols],
                   src2d[k0:k0 + nkc * 128, c0:c0 + ncols].rearrange("(kc p) c -> p kc c", p=128))
        for e in range(2):
            for i, c0 in enumerate(EV_COLS):
                blk(S["wk_ev_in"][e, i], I["ev_w_in"][e], 0, c0, 16, 512)
            blk(S["wk_ev_in"][e, 28], I["ev_w_in"][e], 0, 6144, 16, 64)
            for i in range(24):
                blk(S["wk_od_in"][e, i], I["od_w_in"][e], 0, i * 512, 16, 512)
            for nb in range(4):
                for hf in range(2):
                    blk(S["wk_ev_out"][e, nb * 2 + hf], I["ev_w_out"][e], hf * 2048, nb * 512, 16, 512)
                    blk(S["wk_od_out"][e, nb * 2 + hf], I["od_w_out"][e], hf * 2048, nb * 512, 16, 512)
        for l in range(DEPTH):
            for nb in range(4):
                blk(S["wk_gate"][l, nb], I["ple_w_gate"][l], 0, nb * 512, 16, 512)
                blk(S["wk_proj"][l, nb], I["ple_w_proj"][l], 0, nb * 512, 2, 512)
        zt = self.zero_b
        for cb in range(32):
            kb.dma("sp", S["XBCT"][cb * 128:(cb + 1) * 128, 0:2], zt[:, 0:2], reads=["zero_b"])
            kb.dma("sp", S["XBCT"][cb * 128:(cb + 1) * 128, L + 2:L + 4], zt[:, 0:2], reads=["zero_b"])
            kb.dma("sp", S["VT"][cb * 128:(cb + 1) * 128, 0:15], zt[:, 0:15], reads=["zero_b"])
            kb.dma("sp", S["VT"][cb * 128:(cb + 1) * 128, L + 15:L + 30], zt[:, 0:15], reads=["zero_b"])

    def norm_T(self, es_tiles, Hs, wrep, AT, sq_junk, hn_tiles, stat, hkey):
        kb = self.kb
        ss = stat
        for tt in range(TPS):
            kb.op("act", lambda e, tt=tt: e.activation(out=sq_junk[:], in_=Hs[:, tt, :], func=AF.Square,
                                                        accum_out=ss[:, tt:tt + 1]),
                  reads=[hkey], writes=["sq_junk", "stat"])
        kb.op("act", lambda e: e.activation(out=ss[:, TPS:2 * TPS], in_=ss[:, 0:TPS], func=AF.Sqrt,
                                            bias=EPS, scale=1.0 / D), reads=["stat"], writes=["stat"])
        kb.op("dve", lambda e: e.reciprocal(out=ss[:, 2 * TPS:3 * TPS], in_=ss[:, TPS:2 * TPS]),
              reads=["stat"], writes=["stat"])
        for tt in range(TPS):
            hn = hn_tiles[tt % len(hn_tiles)]
            hk = f"hn{tt % len(hn_tiles)}"
            kb.op("dve", lambda e, tt=tt, hn=hn: e.scalar_tensor_tensor(
                out=hn[:], in0=Hs[:, tt, :], scalar=ss[:, 2 * TPS + tt:2 * TPS + tt + 1], in1=wrep[:],
                op0=ALU.mult, op1=ALU.mult), reads=[hkey, "stat", "wrep"], writes=[hk])
            self.transpose_into(hn, 16, AT, tt, hk, "AT")

    def transpose_into(self, src, nkc, dstT, tt, skey, dkey):
        kb = self.kb
        for g0 in range(0, nkc, 8):
            b = self.next_bank()
            n = min(8, nkc - g0)
            for j in range(n):
                kc = g0 + j
                kb.op("pe", lambda e, kc=kc, j=j, b=b: e.transpose(self.bank_bf(b, 128, j * 128),
                                                                src[:, kc * 128:(kc + 1) * 128], self.ident_b[:]),
                      reads=[skey, "ident_b"], writes=[f"ps{b}"], signal=(j == n - 1))
            eng = "act" if (self.psrr % 2 == 0) else "dve"
            dst = dstT[:, g0:g0 + n, tt * 128:(tt + 1) * 128]
            srcp = self.bank_bf(b, n * 128).rearrange("p (a c) -> p a c", a=n)
            if eng == "act":
                kb.op("act", lambda e, dst=dst, srcp=srcp: e.activation(out=dst, in_=srcp, func=AF.Copy),
                      reads=[f"ps{b}"], writes=[dkey])
            else:
                kb.op("dve", lambda e, dst=dst, srcp=srcp: e.tensor_copy(out=dst, in_=srcp),
                      reads=[f"ps{b}"], writes=[dkey])

    def phase_P(self, lp, ln):
        kb, I, S, nc = self.kb, self.I, self.S, self.nc
        with contextlib.ExitStack() as es:
            Hs = self.sb(es, [128, TPS, D], F32, "Hs")
            AT = self.sb(es, [128, 16, TS], BF16, "AT")
            hn_tiles = [self.sb(es, [128, D], BF16, f"hn{i}") for i in range(2)]
            sq_junk = self.sb(es, [128, D], BF16, "sq_junk")
            stat = self.sb(es, [128, 3 * TPS], F32, "stat")
            wrep = self.sb(es, [128, D], F32, "wrep")
            wbuf = [self.sb(es, [128, 16, 512], BF16, f"wbuf{i}") for i in range(3)]
            stg = [self.sb(es, [128, 512], F32, f"stg{i}") for i in range(3)]
            stgb = [self.sb(es, [128, 512], BF16, f"stgb{i}") for i in range(3)]
            self.wrr = 0
            self.srr = 0
            if lp is not None:
                yT = self.sb(es, [128, 32, TS], BF16, "yT")
                ybf = self.sb(es, [128, 4096], BF16, "ybf")
                ld = [self.sb(es, [128, 1024], F32, f"ld{i}") for i in range(4)]
                gst = self.sb(es, [128, 16], F32, "gst")
                grep = self.sb(es, [128, D], F32, "grep")
                pf = self.sb(es, [128, TPS, 256], F32, "pf")
                pb = self.sb(es, [128, TPS, 256], BF16, "pb")
                pT = self.sb(es, [128, 2, TS], BF16, "pT")
                wpj = [self.sb(es, [128, 2, 512], BF16, f"wpj{i}") for i in range(2)]
                sig = [self.sb(es, [128, 512], F32, f"sig{i}") for i in range(2)]
                if lp % 2 == 0:
                    kb.dma("sp", grep[:], I["ev_gnorm_w"][lp // 2:lp // 2 + 1, :].partition_broadcast(128),
                           writes=["grep"])

            def load_w(blk4, ncols=512, nkc=16):
                i = self.wrr % len(wbuf)
                self.wrr += 1
                t = wbuf[i]
                kb.dma("sp", t[:, 0:nkc, 0:ncols], blk4[:, 0:nkc, 0:ncols], writes=[f"wbuf{i}"])
                return t, f"wbuf{i}"

            def next_stg(bf=False):
                i = self.srr % 3
                self.srr += 1
                return (stgb[i], f"stgb{i}") if bf else (stg[i], f"stg{i}")

            for st in range(NST_RUN):
                t0 = st * TS
                srcH = I["x"] if lp in (None, 0) else S["H"]
                kb.dma("sp", Hs[:], srcH[t0:t0 + TS, :].rearrange("(a p) d -> p a d", p=128), writes=["Hs"])
                if lp is not None:
                    e_idx = lp // 2
                    if lp % 2 == 1:
                        kb.dma("sp", yT[:], S["YOT"][:, t0:t0 + TS].rearrange("(kc p) t -> p kc t", p=128), writes=["yT"])
                    for tt in range(TPS if lp % 2 == 0 else 0):
                        r0 = t0 + tt * 128
                        if lp % 2 == 0:
                            for hf in range(2):
                                c0 = hf * 1024
                                a, b_, c_, d_ = ld[0:4]
                                ka, kb_, kc_, kd_ = [f"ld{j}" for j in range(4)]
                                kb.dma("sp", a[:], S["YS"][r0:r0 + 128, c0:c0 + 1024], writes=[ka])
                                kb.dma("sp", b_[:], S["SZ"][r0:r0 + 128, c0:c0 + 1024], writes=[kb_])
                                kb.dma("sp", c_[:], S["YN"][r0:r0 + 128, c0:c0 + 1024], writes=[kc_])
                                kb.dma("sp", d_[:], S["SG"][r0:r0 + 128, c0:c0 + 1024], writes=[kd_])
                                kb.op("dve", lambda e, a=a, b_=b_: e.tensor_tensor(out=a[:], in0=a[:], in1=b_[:], op=ALU.mult),
                                      reads=[ka, kb_], writes=[ka])
                                kb.op("pool", lambda e, a=a, b_=b_: e.tensor_tensor(out=b_[:], in0=a[:], in1=a[:], op=ALU.mult),
                                      reads=[ka], writes=[kb_])
                                kb.op("dve", lambda e, b_=b_, hf=hf: e.tensor_reduce(
                                    out=gst[:, hf * 4:hf * 4 + 4], in_=b_[:].rearrange("p (g c) -> p g c", g=4),
                                    axis=AX.X, op=ALU.add), reads=[kb_], writes=["gst"])
                                kb.op("act", lambda e, hf=hf: e.activation(out=gst[:, 8 + hf * 4:8 + hf * 4 + 4],
                                                                          in_=gst[:, hf * 4:hf * 4 + 4], func=AF.Sqrt,
                                                                          bias=EPS, scale=1.0 / 256), reads=["gst"], writes=["gst"])
                                kb.op("dve", lambda e, hf=hf: e.reciprocal(out=gst[:, hf * 4:hf * 4 + 4],
                                                                          in_=gst[:, 8 + hf * 4:8 + hf * 4 + 4]),
                                      reads=["gst"], writes=["gst"])
                                kb.op("dve", lambda e, a=a, hf=hf: e.tensor_tensor(
                                    out=a[:].rearrange("p (g c) -> p g c", g=4), in0=a[:].rearrange("p (g c) -> p g c", g=4),
                                    in1=_bc(gst[:, hf * 4:hf * 4 + 4], 256), op=ALU.mult), reads=[ka, "gst"], writes=[ka])
                                kb.op("dve", lambda e, a=a, c0=c0: e.tensor_tensor(out=ybf[:, c0:c0 + 1024], in0=a[:],
                                                                                 in1=grep[:, c0:c0 + 1024], op=ALU.mult),
                                      reads=[ka, "grep"], writes=["ybf"])
                                kb.op("pool", lambda e, c_=c_, d_=d_, c0=c0: e.tensor_tensor(
                                    out=ybf[:, 2048 + c0:2048 + c0 + 1024], in0=c_[:], in1=d_[:], op=ALU.mult),
                                    reads=[kc_, kd_], writes=["ybf"])
                        self.transpose_into(ybf, 32, yT, tt, "ybf", "yT")
                    wo = S["wk_ev_out"][e_idx] if lp % 2 == 0 else S["wk_od_out"][e_idx]
                    for nb in range(4):
                        wa, kwa = load_w(wo[nb * 2])
                        wb_, kwb = load_w(wo[nb * 2 + 1])
                        for tt in range(TPS):
                            b = self.next_bank()
                            pairs = [(yT[:, kc, tt * 128:(tt + 1) * 128], (wa if kc < 16 else wb_)[:, kc % 16, :])
                                     for kc in range(32)]
                            self.mm(self.bank(b), pairs, ["yT", kwa, kwb], f"ps{b}")
                            kb.op("dve", lambda e, tt=tt, nb=nb, b=b: e.tensor_tensor(
                                out=Hs[:, tt, nb * 512:(nb + 1) * 512], in0=self.bank(b),
                                in1=Hs[:, tt, nb * 512:(nb + 1) * 512], op=ALU.add),
                                reads=[f"ps{b}", "Hs"], writes=["Hs"])
                    if "hmix" in self.dbg and lp == 0:
                        kb.dma("pool", self.dbg["hmix"][t0:t0 + TS, :].rearrange("(a p) d -> p a d", p=128), Hs[:],
                               reads=["Hs"], writes=["dbg_hmix"])
                    kb.dma("sp", wrep[:], I["ple_norm_w"][lp:lp + 1, :].partition_broadcast(128), writes=["wrep"])
                    self.norm_T(es, Hs, wrep, AT, sq_junk, hn_tiles, stat, "Hs")
                    kb.dma("sp", pf[:], I["p"][lp, t0:t0 + TS, :].rearrange("(a p) d -> p a d", p=128), writes=["pf"])
                    kb.op("pool", lambda e: e.tensor_copy(out=pb[:], in_=pf[:]), reads=["pf"], writes=["pb"])
                    for tt in range(TPS):
                        b = self.next_bank()
                        for j in range(2):
                            kb.op("pe", lambda e, tt=tt, j=j, b=b: e.transpose(self.bank_bf(b, 128, j * 128),
                                                                           pb[:, tt, j * 128:(j + 1) * 128], self.ident_b[:]),
                                  reads=["pb", "ident_b"], writes=[f"ps{b}"], signal=(j == 1))
                        kb.op("act", lambda e, tt=tt, b=b: e.activation(
                            out=pT[:, :, tt * 128:(tt + 1) * 128],
                            in_=self.bank_bf(b, 256).rearrange("p (a c) -> p a c", a=2), func=AF.Copy),
                            reads=[f"ps{b}"], writes=["pT"])
                    for nb in range(4):
                        wg, kwg = load_w(S["wk_gate"][lp, nb])
                        wp = wpj[nb % 2]
                        kwp = f"wpj{nb % 2}"
                        kb.dma("sp", wp[:], S["wk_proj"][lp, nb], writes=[kwp])
                        for tt in range(TPS):
                            b = self.next_bank()
                            b2 = self.next_bank()
                            self.mm(self.bank(b), [(AT[:, kc, tt * 128:(tt + 1) * 128], wg[:, kc, :]) for kc in range(16)],
                                    ["AT", kwg], f"ps{b}")
                            self.mm(self.bank(b2), [(pT[:, kc, tt * 128:(tt + 1) * 128], wp[:, kc, :]) for kc in range(2)],
                                    ["pT", kwp], f"ps{b2}")
                            sg_ = sig[tt % 2]
                            ks = f"sig{tt % 2}"
                            kb.op("act", lambda e, sg_=sg_, b=b: e.activation(out=sg_[:], in_=self.bank(b), func=AF.Sigmoid),
                                  reads=[f"ps{b}"], writes=[ks])
                            kb.op("dve", lambda e, sg_=sg_, b2=b2: e.tensor_tensor(out=sg_[:], in0=self.bank(b2), in1=sg_[:], op=ALU.mult),
                                  reads=[f"ps{b2}", ks], writes=[ks])
                            kb.op("pool", lambda e, sg_=sg_, tt=tt, nb=nb: e.tensor_tensor(
                                out=Hs[:, tt, nb * 512:(nb + 1) * 512], in0=Hs[:, tt, nb * 512:(nb + 1) * 512],
                                in1=sg_[:], op=ALU.add), reads=[ks, "Hs"], writes=["Hs"])
                if ln is not None:
                    if lp is not None:
                        kb.dma("pool", S["H"][t0:t0 + TS, :].rearrange("(a p) d -> p a d", p=128), Hs[:], reads=["Hs"], writes=["Hd"])
                    if "hple" in self.dbg and lp == 0:
                        kb.dma("pool", self.dbg["hple"][t0:t0 + TS, :].rearrange("(a p) d -> p a d", p=128), Hs[:],
                               reads=["Hs"], writes=["dbg_hple"])
                else:
                    kb.dma("sp", wrep[:], I["final_norm_w"][0:1, :].partition_broadcast(128), writes=["wrep"])
                    ss = stat
                    for tt in range(TPS):
                        kb.op("act", lambda e, tt=tt: e.activation(out=sq_junk[:], in_=Hs[:, tt, :], func=AF.Square,
                                                                    accum_out=ss[:, tt:tt + 1]),
                              reads=["Hs"], writes=["sq_junk", "stat"])
                    kb.op("act", lambda e: e.activation(out=ss[:, TPS:2 * TPS], in_=ss[:, 0:TPS], func=AF.Sqrt,
                                                        bias=EPS, scale=1.0 / D), reads=["stat"], writes=["stat"])
                    kb.op("dve", lambda e: e.reciprocal(out=ss[:, 2 * TPS:3 * TPS], in_=ss[:, TPS:2 * TPS]),
                          reads=["stat"], writes=["stat"])
                    for tt in range(TPS):
                        kb.op("dve", lambda e, tt=tt: e.scalar_tensor_tensor(
                            out=Hs[:, tt, :], in0=Hs[:, tt, :], scalar=ss[:, 2 * TPS + tt:2 * TPS + tt + 1], in1=wrep[:],
                            op0=ALU.mult, op1=ALU.mult), reads=["Hs", "stat", "wrep"], writes=["Hs"])
                    kb.dma("pool", self.out[t0:t0 + TS, :].rearrange("(a p) d -> p a d", p=128), Hs[:], reads=["Hs"], writes=["outd"])
                    continue
                e2 = ln // 2
                nw = I["ev_norm_w"] if ln % 2 == 0 else I["od_norm_w"]
                kb.dma("sp", wrep[:], nw[e2:e2 + 1, :].partition_broadcast(128), writes=["wrep"])
                self.norm_T(es, Hs, wrep, AT, sq_junk, hn_tiles, stat, "Hs")
                kb.dma("pool", S["ATD"][st], AT[:], reads=["AT"], writes=["ATD"])

    def phase_I(self, ln):
        kb, S = self.kb, self.S
        with contextlib.ExitStack() as es:
            NW = 8
            wbuf = [self.sb(es, [128, 16, 512], BF16, f"iw{i}") for i in range(NW)]
            ATb = [self.sb(es, [128, 16, TS], BF16, f"iAT{i}") for i in range(3)]
            stg = [self.sb(es, [128, 512], F32, f"istg{i}") for i in range(4)]
            stgb = [self.sb(es, [128, 512], BF16, f"istgb{i}") for i in range(4)]
            self.srr = 0

            def next_stg(bf=False):
                i = self.srr % 4
                self.srr += 1
                return (stgb[i], f"istgb{i}") if bf else (stg[i], f"istg{i}")

            blocks = self.blocks_even(ln // 2, next_stg) if ln % 2 == 0 else self.blocks_odd(ln // 2, next_stg)
            groups, cur, n = [], [], 0
            for bl in blocks:
                if n + len(bl[0]) > NW:
                    groups.append(cur)
                    cur, n = [], 0
                cur.append(bl)
                n += len(bl[0])
            if cur:
                groups.append(cur)
            ati = 0
            for grp in groups:
                wi = 0
                loaded = []
                for (wl, body) in grp:
                    ws = []
                    for (blk4, ncols) in wl:
                        t, k_ = wbuf[wi], f"iw{wi}"
                        wi += 1
                        kb.dma("sp", t[:, :, 0:ncols], blk4[:, :, 0:ncols], writes=[k_])
                        ws.append((t, k_))
                    loaded.append((ws, body))
                for tb in range(L // TS):
                    A_, kA = ATb[ati % 3], f"iAT{ati % 3}"
                    ati += 1
                    kb.dma("sp", A_[:], S["ATD"][tb], reads=["ATD"], writes=[kA])
                    for (ws, body) in loaded:
                        body(ws, A_, kA, tb * TS)

    def tok_block(self, AT, kA, w, kw, tt, ncols=512):
        b = self.next_bank()
        self.mm(self.bank(b, ncols), [(AT[:, kc, tt * 128:(tt + 1) * 128], w[:, kc, 0:ncols]) for kc in range(16)],
                [kA, kw], f"ps{b}")
        return b

    def feat_block(self, AT, kA, w, kw, cl):
        b = self.next_bank()
        self.mm(self.bank(b), [(w[:, kc, cl * 128:(cl + 1) * 128], AT[:, kc, :]) for kc in range(16)],
                [kA, kw], f"ps{b}")
        return b

    def blocks_even(self, e, next_stg):
        kb, S = self.kb, self.S
        W = S["wk_ev_in"][e]
        ci = {c: i for i, c in enumerate(EV_COLS)}
        out = []

        def tok_body(dst, mode, nb):
            def body(ws, AT, kA, t0):
                (w, kw), = ws
                for tt in range(TPS):
                    b = self.tok_block(AT, kA, w, kw, tt)
                    r0 = t0 + tt * 128
                    if mode == "silu":
                        s, ks = next_stg()
                        kb.op("act", lambda e_, s=s, b=b: e_.activation(out=s[:], in_=self.bank(b), func=AF.Silu),
                              reads=[f"ps{b}"], writes=[ks])
                    else:
                        s, ks = next_stg(bf=True)
                        kb.op("dve", lambda e_, s=s, b=b: e_.tensor_copy(out=s[:], in_=self.bank(b)),
                              reads=[f"ps{b}"], writes=[ks])
                    kb.dma("pool", S[dst][r0:r0 + 128, nb * 512:(nb + 1) * 512], s[:], reads=[ks], writes=[dst])
            return body

        def feat_body(dst, doff, nb):
            def body(ws, AT, kA, t0):
                (w, kw), = ws
                for cl in range(4):
                    b = self.feat_block(AT, kA, w, kw, cl)
                    s, ks = next_stg(bf=True)
                    if cl % 2 == 0:
                        kb.op("act", lambda e_, s=s, b=b: e_.activation(out=s[:], in_=self.bank(b), func=AF.Copy),
                              reads=[f"ps{b}"], writes=[ks])
                    else:
                        kb.op("dve", lambda e_, s=s, b=b: e_.tensor_copy(out=s[:], in_=self.bank(b)),
                              reads=[f"ps{b}"], writes=[ks])
                    ch0 = (nb * 4 + cl) * 128
                    kb.dma("pool", S[dst][ch0:ch0 + 128, doff + t0:doff + t0 + TS], s[:], reads=[ks], writes=[dst])
            return body

        def dt_body(ws, AT, kA, t0):
            (w, kw), = ws
            s, ks = next_stg()
            for tt in range(TPS):
                b = self.tok_block(AT, kA, w, kw, tt, 64)
                kb.op("dve", lambda e_, s=s, b=b, tt=tt: e_.tensor_copy(out=s[:, tt * 64:(tt + 1) * 64], in_=self.bank(b, 64)),
                      reads=[f"ps{b}"], writes=[ks])
            kb.dma("pool", S["DT"][t0:t0 + TS, :].rearrange("(a p) c -> p a c", p=128),
                   s[:, 0:TPS * 64].rearrange("p (a c) -> p a c", a=TPS), reads=[ks], writes=["DT"])

        for (c_base, dst, mode) in ((0, "SZ", "silu"), (12352, "SG", "silu"), (10304, "V", "bf")):
            for nb in range(4):
                out.append(([(W[ci[c_base + nb * 512]], 512)], tok_body(dst, mode, nb)))
        for (c_base, nblk, dst, doff) in ((2048, 8, "XBCT", 2), (6208, 4, "QT", 0), (8256, 4, "KT", 0)):
            for nb in range(nblk):
                out.append(([(W[ci[c_base + nb * 512]], 512)], feat_body(dst, doff, nb)))
        out.append(([(W[28], 64)], dt_body))
        return out

    def blocks_odd(self, e, next_stg):
        kb, S = self.kb, self.S
        W = S["wk_od_in"][e]
        out = []

        def v_body(nb):
            def body(ws, AT, kA, t0):
                (wa, kwa), (wg, kwg) = ws
                for cl in range(4):
                    ba = self.feat_block(AT, kA, wa, kwa, cl)
                    bg = self.feat_block(AT, kA, wg, kwg, cl)
                    s, ks = next_stg()
                    sb_, ksb = next_stg(bf=True)
                    kb.op("act", lambda e_, s=s, bg=bg: e_.activation(out=s[:], in_=self.bank(bg), func=AF.Sigmoid),
                          reads=[f"ps{bg}"], writes=[ks])
                    kb.op("dve", lambda e_, s=s, sb_=sb_, ba=ba: e_.tensor_tensor(out=sb_[:], in0=self.bank(ba), in1=s[:], op=ALU.mult),
                          reads=[f"ps{ba}", ks], writes=[ksb])
                    ch0 = (nb * 4 + cl) * 128
                    kb.dma("pool", S["VT"][ch0:ch0 + 128, 15 + t0:15 + t0 + TS], sb_[:], reads=[ksb], writes=["VT"])
            return body

        def g_body(nb):
            def body(ws, AT, kA, t0):
                (w, kw), = ws
                for cl in range(4):
                    b = self.feat_block(AT, kA, w, kw, cl)
                    s, ks = next_stg()
                    kb.op("act", lambda e_, s=s, b=b: e_.activation(out=s[:], in_=self.bank(b), func=AF.Silu),
                          reads=[f"ps{b}"], writes=[ks])
                    ch0 = (nb * 4 + cl) * 128
                    kb.dma("pool", S["SG2T"][ch0:ch0 + 128, t0:t0 + TS], s[:], reads=[ks], writes=["SG2T"])
            return body

        for nb in range(8):
            out.append(([(W[nb], 512), (W[8 + nb], 512)], v_body(nb)))
        for nb in range(8):
            out.append(([(W[16 + nb], 512)], g_body(nb)))
        return out

    def phase_E(self, layer):
        kb, I, S = self.kb, self.I, self.S
        e = layer // 2
        with contextlib.ExitStack() as es:
            C = {}
            C["tri"] = self.sb(es, [128, 3, 128], F32, "tri")
            C["maskb"] = self.sb(es, [128, 2, 512], BF16, "maskb")
            C["dtb"] = self.sb(es, [128, 64], F32, "dtb")
            C["arep"] = self.sb(es, [128, 64], F32, "arep")
            C["dsk"] = self.sb(es, [128, 32], F32, "dsk")
            kb.dma("sp", C["tri"][:], I["c_tri"].rearrange("a p c -> p a c"), writes=["tri"])
            kb.dma("pool", C["maskb"][:], I["c_mask"].rearrange("a p c -> p a c"), writes=["maskb"])
            kb.dma("sp", C["dtb"][:], I["ev_dt_bias"][e:e + 1, :].partition_broadcast(128), writes=["dtb"])
            kb.dma("sp", C["arep"][:], I["ev_a_log"][e:e + 1, :].partition_broadcast(128), writes=["arep"])
            kb.dma("sp", C["dsk"][:], I["ev_d_skip"][e:e + 1, :].partition_broadcast(128), writes=["dsk"])
            kb.op("act", lambda e_: e_.activation(out=C["arep"][:], in_=C["arep"][:], func=AF.Exp), reads=["arep"], writes=["arep"])
            kb.op("dve", lambda e_: e_.tensor_scalar(out=C["arep"][:], in0=C["arep"][:], scalar1=-1.0, scalar2=None, op0=ALU.mult),
                  reads=["arep"], writes=["arep"])
            for nm, shp in (("dtr", [128, 64]), ("x1", [128, 64]), ("dtv", [128, 64]), ("lndt", [128, 64]), ("da", [128, 64]),
                            ("nda", [128, 32]), ("cums", [128, 192]), ("Ein", [128, 6, 32]), ("Eout", [128, 6, 32]), ("biasf", [128, 32])):
                C[nm] = self.sb(es, shp, F32, nm)
            self.C = C
            if "A" in E_PARTS:
                self.ssd_pass_A(e)
                kb.barrier()
            if "C" in E_PARTS:
                self.ssd_pass_C(e)
                kb.barrier()
        if "N" in E_PARTS:
            self.na(e)

    def dtq(self, c, fixed=None):
        kb, S, C = self.kb, self.S, self.C
        K_ = ["dtq"]
        kb.dma("sp", C["dtr"][:], S["DT"][c * 128:(c + 1) * 128, :], reads=["DT"], writes=["dtr"])
        kb.op("dve", lambda e: e.tensor_tensor(out=C["x1"][:], in0=C["dtr"][:], in1=C["dtb"][:], op=ALU.add),
              reads=["dtr", "dtb"], writes=K_)
        kb.op("act", lambda e: e.activation(out=C["x1"][:], in_=C["x1"][:], func=AF.Exp), reads=K_, writes=K_)
        kb.op("act", lambda e: e.activation(out=C["dtv"][:], in_=C["x1"][:], func=AF.Ln, bias=1.0, scale=1.0), reads=K_, writes=K_)
        kb.op("act", lambda e: e.activation(out=C["lndt"][:], in_=C["dtv"][:], func=AF.Ln), reads=K_, writes=K_)
        kb.op("dve", lambda e: e.tensor_tensor(out=C["da"][:], in0=C["dtv"][:], in1=C["arep"][:], op=ALU.mult),
              reads=K_ + ["arep"], writes=K_)
        kb.op("dve", lambda e: e.tensor_scalar(out=C["nda"][:], in0=C["da"][:, 32:64], scalar1=-1.0, scalar2=None, op0=ALU.mult),
              reads=K_, writes=K_)
        if fixed is None:
            b = self.next_bank()
            off, pk = 0, f"ps{b}"
        else:
            b, off, pk = fixed
        tri = C["tri"]
        for j in range(3):
            kb.op("pe", lambda e, j=j: e.matmul(self.bank(b, 64, off + j * 64), lhsT=tri[:, j, :], rhs=C["da"][:], start=True, stop=True),
                  reads=K_ + ["tri"], writes=[pk], signal=(j == 2))
        cums = C["cums"]
        kb.op("act", lambda e: e.activation(out=cums[:], in_=self.bank(b, 192, off), func=AF.Copy), reads=[pk], writes=K_)
        cI, cE, tot = cums[:, 0:64], cums[:, 64:128], cums[:, 128:192]
        Ein, Eout = C["Ein"], C["Eout"]
        R = K_
        kb.op("dve", lambda e: e.tensor_copy(out=Ein[:, 2, :], in_=cI[:, 0:32]), reads=R, writes=K_)
        kb.op("dve", lambda e: e.tensor_copy(out=Ein[:, 4:6, :], in_=tot.rearrange("p (a c) -> p a c", a=2)), reads=R, writes=K_)
        kb.op("dve", lambda e: e.tensor_tensor(out=Ein[:, 0, :], in0=tot[:, 0:32], in1=Ein[:, 2, :], op=ALU.subtract), reads=R, writes=K_)
        kb.op("dve", lambda e: e.tensor_tensor(out=Ein[:, 0, :], in0=Ein[:, 0, :], in1=C["lndt"][:, 0:32], op=ALU.add), reads=R, writes=K_)
        kb.op("dve", lambda e: e.tensor_tensor(out=Ein[:, 1, :], in0=cE[:, 32:64], in1=C["lndt"][:, 32:64], op=ALU.add), reads=R, writes=K_)
        kb.op("dve", lambda e: e.tensor_tensor(out=Ein[:, 3, :], in0=tot[:, 32:64], in1=cE[:, 32:64], op=ALU.subtract), reads=R, writes=K_)
        kb.op("dve", lambda e: e.tensor_tensor(out=C["biasf"][:], in0=C["lndt"][:, 0:32], in1=Ein[:, 2, :], op=ALU.subtract), reads=R, writes=K_)
        kb.op("act", lambda e: e.activation(out=Eout[:], in_=Ein[:], func=AF.Exp), reads=K_, writes=K_)

    def ssd_pass_A(self, e):
        kb, I, S, C = self.kb, self.I, self.S, self.C
        with contextlib.ExitStack() as es:
            wT = self.sb(es, [128, 32, 5], F32, "wT")
            diag = self.sb(es, [128, 32, 5, 128], BF16, "diag")
            cbias = self.sb(es, [128, 32], F32, "cbias")
            brow = self.sb(es, [1, 3072], BF16, "brow")
            xin = [self.sb(es, [128, 32, 516], BF16, f"xin{i}") for i in range(2)]
            xc = self.sb(es, [128, 4, 2048], BF16, "xc")
            btok = self.sb(es, [128, 4, 1024], BF16, "btok")
            bT = self.sb(es, [128, 8, 512], BF16, "bT")
            cT = self.sb(es, [128, 8, 512], BF16, "cT")
            Sf = self.sb(es, [128, 2048], F32, "Sf")
            xw = [self.sb(es, [128, 2048], BF16, f"xw{i}") for i in range(2)]
            stgb = self.sb(es, [128, 2048], BF16, "stgb")
            stgf = self.sb(es, [128, 2048], F32, "stgf")
            kb.dma("sp", wT[:], I["ev_conv_wT"][e].rearrange("(cb p) k -> p cb k", p=128), writes=["wT"])
            kb.dma("sp", cbias[:], I["ev_conv_bpm"][e], writes=["cbias"])
            kb.dma("pool", brow[:], I["ev_conv_b"][e:e + 1, 0:3072], writes=["brow"])
            for cb in range(32):
                for k in range(5):
                    kb.op("dve", lambda e_, cb=cb, k=k: e_.tensor_scalar(out=diag[:, cb, k, :], in0=self.ident_f[:],
                                                                       scalar1=wT[:, cb, k:k + 1], scalar2=None, op0=ALU.mult),
                          reads=["wT", "ident_f"], writes=["diag"])
            kb.op("dve", lambda e_: e_.memset(Sf[:], 0.0), writes=["Sf"])
            for blk in range(NT // 4):
                t0 = blk * 512
                xi = xin[blk % 2]
                kx = f"xin{blk % 2}"
                kb.dma("sp", xi[:], S["XBCT"][:, t0:t0 + 516].rearrange("(cb p) t -> p cb t", p=128), reads=["XBCT"], writes=[kx])
                for ci in range(4):
                    for cbg in range(6):
                        b = self.next_bank()
                        for j in range(4):
                            cb = cbg * 4 + j
                            pairs = [(xi[:, cb, ci * 128 + k:ci * 128 + k + 128], diag[:, cb, k, :]) for k in range(5)]
                            pairs.append((self.ones_b[0:1, :], brow[0:1, cb * 128:(cb + 1) * 128]))
                            self.mm(self.bank(b, 128, j * 128), pairs, [kx, "diag", "brow", "ones_b"], f"ps{b}")
                        dst = xc[:, ci, cbg * 512:(cbg + 1) * 512] if cbg < 4 else btok[:, ci, (cbg - 4) * 512:(cbg - 3) * 512]
                        kb.op("act", lambda e_, dst=dst, b=b: e_.activation(out=dst, in_=self.bank(b), func=AF.Silu),
                              reads=[f"ps{b}"], writes=["xc" if cbg < 4 else "btok"])
                for cb in range(16, 32):
                    b = self.next_bank()
                    pairs = [(diag[:, cb, k, :], xi[:, cb, k:k + 512]) for k in range(5)]
                    self.mm(self.bank(b), pairs, [kx, "diag"], f"ps{b}")
                    dst = bT[:, cb - 16, :] if cb < 24 else cT[:, cb - 24, :]
                    kb.op("act", lambda e_, dst=dst, b=b, cb=cb: e_.activation(out=dst, in_=self.bank(b), func=AF.Silu,
                                                                           bias=cbias[:, cb:cb + 1]),
                          reads=[f"ps{b}", "cbias"], writes=["bT" if cb < 24 else "cT"])
                kb.dma("pool", S["XC"][t0:t0 + 512, :].rearrange("(a p) d -> p a d", p=128), xc[:], reads=["xc"], writes=["XC"])
                kb.dma("pool", S["BT"][:, t0:t0 + 512].rearrange("(g p) t -> p g t", p=128), bT[:], reads=["bT"], writes=["BT"])
                kb.dma("pool", S["CT"][:, t0:t0 + 512].rearrange("(g p) t -> p g t", p=128), cT[:], reads=["cT"], writes=["CT"])
                for ci in range(4):
                    c = blk * 4 + ci
                    self.dtq(c)
                    Eout = C["Eout"]
                    for d_ in range(2):
                        xw_ = xw[d_]
                        kxw = f"xw{d_}"
                        kb.op("dve", lambda e_, xw_=xw_, ci=ci, d_=d_: e_.tensor_tensor(
                            out=xw_[:].rearrange("p (h c) -> p h c", h=32), in0=xc[:, ci, :].rearrange("p (h c) -> p h c", h=32),
                            in1=_bc(Eout[:, d_, :], 64), op=ALU.mult), reads=["xc", "dtq"], writes=[kxw])
                        banks = [self.next_bank() for _ in range(4)]
                        for g in range(8):
                            bk = banks[g // 2]
                            self.mm(self.bank(bk, 256, (g % 2) * 256), [(btok[:, ci, g * 128:(g + 1) * 128], xw_[:, g * 256:(g + 1) * 256])],
                                    ["btok", kxw], f"ps{bk}")
                        if d_ == 0:
                            kb.op("act", lambda e_: e_.activation(out=stgb[:], in_=Sf[:], func=AF.Copy), reads=["Sf"], writes=["stgb"])
                            kb.dma("pool", S["SPF"][c], stgb[:], reads=["stgb"], writes=["SPF"])
                            kb.op("dve", lambda e_: e_.tensor_tensor(out=Sf[:].rearrange("p (h c) -> p h c", h=32),
                                                                    in0=Sf[:].rearrange("p (h c) -> p h c", h=32),
                                                                    in1=_bc(Eout[:, 4, :], 64), op=ALU.mult),
                                  reads=["Sf", "dtq", "stgb"], writes=["Sf"])
                            for j, bk in enumerate(banks):
                                kb.op("dve", lambda e_, j=j, bk=bk: e_.tensor_tensor(out=Sf[:, j * 512:(j + 1) * 512], in0=self.bank(bk),
                                                                                 in1=Sf[:, j * 512:(j + 1) * 512], op=ALU.add),
                                      reads=[f"ps{bk}", "Sf"], writes=["Sf"])
                        else:
                            for j, bk in enumerate(banks):
                                kb.op("act", lambda e_, j=j, bk=bk: e_.activation(out=stgf[:, j * 512:(j + 1) * 512], in_=self.bank(bk), func=AF.Copy),
                                      reads=[f"ps{bk}"], writes=["stgf"])
                            kb.dma("pool", S["LSB"][c], stgf[:], reads=["stgf"], writes=["LSB"])

    def ssd_pass_C(self, e):
        kb, I, S, C = self.kb, self.I, self.S, self.C
        with contextlib.ExitStack() as es:
            xc = self.sb(es, [128, 2048], BF16, "xcC")
            bT = self.sb(es, [128, 8, 128], BF16, "bTC")
            cT = self.sb(es, [128, 8, 128], BF16, "cTC")
            spf = self.sb(es, [128, 2048], BF16, "spf")
            lsb = self.sb(es, [128, 2048], F32, "lsb")
            Sb = self.sb(es, [128, 2048], F32, "Sb")
            Sbb = self.sb(es, [128, 2048], BF16, "Sbb")
            Xf = self.sb(es, [128, 32, 128], F32, "Xf")
            Xb = self.sb(es, [128, 32, 128], F32, "Xb")
            G = [self.sb(es, [128, 4, 128], F32, f"G{i}") for i in range(4)]
            Wt = [self.sb(es, [128, 4, 128], BF16, f"Wt{i}") for i in range(2)]
            ys = self.sb(es, [128, 2048], F32, "ysC")
            t1 = self.sb(es, [128, 512], F32, "t1")
            t2 = self.sb(es, [128, 512], F32, "t2")
            xd = self.sb(es, [128, 2048], F32, "xd")
            ones_f = C["tri"][:, 2, :]
            kb.op("dve", lambda e_: e_.memset(Sb[:], 0.0), writes=["Sb"])
            kb.op("dve", lambda e_: e_.memset(Sbb[:], 0.0), writes=["Sbb"])
            for c in range(NT - 1, -1, -1):
                r0 = c * 128
                kb.dma("sp", xc[:], S["XC"][r0:r0 + 128, :], reads=["XC"], writes=["xcC"])
                kb.dma("sp", bT[:], S["BT"][:, r0:r0 + 128].rearrange("(g p) t -> p g t", p=128), reads=["BT"], writes=["bTC"])
                kb.dma("sp", cT[:], S["CT"][:, r0:r0 + 128].rearrange("(g p) t -> p g t", p=128), reads=["CT"], writes=["cTC"])
                kb.dma("sp", spf[:], S["SPF"][c], reads=["SPF"], writes=["spf"])
                kb.dma("sp", lsb[:], S["LSB"][c], reads=["LSB"], writes=["lsb"])
                self.dtq(c)
                Ein, Eout = C["Ein"], C["Eout"]
                kb.op("dve", lambda e_: e_.tensor_tensor(out=Xf[:], in0=_bc(C["da"][:, 0:32], 128),
                                                        in1=C["tri"][:, 0, :].unsqueeze(1).broadcast_to([128, 32, 128]), op=ALU.mult),
                      reads=["dtq", "tri"], writes=["Xf"])
                kb.op("dve", lambda e_: e_.tensor_tensor(out=Xb[:], in0=_bc(C["nda"][:, 0:32], 128),
                                                        in1=C["tri"][:, 1, :].unsqueeze(1).broadcast_to([128, 32, 128]), op=ALU.mult),
                      reads=["dtq", "tri"], writes=["Xb"])
                bY, bF, bB, bS = 0, 1, 2, 3

                def st_a(g):
                    gi = g % 2
                    self.mm(self.bank(bS, 128, gi * 128), [(bT[:, g, :], cT[:, g, :])], ["bTC", "cTC"], f"ps{bS}")
                    for d_ in range(2):
                        bR = 4 + (self.psrr % 4)
                        self.psrr += 1
                        X_ = Xf if d_ == 0 else Xb
                        pairs = [(ones_f, X_[:, g * 4:(g + 1) * 4, :].rearrange("p a c -> p (a c)")),
                                 (self.ident_b[:], C["maskb"][:, d_, :])]
                        self.mm(self.bank(bR), pairs, ["Xf" if d_ == 0 else "Xb", "tri", "ident_b", "maskb"], f"ps{bR}")
                        Gd = G[gi * 2 + d_]
                        kG = f"G{gi * 2 + d_}"
                        for h in range(4):
                            bias = C["biasf"][:, g * 4 + h:g * 4 + h + 1] if d_ == 0 else Ein[:, 1, g * 4 + h:g * 4 + h + 1]
                            kb.op("act", lambda e_, Gd=Gd, h=h, bR=bR, bias=bias: e_.activation(
                                out=Gd[:, h, :], in_=self.bank(bR, 128, h * 128), func=AF.Exp, bias=bias, scale=1.0),
                                reads=[f"ps{bR}", "dtq"], writes=[kG])
                    Gf, Gb = G[gi * 2], G[gi * 2 + 1]
                    kb.op("pool", lambda e_, Gf=Gf, Gb=Gb: e_.tensor_tensor(out=Gf[:], in0=Gf[:], in1=Gb[:], op=ALU.add),
                          reads=[f"G{gi * 2}", f"G{gi * 2 + 1}"], writes=[f"G{gi * 2}"])
                    W_ = Wt[gi]
                    kb.op("dve", lambda e_, W_=W_, Gf=Gf, gi=gi: e_.tensor_tensor(
                        out=W_[:], in0=Gf[:], in1=self.bank(bS, 128, gi * 128).unsqueeze(1).broadcast_to([128, 4, 128]), op=ALU.mult),
                        reads=[f"G{gi * 2}", f"ps{bS}"], writes=[f"Wt{gi}"])

                def st_b(g):
                    gi = g % 2
                    W_ = Wt[gi]
                    for h in range(4):
                        hh = g * 4 + h
                        self.mm(self.bank(bY, 64, (gi * 4 + h) * 64), [(W_[:, h, :], xc[:, hh * 64:(hh + 1) * 64])],
                                [f"Wt{gi}", "xcC"], f"ps{bY}")
                    self.mm(self.bank(bF, 256, gi * 256), [(cT[:, g, :], spf[:, g * 256:(g + 1) * 256])], ["cTC", "spf"], f"ps{bF}")
                    self.mm(self.bank(bB, 256, gi * 256), [(cT[:, g, :], Sbb[:, g * 256:(g + 1) * 256])], ["cTC", "Sbb"], f"ps{bB}")

                def epi(q):
                    hs = slice(q * 8, q * 8 + 8)
                    kb.op("dve", lambda e_, hs=hs: e_.tensor_tensor(out=t1[:].rearrange("p (h c) -> p h c", h=8),
                                                                  in0=self.bank(bF).rearrange("p (h c) -> p h c", h=8),
                                                                  in1=_bc(Eout[:, 2, hs], 64), op=ALU.mult),
                          reads=[f"ps{bF}", "dtq"], writes=["t1"])
                    kb.op("dve", lambda e_, hs=hs: e_.tensor_tensor(out=t2[:].rearrange("p (h c) -> p h c", h=8),
                                                                  in0=self.bank(bB).rearrange("p (h c) -> p h c", h=8),
                                                                  in1=_bc(Eout[:, 3, hs], 64), op=ALU.mult),
                          reads=[f"ps{bB}", "dtq"], writes=["t2"])
                    kb.op("pool", lambda e_: e_.tensor_tensor(out=t1[:], in0=t1[:], in1=t2[:], op=ALU.add), reads=["t1", "t2"], writes=["t1"])
                    kb.op("dve", lambda e_, q=q: e_.tensor_tensor(out=ys[:, q * 512:(q + 1) * 512], in0=self.bank(bY), in1=t1[:], op=ALU.add),
                          reads=[f"ps{bY}", "t1"], writes=["ysC"])

                st_a(0)
                for g in range(8):
                    if g + 1 < 8:
                        st_a(g + 1)
                    st_b(g)
                    if g % 2 == 1:
                        epi(g // 2)
                kb.op("pool", lambda e_: e_.tensor_tensor(out=xd[:].rearrange("p (h c) -> p h c", h=32),
                                                         in0=xc[:].rearrange("p (h c) -> p h c", h=32),
                                                         in1=_bc(C["dsk"][:, :], 64), op=ALU.mult), reads=["xcC", "dsk"], writes=["xd"])
                kb.op("dve", lambda e_: e_.tensor_tensor(out=ys[:], in0=ys[:], in1=xd[:], op=ALU.add), reads=["ysC", "xd"], writes=["ysC"])
                kb.dma("pool", S["YS"][r0:r0 + 128, :], ys[:], reads=["ysC"], writes=["YS"])
                kb.op("dve", lambda e_: e_.tensor_tensor(out=Sb[:].rearrange("p (h c) -> p h c", h=32),
                                                        in0=Sb[:].rearrange("p (h c) -> p h c", h=32),
                                                        in1=_bc(Eout[:, 5, :], 64), op=ALU.mult), reads=["Sb", "dtq"], writes=["Sb"])
                kb.op("dve", lambda e_: e_.tensor_tensor(out=Sb[:], in0=Sb[:], in1=lsb[:], op=ALU.add), reads=["Sb", "lsb"], writes=["Sb"])
                kb.op("act", lambda e_: e_.activation(out=Sbb[:], in_=Sb[:], func=AF.Copy), reads=["Sb"], writes=["Sbb"])

    def na(self, e):
        kb, I, S = self.kb, self.I, self.S
        rows = L // 64
        scale = 128 ** -0.5
        RB = 16
        with contextlib.ExitStack() as es:
            KTh = self.sb(es, [128, L], BF16, "KTh")
            QTh = self.sb(es, [128, L], BF16, "QTh")
            V0 = self.sb(es, [128, NT, 132], BF16, "V0")
            V1 = self.sb(es, [128, NT, 132], BF16, "V1")
            bt = self.sb(es, [128, 8, 256], F32, "bt")
            nam = self.sb(es, [128, 256], F32, "nam")
            NBUF = 4
            s2 = [self.sb(es, [128, 256], F32, f"s2_{i}") for i in range(NBUF)]
            pT = [self.sb(es, [128, 256], BF16, f"pT_{i}") for i in range(NBUF)]
            rinv = [self.sb(es, [64, 1], F32, f"rinv{i}") for i in range(NBUF)]
            obuf = [self.sb(es, [64, RB, 128], F32, f"obuf{i}") for i in range(2)]
            kb.dma("sp", nam[:], I["c_namask"], writes=["nam"])
            kb.op("dve", lambda e_: e_.memset(V0[:, :, 128:129], 1.0), writes=["V0"])
            kb.op("dve", lambda e_: e_.memset(V1[:, :, 128:129], 1.0), writes=["V1"])
            for h in range(16):
                kb.dma("sp", KTh[:], S["KT"][h * 128:(h + 1) * 128, :], reads=["KT"], writes=["KTh"])
                kb.dma("sp", QTh[:], S["QT"][h * 128:(h + 1) * 128, :], reads=["QT"], writes=["QTh"])
                kb.dma("sp", V0[:, :, 0:128], S["V"][:, h * 128:(h + 1) * 128].rearrange("(j p) d -> p j d", p=128), reads=["V"], writes=["V0"])
                kb.dma("sp", V1[:, 0:NT - 1, 0:128], S["V"][64:L - 64, h * 128:(h + 1) * 128].rearrange("(j p) d -> p j d", p=128),
                       reads=["V"], writes=["V1"])
                kb.dma("sp", bt[:], I["ev_rpbt"][e, h].rearrange("a p c -> p a c"), writes=["bt"])
                kb.op("dve", lambda e_: e_.tensor_tensor(out=bt[:], in0=bt[:], in1=nam[:].unsqueeze(1).broadcast_to([128, 8, 256]), op=ALU.add),
                      reads=["bt", "nam"], writes=["bt"])
                def stage_a(r):
                    rs = min(max(r - 4, 0), rows - 8)
                    pat = r - rs
                    ks = rs * 64
                    i2 = r % NBUF
                    b = self.next_bank(0, 4)
                    for c_ in range(4):
                        self.mm(self.bank(b, 64, c_ * 64), [(KTh[:, ks + c_ * 128:ks + (c_ + 1) * 128], QTh[:, r * 64:(r + 1) * 64])],
                                ["KTh", "QTh"], f"ps{b}")
                    kb.op("dve", lambda e_, i2=i2, b=b, pat=pat: e_.scalar_tensor_tensor(
                        out=s2[i2][:], in0=self.bank(b, 256), scalar=scale, in1=bt[:, pat, :], op0=ALU.mult, op1=ALU.add),
                        reads=[f"ps{b}", "bt"], writes=[f"s2_{i2}"])
                    kb.op("act", lambda e_, i2=i2: e_.activation(out=pT[i2][:], in_=s2[i2][:], func=AF.Exp),
                          reads=[f"s2_{i2}"], writes=[f"pT_{i2}"])

                def stage_b(r):
                    rs = min(max(r - 4, 0), rows - 8)
                    ks = rs * 64
                    i2 = r % NBUF
                    b2 = self.next_bank(4, 8)
                    pairs = []
                    for c_ in range(4):
                        tk = ks + c_ * 128
                        vsel = V0[:, tk // 128, 0:129] if tk % 128 == 0 else V1[:, (tk - 64) // 128, 0:129]
                        pairs.append((pT[i2][:, c_ * 64:(c_ + 1) * 64], vsel))
                    self.mm(self.psum[0:64, b2 * 512:b2 * 512 + 129], pairs, [f"pT_{i2}", "V0", "V1"], f"ps{b2}")
                    kb.op("dve", lambda e_, i2=i2, b2=b2: e_.reciprocal(out=rinv[i2][:], in_=self.psum[0:64, b2 * 512 + 128:b2 * 512 + 129]),
                          reads=[f"ps{b2}"], writes=[f"rinv{i2}"])
                    ob = obuf[(r // RB) % 2]
                    kob = f"obuf{(r // RB) % 2}"
                    kb.op("act", lambda e_, ob=ob, r=r, i2=i2, b2=b2: e_.activation(
                        out=ob[:, r % RB, :], in_=self.psum[0:64, b2 * 512:b2 * 512 + 128], func=AF.Copy, scale=rinv[i2][:]),
                        reads=[f"ps{b2}", f"rinv{i2}"], writes=[kob])
                    if r % RB == RB - 1:
                        rr0 = (r - RB + 1) * 64
                        kb.dma("pool", S["YN"][rr0:rr0 + RB * 64, h * 128:(h + 1) * 128].rearrange("(r q) d -> q r d", q=64), ob[:],
                               reads=[kob], writes=["YN"])

                SK = 2
                for r in range(rows + SK):
                    if r < rows:
                        stage_a(r)
                    if r >= SK:
                        stage_b(r - SK)

    def phase_O(self, layer):
        kb, I, S = self.kb, self.I, self.S
        e = layer // 2
        NB = L // 512
        with contextlib.ExitStack() as es:
            wT = self.sb(es, [128, 32, 31], F32, "wT31")
            pm = self.sb(es, [128, 3, 32], F32, "pm31")
            dgs = [self.sb(es, [128, 31, 128], BF16, f"dg{i}") for i in range(2)]
            vins = [self.sb(es, [128, L + 30], BF16, f"vin{i}") for i in range(2)]
            stg = [self.sb(es, [128, 512], F32, f"stgO{i}") for i in range(3)]
            kb.dma("sp", wT[:], I["od_dw_wT"][e].rearrange("(cb p) k -> p cb k", p=128), writes=["wT31"])
            kb.dma("sp", pm[:], I["od_pm"][e].rearrange("a p c -> p a c"), writes=["pm31"])
            si = 0
            for cb in range(32):
                dg, kd = dgs[cb % 2], f"dg{cb % 2}"
                vi, kv = vins[cb % 2], f"vin{cb % 2}"
                for k in range(31):
                    kb.op("dve", lambda e_, k=k, dg=dg, cb=cb: e_.tensor_scalar(
                        out=dg[:, k, :], in0=self.ident_f[:], scalar1=wT[:, cb, k:k + 1], scalar2=None, op0=ALU.mult),
                        reads=["wT31", "ident_f"], writes=[kd])
                kb.dma("sp", vi[:], S["VT"][cb * 128:(cb + 1) * 128, :], reads=["VT"], writes=[kv])
                for tb in range(NB):
                    b = self.next_bank()
                    self.mm(self.bank(b), [(dg[:, k, :], vi[:, tb * 512 + k:tb * 512 + k + 512]) for k in range(31)],
                            [kd, kv], f"ps{b}")
                    s_, ks_ = stg[si % 3], f"stgO{si % 3}"
                    si += 1
                    kb.op("act", lambda e_, s_=s_, b=b, cb=cb: e_.activation(out=s_[:], in_=self.bank(b), func=AF.Identity,
                                                                         bias=pm[:, 0, cb:cb + 1], scale=1.0),
                          reads=[f"ps{b}", "pm31"], writes=[ks_])
                    kb.dma("pool", S["VCT"][cb * 128:(cb + 1) * 128, tb * 512:(tb + 1) * 512], s_[:], reads=[ks_], writes=["VCT"])
        kb.barrier()
        with contextlib.ExitStack() as es:
            pm = self.sb(es, [128, 3, 32], F32, "pm31b")
            ones_f = self.sb(es, [128, 128], F32, "ones_f")
            vblk = self.sb(es, [128, 32, 512], F32, "vblk")
            sq = [self.sb(es, [128, 512], F32, f"sq{i}") for i in range(2)]
            sgt = [self.sb(es, [128, 512], F32, f"sgt{i}") for i in range(3)]
            tt_ = [self.sb(es, [128, 512], F32, f"tO{i}") for i in range(2)]
            yb = [self.sb(es, [128, 512], BF16, f"ybO{i}") for i in range(3)]
            mean = self.sb(es, [128, 512], F32, "meanO")
            rstd = self.sb(es, [128, 512], F32, "rstdO")
            tmp = self.sb(es, [128, 512], F32, "tmpO")
            kb.dma("sp", pm[:], I["od_pm"][e].rearrange("a p c -> p a c"), writes=["pm31b"])
            kb.dma("sp", ones_f[:], I["c_tri"][2], writes=["ones_f"])
            for tb in range(NB):
                c0 = tb * 512
                kb.dma("sp", vblk[:], S["VCT"][:, c0:c0 + 512].rearrange("(cb p) t -> p cb t", p=128), reads=["VCT"], writes=["vblk"])
                bA, bB = 0, 1
                for cb in range(32):
                    q_, kq = sq[cb % 2], f"sq{cb % 2}"
                    kb.op("act", lambda e_, q_=q_, cb=cb: e_.activation(out=q_[:], in_=vblk[:, cb, :], func=AF.Square), reads=["vblk"], writes=[kq])
                    kb.op("pe", lambda e_, cb=cb: e_.matmul(self.bank(bA), lhsT=ones_f[:], rhs=vblk[:, cb, :], start=(cb == 0), stop=(cb == 31)),
                          reads=["vblk", "ones_f"], writes=[f"ps{bA}"], signal=(cb == 31))
                    kb.op("pe", lambda e_, cb=cb, q_=q_: e_.matmul(self.bank(bB), lhsT=ones_f[:], rhs=q_[:], start=(cb == 0), stop=(cb == 31)),
                          reads=[kq, "ones_f"], writes=[f"ps{bB}"], signal=True)
                kb.op("dve", lambda e_: e_.tensor_scalar(out=mean[:], in0=self.bank(bA), scalar1=1.0 / 4096, scalar2=None, op0=ALU.mult),
                      reads=[f"ps{bA}"], writes=["meanO"])
                kb.op("dve", lambda e_: e_.tensor_tensor(out=tmp[:], in0=mean[:], in1=mean[:], op=ALU.mult), reads=["meanO"], writes=["tmpO"])
                kb.op("dve", lambda e_: e_.scalar_tensor_tensor(out=tmp[:], in0=self.bank(bB), scalar=1.0 / 4096, in1=tmp[:],
                                                                op0=ALU.mult, op1=ALU.subtract), reads=[f"ps{bB}", "tmpO"], writes=["tmpO"])
                kb.op("act", lambda e_: e_.activation(out=tmp[:], in_=tmp[:], func=AF.Sqrt, bias=EPS, scale=1.0), reads=["tmpO"], writes=["tmpO"])
                kb.op("dve", lambda e_: e_.reciprocal(out=rstd[:], in_=tmp[:]), reads=["tmpO"], writes=["rstdO"])
                for cb in range(32):
                    g_, kg = sgt[cb % 3], f"sgt{cb % 3}"
                    t_, kt = tt_[cb % 2], f"tO{cb % 2}"
                    y_, ky = yb[cb % 3], f"ybO{cb % 3}"
                    kb.dma("sp", g_[:], S["SG2T"][cb * 128:(cb + 1) * 128, c0:c0 + 512], reads=["SG2T"], writes=[kg])
                    kb.op("dve", lambda e_, t_=t_, cb=cb: e_.tensor_tensor(out=t_[:], in0=vblk[:, cb, :], in1=mean[:], op=ALU.subtract),
                          reads=["vblk", "meanO"], writes=[kt])
                    kb.op("pool", lambda e_, t_=t_: e_.tensor_tensor(out=t_[:], in0=t_[:], in1=rstd[:], op=ALU.mult), reads=[kt, "rstdO"], writes=[kt])
                    kb.op("act", lambda e_, t_=t_, cb=cb: e_.activation(out=t_[:], in_=t_[:], func=AF.Silu, bias=pm[:, 2, cb:cb + 1],
                                                                     scale=pm[:, 1, cb:cb + 1]), reads=[kt, "pm31b"], writes=[kt])
                    kb.op("dve", lambda e_, t_=t_, g_=g_, y_=y_: e_.tensor_tensor(out=y_[:], in0=t_[:], in1=g_[:], op=ALU.mult),
                          reads=[kt, kg], writes=[ky])
                    kb.dma("pool", S["YOT"][cb * 128:(cb + 1) * 128, c0:c0 + 512], y_[:], reads=[ky], writes=["YOT"])


def _consts():
    c = {}
    c["c_ident"] = np.eye(128, dtype=np.float32)
    t = np.arange(128)
    tri = np.zeros((3, 128, 128), np.float32)
    tri[0] = (t[:, None] <= t[None, :])
    tri[1] = (t[:, None] < t[None, :])
    tri[2] = 1.0
    c["c_tri"] = tri
    m = np.zeros((2, 128, 512), np.float32)
    mf = np.where(t[:, None] <= t[None, :], 0.0, NEG)
    mb = np.where(t[:, None] >= t[None, :], 0.0, NEG)
    m[0] = np.tile(mf, (1, 4))
    m[1] = np.tile(mb, (1, 4))
    c["c_mask"] = m.astype(np.float32)
    cols = np.arange(64)
    cs = np.clip(cols - 8, 0, 48)
    valid = (cols[None, :] >= cs[:, None]) & (cols[None, :] < cs[:, None] + 16)
    mk = np.where(valid.T, 0.0, NEG).astype(np.float32)
    mk2 = np.concatenate([mk, mk], 0)
    c["c_namask"] = np.tile(mk2, (1, 4)).astype(np.float32)
    return c


def _rpb_table(rpb):
    cols = np.arange(64)
    coff = np.clip(cols[None, :] - cols[:, None], -15, 15) + 15
    out = np.zeros((2, 16, 8, 128, 256), np.float32)
    for pat in range(8):
        delta = -pat
        for c_ in range(4):
            for il in range(2):
                i = 2 * c_ + il
                roff = delta + i + 7
                g = rpb[:, :, roff, :][:, :, coff]
                out[:, :, pat, il * 64:(il + 1) * 64, c_ * 64:(c_ + 1) * 64] = np.transpose(g, (0, 1, 3, 2))
    return out


_NC_CACHE = {}


def kernel(x, p, ev_norm_w, ev_w_in, ev_conv_w, ev_conv_b, ev_dt_bias_f, ev_dt_bias_b,
           ev_a_log_f, ev_a_log_b, ev_d_skip, ev_gnorm_w, ev_rpb, ev_w_out,
           od_norm_w, od_w_in, od_dw_w, od_dw_b, od_ln_w, od_ln_b, od_w_out,
           ple_norm_w, ple_w_gate, ple_w_proj, final_norm_w):
    f = lambda a: np.ascontiguousarray(np.asarray(a, dtype=np.float32))
    if "nc" not in _NC_CACHE:
        _NC_CACHE["nc"] = Prog().build()
    nc = _NC_CACHE["nc"]
    shared = {
        "ev_norm_w": f(ev_norm_w), "ev_w_in": f(ev_w_in), "ev_conv_wT": f(np.transpose(f(ev_conv_w), (0, 2, 1))),
        "ev_conv_b": f(ev_conv_b), "ev_conv_bpm": f(np.transpose(f(ev_conv_b).reshape(2, 32, 128), (0, 2, 1))),
        "ev_dt_bias": f(np.concatenate([ev_dt_bias_f, ev_dt_bias_b], 1)),
        "ev_a_log": f(np.concatenate([ev_a_log_f, ev_a_log_b], 1)),
        "ev_d_skip": f(ev_d_skip), "ev_gnorm_w": f(ev_gnorm_w), "ev_rpbt": _rpb_table(f(ev_rpb)),
        "ev_w_out": f(ev_w_out), "od_norm_w": f(od_norm_w), "od_w_in": f(od_w_in), "od_dw_wT": f(np.transpose(f(od_dw_w), (0, 2, 1))),
        "od_pm": f(np.transpose(np.stack([f(od_dw_b), f(od_ln_w), f(od_ln_b)], 1).reshape(2, 3, 32, 128), (0, 1, 3, 2))),
        "od_w_out": f(od_w_out),
        "ple_norm_w": f(ple_norm_w), "ple_w_gate": f(ple_w_gate), "ple_w_proj": f(ple_w_proj),
        "final_norm_w": f(final_norm_w).reshape(1, D),
    }
    shared.update(_consts())
    x = f(x)
    p = f(p)
    in_maps = []
    for c in range(NCORES):
        b = c % 2
        m = dict(shared)
        m["x"] = np.ascontiguousarray(x[b, :L])
        m["p"] = np.ascontiguousarray(p[:, b, :L])
        in_maps.append(m)
    res = run_bass_kernel_spmd(nc, in_maps, core_ids=list(range(NCORES)))
    kernel.last = res
    return np.stack([res.results[0]["out"], res.results[1]["out"]], 0)
```
